# Optimizing a Trainium2 kernel written in Bass

```python
import jax, jax.numpy as jnp
from jax import lax
import numpy as np

D_MODEL = 1024
BATCH = 4
SEQ = 8192
DEPTH = 1
DEC_BATCH = 16
DEC_SEQ = 16
PAST_LEN = 1024

CHUNK = 64
D_CONV = D_MODEL // 2
CONV_W = 3
HG_HEADS = 4
HG_KDIM = 128
HG_VDIM = 128
HG_W = HG_HEADS * HG_KDIM
N_MEM = 256
X_HEADS = 4
X_HEAD_DIM = D_MODEL // X_HEADS
D_FF = 4 * D_MODEL
N_BRANCH = 2
IN_COLS = 3 * D_CONV + 4 * HG_W + N_BRANCH * D_MODEL
EPS = 1e-6

kernel_name = "hybrid_conv_hgrn2_memxattn_step"

F32 = jnp.float32


def _rmsnorm(x, g):
    xf = x.astype(F32)
    y = xf * lax.rsqrt(jnp.mean(xf * xf, axis=-1, keepdims=True) + EPS)
    return (y * g.astype(F32)).astype(x.dtype)


def _causal_conv(u, prefix, w):
    T = u.shape[1]
    padded = jnp.concatenate([prefix.astype(u.dtype), u], axis=1)
    y = padded[:, 0:T] * w[0]
    for j in range(1, CONV_W):
        y = y + padded[:, j:j + T] * w[j]
    return y, padded[:, T:]


def _hgrn_block(q, k, v, logf, S0):
    L = q.shape[2]
    b = jnp.cumsum(logf, axis=2)
    causal = jnp.tril(jnp.ones((L, L), dtype=bool))
    diff = b[:, :, :, None, :] - b[:, :, None, :, :]
    decay = jnp.exp(jnp.where(causal[None, None, :, :, None], diff, -jnp.inf))
    scores = jnp.einsum('bhtk,bhsk,bhtsk->bhts', q, k, decay)
    o = jnp.einsum('bhts,bhsv->bhtv', scores, v) + jnp.einsum('bhtk,bhkv->bhtv', q * jnp.exp(b), S0)
    b_last = b[:, :, -1:, :]
    S = jnp.exp(b_last[:, :, 0, :])[..., None] * S0 + jnp.einsum('bhsk,bhsv->bhkv', k * jnp.exp(b_last - b), v)
    return o, S


def _hgrn_recurrence(q, k, v, logf, S0):
    B, H, T, K = q.shape
    if T <= CHUNK:
        return _hgrn_block(q, k, v, logf, S0)
    nc = T // CHUNK

    def to_blocks(a):
        return a.reshape(B, H, nc, CHUNK, a.shape[-1]).transpose(2, 0, 1, 3, 4)

    def step(S, blk):
        qb, kb, vb, fb = blk
        o, S = _hgrn_block(qb, kb, vb, fb, S)
        return S, o

    S, o = lax.scan(step, S0, (to_blocks(q), to_blocks(k), to_blocks(v), to_blocks(logf)))
    o = o.transpose(1, 2, 0, 3, 4).reshape(B, H, T, v.shape[-1])
    return o, S


def _mixer(h, conv_prefix, S0, w_in, conv_w, lb, hg_norm, w_conv_out, w_hg_out, w_o):
    Bn, T, _ = h.shape
    splits = (D_CONV, 2 * D_CONV, 3 * D_CONV,
              3 * D_CONV + HG_W, 3 * D_CONV + 2 * HG_W, 3 * D_CONV + 3 * HG_W, 3 * D_CONV + 4 * HG_W,
              3 * D_CONV + 4 * HG_W + D_MODEL)
    proj = h @ w_in
    cb, cc, cx, hq, hf, hi, hg, ga, gb = jnp.split(proj, splits, axis=-1)
    y_conv, conv_state = _causal_conv(cc * cx, conv_prefix, conv_w)
    y_a = (cb * y_conv) @ w_conv_out
    def heads(a):
        return a.astype(F32).reshape(Bn, T, HG_HEADS, -1).transpose(0, 2, 1, 3)
    q = jax.nn.silu(heads(hq))
    lbh = lb.astype(F32).reshape(1, HG_HEADS, 1, HG_KDIM)
    f = lbh + (1.0 - lbh) * jax.nn.sigmoid(heads(hf))
    logf = jnp.log(f)
    k = 1.0 - f
    v = heads(hi)
    o, S = _hgrn_recurrence(q, k, v, logf, S0.astype(F32))
    o = o.transpose(0, 2, 1, 3)
    o = o * lax.rsqrt(jnp.mean(o * o, axis=-1, keepdims=True) + EPS) * hg_norm.astype(F32)
    o = o * jax.nn.silu(hg.astype(F32).reshape(Bn, T, HG_HEADS, HG_VDIM))
    y_b = o.reshape(Bn, T, HG_W).astype(h.dtype) @ w_hg_out
    merged = jax.nn.sigmoid(ga) * y_a + jax.nn.sigmoid(gb) * y_b
    return merged @ w_o, conv_state, S


def _mem_kv(mem, g, w_xk, w_xv):
    Bn = mem.shape[0]
    mh = _rmsnorm(mem, g)
    mk = (mh @ w_xk).reshape(Bn, N_MEM, X_HEADS, X_HEAD_DIM)
    mv = (mh @ w_xv).reshape(Bn, N_MEM, X_HEADS, X_HEAD_DIM)
    return mk, mv


def _cross_attn(h, mk, mv, w_xq, w_xo):
    Bn, T, _ = h.shape
    q = (h @ w_xq).reshape(Bn, T, X_HEADS, X_HEAD_DIM)
    s = jnp.einsum('bthd,bmhd->bhtm', q, mk.astype(q.dtype)).astype(F32) * (X_HEAD_DIM ** -0.5)
    p = jax.nn.softmax(s, axis=-1).astype(h.dtype)
    o = jnp.einsum('bhtm,bmhd->bthd', p, mv.astype(h.dtype)).reshape(Bn, T, D_MODEL)
    return o @ w_xo


def _ffn(h, w_up, w_down):
    a = jax.nn.relu(h @ w_up)
    return (a * a) @ w_down


def _layer(x, conv_prefix, S0, mk, mv, lb, norm_mix, w_in, conv_w, hg_norm, w_conv_out, w_hg_out, w_o,
           norm_x, w_xq, w_xo, norm_ffn, w_up, w_down):
    m, conv_state, S = _mixer(_rmsnorm(x, norm_mix), conv_prefix, S0, w_in, conv_w, lb, hg_norm,
                              w_conv_out, w_hg_out, w_o)
    x = x + m
    x = x + _cross_attn(_rmsnorm(x, norm_x), mk, mv, w_xq, w_xo)
    x = x + _ffn(_rmsnorm(x, norm_ffn), w_up, w_down)
    return x, conv_state, S


def setup_inputs(seed: int = 0) -> dict:
    key = jax.random.key(seed)
    ks = jax.random.split(key, 32)
    nrm = lambda k, shape, s: jax.random.normal(k, shape, F32) * s
    gain = lambda k, shape: 1.0 + 0.05 * jax.random.normal(k, shape, F32)
    return {
        "x_prompt": nrm(ks[0], (BATCH, SEQ, D_MODEL), 1.0),
        "x_sample": nrm(ks[1], (DEC_BATCH, DEC_SEQ, D_MODEL), 1.0),
        "mem_prompt": nrm(ks[2], (BATCH, N_MEM, D_MODEL), 1.0),
        "state_conv": nrm(ks[3], (DEPTH, DEC_BATCH, CONV_W - 1, D_CONV), 1.0),
        "state_hgrn": nrm(ks[4], (DEPTH, DEC_BATCH, HG_HEADS, HG_KDIM, HG_VDIM), 0.5),
        "cache_mem_k": nrm(ks[5], (DEPTH, DEC_BATCH, N_MEM, X_HEADS, X_HEAD_DIM), 1.0),
        "cache_mem_v": nrm(ks[6], (DEPTH, DEC_BATCH, N_MEM, X_HEADS, X_HEAD_DIM), 1.0),
        "norm_mix": gain(ks[7], (DEPTH, D_MODEL)),
        "w_in": nrm(ks[8], (DEPTH, D_MODEL, IN_COLS), D_MODEL ** -0.5),
        "conv_w": nrm(ks[9], (DEPTH, CONV_W, D_CONV), CONV_W ** -0.5),
        "hg_lb": nrm(ks[10], (DEPTH + 1, HG_W), 0.5),
        "hg_norm": gain(ks[11], (DEPTH, HG_VDIM)),
        "w_conv_out": nrm(ks[12], (DEPTH, D_CONV, D_MODEL), D_CONV ** -0.5),
        "w_hg_out": nrm(ks[13], (DEPTH, HG_W, D_MODEL), HG_W ** -0.5),
        "w_o": nrm(ks[14], (DEPTH, D_MODEL, D_MODEL), D_MODEL ** -0.5),
        "norm_x": gain(ks[15], (DEPTH, D_MODEL)),
        "norm_mem": gain(ks[16], (DEPTH, D_MODEL)),
        "w_xq": nrm(ks[17], (DEPTH, D_MODEL, D_MODEL), D_MODEL ** -0.5),
        "w_xk": nrm(ks[18], (DEPTH, D_MODEL, D_MODEL), D_MODEL ** -0.5),
        "w_xv": nrm(ks[19], (DEPTH, D_MODEL, D_MODEL), D_MODEL ** -0.5),
        "w_xo": nrm(ks[20], (DEPTH, D_MODEL, D_MODEL), D_MODEL ** -0.5),
        "norm_ffn": gain(ks[21], (DEPTH, D_MODEL)),
        "w_up": nrm(ks[22], (DEPTH, D_MODEL, D_FF), D_MODEL ** -0.5),
        "w_down": nrm(ks[23], (DEPTH, D_FF, D_MODEL), D_FF ** -0.5),
        "norm_final": gain(ks[24], (D_MODEL,)),
    }


def reference(x_prompt, x_sample, mem_prompt, state_conv, state_hgrn, cache_mem_k, cache_mem_v,
              norm_mix, w_in, conv_w, hg_lb, hg_norm, w_conv_out, w_hg_out, w_o,
              norm_x, norm_mem, w_xq, w_xk, w_xv, w_xo, norm_ffn, w_up, w_down, norm_final):
    lb_all = jnp.cumsum(jax.nn.softmax(hg_lb.astype(F32), axis=0), axis=0)
    xp, xs = x_prompt, x_sample
    Bp = x_prompt.shape[0]
    conv_p, hg_p, mk_p_l, mv_p_l, conv_s, hg_s = [], [], [], [], [], []
    for l in range(DEPTH):
        shared = dict(lb=lb_all[l], norm_mix=norm_mix[l], w_in=w_in[l], conv_w=conv_w[l], hg_norm=hg_norm[l],
                      w_conv_out=w_conv_out[l], w_hg_out=w_hg_out[l], w_o=w_o[l], norm_x=norm_x[l],
                      w_xq=w_xq[l], w_xo=w_xo[l], norm_ffn=norm_ffn[l], w_up=w_up[l], w_down=w_down[l])
        mk_p, mv_p = _mem_kv(mem_prompt, norm_mem[l], w_xk[l], w_xv[l])
        prefix0 = jnp.zeros((Bp, CONV_W - 1, D_CONV), xp.dtype)
        S0 = jnp.zeros((Bp, HG_HEADS, HG_KDIM, HG_VDIM), F32)
        xp, cst_p, S_p = _layer(xp, prefix0, S0, mk_p, mv_p, **shared)
        conv_p.append(cst_p.astype(x_prompt.dtype))
        hg_p.append(S_p.astype(x_prompt.dtype))
        mk_p_l.append(mk_p)
        mv_p_l.append(mv_p)
        xs, cst_s, S_s = _layer(xs, state_conv[l], state_hgrn[l], cache_mem_k[l], cache_mem_v[l], **shared)
        conv_s.append(cst_s.astype(x_sample.dtype))
        hg_s.append(S_s.astype(x_sample.dtype))
    y_prompt = _rmsnorm(xp, norm_final)
    y_sample = _rmsnorm(xs, norm_final)
    return (y_prompt, y_sample, jnp.stack(conv_p), jnp.stack(hg_p), jnp.stack(mk_p_l), jnp.stack(mv_p_l),
            jnp.stack(conv_s), jnp.stack(hg_s))
```

```python
import numpy as np
from contextlib import ExitStack
import concourse.bass as bass
import concourse.mybir as mybir
from concourse.bass_utils import run_bass_kernel_spmd

F32 = mybir.dt.float32
BF16 = mybir.dt.bfloat16
AF = mybir.ActivationFunctionType
ALU = mybir.AluOpType

D = 1024
TM = 512
EPS = 1e-6
NCV = 56
N_CORES = 8


class Tr:
    __slots__ = ("w", "r", "name")

    def __init__(self, name="", fence=None):
        self.w = None
        self.r = list(fence) if fence else []
        self.name = name


class Src:
    def __init__(self, name, sem, unit):
        self.name, self.sem, self.unit, self.count = name, sem, unit, 0


class Eng(Src):
    def __init__(self, name, h, sem, self_sync):
        super().__init__(name, sem, 1)
        self.h = h
        self.seen = {}
        self.self_sync = self_sync


class FM:
    def __init__(self, b, name, nch, T, dt, es):
        self.t = es.enter_context(b.nc.sbuf_tensor(b.nm(name), [128, nch, T], dt))
        self.tr = [b.newtr(name + str(i)) for i in range(nch)]
        self.nch, self.T = nch, T


class Bld:
    def __init__(self):
        self.nc = bass.Bass("TRN2", target_bir_lowering=False)
        self.es = ExitStack()
        self.uid = 0
        self.fence = {}
        self.phase_trs = []
        nc = self.nc
        mk = lambda n: self.es.enter_context(nc.semaphore(n))
        self.PE = Eng("pe", nc.tensor, mk("s_pe"), False)
        self.ACT = Eng("act", nc.scalar, mk("s_act"), True)
        self.DVE = Eng("dve", nc.vector, mk("s_dve"), True)
        self.POOL = Eng("pool", nc.gpsimd, mk("s_pool"), True)
        self.SP = Eng("sp", nc.sync, mk("s_sp"), False)
        self.slots = {}

    def nm(self, n):
        self.uid += 1
        return "%s_%d" % (n, self.uid)

    def slot(self, name):
        if name not in self.slots:
            self.slots[name] = Src(name, self.es.enter_context(self.nc.semaphore("d_" + name)), 16)
        return self.slots[name]

    def newtr(self, name="", arena=False):
        if arena:
            t = Tr(name, fence=list(self.fence.items()))
            self.phase_trs.append(t)
            return t
        return Tr(name)

    def end_phase(self):
        for t in self.phase_trs:
            acc = list(t.r)
            if t.w:
                acc.append(t.w)
            for (s, i) in acc:
                if self.fence.get(s, 0) < i:
                    self.fence[s] = i
        self.phase_trs = []

    def _wait(self, e, deps):
        need = {}
        for (s, i) in deps:
            if s is e:
                if not e.self_sync:
                    continue
            if need.get(s, 0) < i:
                need[s] = i
        for s, i in need.items():
            if e.seen.get(s, 0) < i:
                e.h.wait_ge(s.sem, i * s.unit)
                e.seen[s] = i

    @staticmethod
    def _deps(reads, writes):
        deps = []
        for t in reads:
            if t.w:
                deps.append(t.w)
        for t in writes:
            if t.w:
                deps.append(t.w)
            deps.extend(t.r)
        return deps

    def op(self, e, fn, reads=(), writes=()):
        self._wait(e, self._deps(reads, writes))
        ins = fn()
        e.count += 1
        ins.then_inc(e.sem, 1)
        me = (e, e.count)
        for t in reads:
            t.r.append(me)
        for t in writes:
            t.w = me
            t.r = []

    def dma(self, e, out, in_, slot, reads=(), writes=(), chain=False):
        deps = self._deps(reads, writes)
        if slot.count > 0 and not chain:
            deps.append((slot, slot.count))
        self._wait(e, deps)
        e.h.dma_start(out=out, in_=in_).then_inc(slot.sem, 16)
        slot.count += 1
        me = (slot, slot.count)
        for t in reads:
            t.r.append(me)
        for t in writes:
            t.w = me
            t.r = []


def build(n_pre=8, n_main=8, sample=True, do_mem=True):
    b = Bld()
    nc = b.nc
    es = b.es
    PE, ACT, DVE, POOL, SP = b.PE, b.ACT, b.DVE, b.POOL, b.SP

    def din(name, shape):
        return nc.dram_tensor(name, list(shape), F32, kind="ExternalInput").ap()

    def dout(name, shape):
        return nc.dram_tensor(name, list(shape), F32, kind="ExternalOutput").ap()

    xm = din("xm", [4096, D])
    xp = din("xp", [4096, D])
    xs = din("xs", [64, D])
    mem = din("mem", [256, D])
    sconv = din("sconv", [128, 4, 2, 2])
    shg = din("shg", [2, 4, 128, 128])
    ck = din("ck", [2, 256, D])
    cv = din("cv", [2, 256, D])
    cvec_d = din("cvec", [128, NCV])
    gfin_d = din("gfin", [128, D])
    w_in = din("w_in", [D, 5632])
    w_conv_out = din("w_conv_out", [512, D])
    w_hg_out = din("w_hg_out", [512, D])
    w_o = din("w_o", [D, D])
    w_xq = din("w_xq", [D, D])
    w_xk = din("w_xk", [D, D])
    w_xv = din("w_xv", [D, D])
    w_xo = din("w_xo", [D, D])
    w_up = din("w_up", [D, 4096])
    w_down = din("w_down", [4096, D])

    y_o = dout("y", [4096, D])
    ys_o = dout("ys", [64, D])
    conv_o = dout("conv_o", [128, 4, 2])
    hg_o = dout("hg_o", [4, 128, 128])
    mk_o = dout("mk_o", [256, D])
    mv_o = dout("mv_o", [256, D])
    convs_o = dout("convs_o", [128, 4, 2, 2])
    hgs_o = dout("hgs_o", [2, 4, 128, 128])

    def kp(ap):
        return ap.rearrange("(k p) c -> p k c", p=128)

    blocks = {}
    for j, nme in enumerate(["cb", "cc", "cx", "hq", "hf", "hi", "hg", "ga0", "ga1", "gb0", "gb1"]):
        blocks[nme] = [(kp(w_in)[:, :, j * 512:(j + 1) * 512], 0, 8)]
    for c in range(2):
        cs = slice(c * 512, (c + 1) * 512)
        blocks["cvhg%d" % c] = [(kp(w_conv_out)[:, :, cs], 0, 4), (kp(w_hg_out)[:, :, cs], 4, 4)]
        blocks["wo%d" % c] = [(kp(w_o)[:, :, cs], 0, 8)]
        blocks["xq%d" % c] = [(kp(w_xq)[:, :, cs], 0, 8)]
        blocks["xk%d" % c] = [(kp(w_xk)[:, :, cs], 0, 8)]
        blocks["xv%d" % c] = [(kp(w_xv)[:, :, cs], 0, 8)]
        blocks["xo%d" % c] = [(kp(w_xo)[:, :, cs], 0, 8)]
        for r in range(4):
            blocks["dn%d_%d" % (r, c)] = [(kp(w_down[r * 1024:(r + 1) * 1024, :])[:, :, cs], 0, 8)]
    for c in range(8):
        blocks["up%d" % c] = [(kp(w_up)[:, :, c * 512:(c + 1) * 512], 0, 8)]
    bnames = list(blocks.keys())
    bidx = {n: i for i, n in enumerate(bnames)}
    wscr = nc.dram_tensor("wscr", [len(bnames), 128, 8, 512], BF16, kind="Internal").ap()
    scr_tr = {n: Tr("scr_" + n) for n in bnames}
    converted = set()

    main_seq = (["hf", "cc", "cx", "cb", "hq", "hi", "hg", "ga0", "ga1", "gb0", "gb1", "cvhg0", "cvhg1",
                 "wo0", "wo1", "xq0", "xq1", "xo0", "xo1"] + ["up%d" % c for c in range(8)]
                + ["dn%d_%d" % (r, c) for c in range(2) for r in range(4)])
    seq = ["xk0", "xk1", "xv0", "xv1"] if do_mem else []
    for p in range(n_pre):
        seq += ["hf", "hi"]
        if p == n_pre - 1:
            seq += ["cc", "cx"]
    for t in range(n_main):
        seq += main_seq
    if sample:
        seq += main_seq

    NWB = 3
    wbuf = [es.enter_context(nc.sbuf_tensor("wbuf%d" % i, [128, 8, 512], BF16)) for i in range(NWB)]
    wtr = [Tr("wbuf%d" % i) for i in range(NWB)]
    wslot = [b.slot("wld%d" % i) for i in range(NWB)]
    wslot_sw = [b.slot("wlds%d" % i) for i in range(NWB)]
    sslot = [b.slot("wst%d" % i) for i in range(4)]
    wstate = {"issued": 0, "pos": 0, "nst": 0}

    def w_issue(j):
        name = seq[j]
        s = j % NWB
        if name not in converted:
            converted.add(name)
            first = True
            for (src, k0, nk) in blocks[name]:
                b.dma(POOL, wbuf[s][:, k0:k0 + nk, :], src, wslot_sw[s], writes=[wtr[s]], chain=not first)
                first = False
            st = sslot[wstate["nst"] % 4]
            wstate["nst"] += 1
            b.dma(POOL, wscr[bidx[name]], wbuf[s][:, :, :], st, reads=[wtr[s]], writes=[scr_tr[name]])
        else:
            b.dma(SP, wbuf[s][:, :, :], wscr[bidx[name]], wslot[s], reads=[scr_tr[name]], writes=[wtr[s]])

    def w_get(expect, ahead=True):
        i = wstate["pos"]
        assert seq[i] == expect, (seq[i], expect, i)
        while wstate["issued"] < (min(i + NWB, len(seq)) if ahead else i + 1):
            w_issue(wstate["issued"])
            wstate["issued"] += 1
        wstate["pos"] += 1
        return wbuf[i % NWB], wtr[i % NWB]

    def sb(name, shape, dt=F32):
        return es.enter_context(nc.sbuf_tensor("sb_" + name, list(shape), dt))

    NXB = 2
    xres = [sb("xres%d" % i, [128, 4, D]) for i in range(NXB)]
    xres_tr = [[Tr("xres") for _ in range(4)] for _ in range(NXB)]
    hT = sb("hT", [128, 8, TM], BF16)
    hT_tr = [Tr("hT%d" % s) for s in range(4)]
    Sst = sb("Sst", [128, 4, 128])
    S_tr = [Tr("S%d" % h) for h in range(4)]
    Sb = sb("Sb", [128, 4, 128], BF16)
    Sb_tr = [Tr("Sb%d" % h) for h in range(4)]
    cvec = sb("cvec", [128, NCV])
    gfin = sb("gfin", [128, D])
    lbv = sb("lbv", [128, 4])
    omlb = sb("omlb", [128, 4])
    ident = sb("ident", [128, 128], BF16)
    ones = sb("ones", [128, 128], BF16)
    mask64 = sb("mask64", [128, 128])
    mask32 = sb("mask32", [64, 64])
    zeros = sb("zeros", [128, 64])
    neghalf = sb("neghalf", [128, 4])
    ccs_zero = sb("zeros512", [128, TM])
    mkT = sb("mkT", [128, 8, 256], BF16)
    mv = sb("mv", [128, 2, D], BF16)
    mkT_tr, mv_tr = Tr("mkT"), Tr("mv")
    uprev = sb("uprev", [128, 4, 2])
    uprev_tr = Tr("uprev")
    junk = sb("junk", [128, D], BF16)
    junk_tr = Tr("junk")
    xn = [sb("xn%d" % i, [128, D], BF16) for i in range(2)]
    xn_tr = [Tr("xn0"), Tr("xn1")]
    yout = [sb("yout%d" % i, [128, D]) for i in range(2)]
    yout_tr = [Tr("yo0"), Tr("yo1")]
    stat = sb("stat", [128, 16])
    stat_tr = [Tr("stat%d" % i) for i in range(4)]
    const_tr = Tr("const")
    cnt = {"xn": 0, "yo": 0, "st": 0, "psF": 0, "psB": 0, "ld": 0, "os": 0}

    NPF = 4
    psF = [es.enter_context(nc.psum_tensor("psF%d" % i, [128, 512], F32)) for i in range(NPF)]
    psF_tr = [Tr("psF%d" % i) for i in range(NPF)]
    psB = [es.enter_context(nc.psum_tensor("psB%d" % i, [128, 1024], BF16)) for i in range(2)]
    psB_tr = [Tr("psB0"), Tr("psB1")]

    psO = [es.enter_context(nc.psum_tensor("psO%d" % i, [128, 512], F32)) for i in range(2)]
    psO_tr = [Tr("psO0"), Tr("psO1")]

    def next_psO():
        i = cnt.setdefault("psO", 0) % 2
        cnt["psO"] += 1
        return psO[i], psO_tr[i]

    def next_psF():
        if cnt.get("in_rec", 0):
            i = cnt["psF"] % NPF
            cnt["psF"] += 1
            return psF[i], psF_tr[i]
        i = cnt.setdefault("ps6", 0) % (NPF + 2)
        cnt["ps6"] += 1
        if i < NPF:
            return psF[i], psF_tr[i]
        return psO[i - NPF], psO_tr[i - NPF]

    def next_psB():
        i = cnt["psB"] % 2
        cnt["psB"] += 1
        return psB[i], psB_tr[i]

    ldslots = [b.slot("ld%d" % i) for i in range(6)]
    stslots = [b.slot("st%d" % i) for i in range(6)]

    def ld_slot():
        cnt["ld"] += 1
        return ldslots[cnt["ld"] % 6]

    plslots = [b.slot("pl%d" % i) for i in range(2)]

    def pl_slot():
        cnt["pl"] = cnt.get("pl", 0) + 1
        return plslots[cnt["pl"] % 2]

    def st_slot():
        cnt["os"] += 1
        return stslots[cnt["os"] % 6]

    def mm(out_ap, pairs, reads, ptr, start=True, stop=True):
        def fn():
            ins = None
            n = len(pairs)
            for i, (l, r) in enumerate(pairs):
                ins = nc.tensor.matmul(out_ap, lhsT=l, rhs=r, start=(start and i == 0), stop=(stop and i == n - 1))
            return ins
        b.op(PE, fn, reads=reads, writes=[ptr])

    b.dma(SP, cvec[:, :], cvec_d[:, :], ld_slot(), writes=[const_tr])
    gfin_tr = Tr("gfin")
    b.dma(SP, gfin[:, :], gfin_d[:, :], ld_slot(), writes=[gfin_tr])
    mtr = Tr("masks")

    P = lambda fn, wr: b.op(POOL, fn, writes=wr)
    id_tr = Tr("ident")
    P(lambda: nc.gpsimd.memset(ident[:, :], 1.0), [id_tr])
    P(lambda: nc.gpsimd.affine_select(out=ident[:, :], in_=ident[:, :], pattern=[[-1, 128]], compare_op=ALU.is_equal,
                                      fill=0.0, base=0, channel_multiplier=1), [id_tr])
    m64_tr, m32_tr = Tr("m64"), Tr("m32")
    P(lambda: nc.gpsimd.memset(mask64[:, :], 1.0), [m64_tr])
    P(lambda: nc.gpsimd.memset(mask32[:, :], 1.0), [m32_tr])
    P(lambda: nc.gpsimd.affine_select(out=mask64[:, :], in_=mask64[:, :], pattern=[[1, 128]], compare_op=ALU.is_ge,
                                      fill=0.0, base=0, channel_multiplier=-1), [m64_tr])
    P(lambda: nc.gpsimd.affine_select(out=mask32[:, :], in_=mask32[:, :], pattern=[[1, 64]], compare_op=ALU.is_ge,
                                      fill=0.0, base=0, channel_multiplier=-1), [m32_tr])
    P(lambda: nc.gpsimd.memset(mask64[0:64, 64:128], 0.0), [m64_tr])
    P(lambda: nc.gpsimd.memset(mask32[0:32, 32:64], 0.0), [m32_tr])
    P(lambda: nc.gpsimd.memset(uprev[:, :, :], 0.0), [uprev_tr])
    P(lambda: nc.gpsimd.memset(stat[:, :], 1.0), stat_tr)
    P(lambda: nc.gpsimd.memset(Sst[:, :, :], 0.0), S_tr)
    P(lambda: nc.gpsimd.memset(zeros[:, :], 0.0), [mtr])
    P(lambda: nc.gpsimd.memset(neghalf[:, :], -0.5), [mtr])
    P(lambda: nc.gpsimd.memset(ccs_zero[:, :], 0.0), [mtr])
    b.op(POOL, lambda: nc.gpsimd.memset(ones[:, :], 1.0), reads=[id_tr, m64_tr, m32_tr], writes=[mtr])
    lb_tr = Tr("lb")
    b.op(DVE, lambda: nc.vector.tensor_tensor(out=lbv[:, :], in0=cvec[:, 32:36], in1=cvec[:, 36:40], op=ALU.subtract),
         reads=[const_tr], writes=[lb_tr])
    b.op(ACT, lambda: nc.scalar.activation(out=lbv[:, :], in_=lbv[:, :], func=AF.Sigmoid), writes=[lb_tr])
    b.op(DVE, lambda: nc.vector.tensor_scalar(out=omlb[:, :], in0=lbv[:, :], scalar1=-1.0, scalar2=1.0,
                                              op0=ALU.mult, op1=ALU.add), reads=[lb_tr], writes=[const_tr])
    G_MIX, G_X, G_FFN, G_MEM = 0, 8, 16, 24

    def norm_stats(xr, xr_tr, subt, sel=None):
        nsub = len(subt)
        nmax = max(n for (_, n) in subt)
        if sel is None:
            sel = list(range(nsub))
        g = cnt["st"] % 2
        cnt["st"] += 1
        ssv = stat[:, g * 8:g * 8 + 4]
        rsv = stat[:, g * 8 + 4:g * 8 + 8]
        stt = stat_tr[g]
        for s in sel:
            c0, n = subt[s]
            b.op(ACT, lambda: nc.scalar.activation(out=junk[:n, :], in_=xr[:n, s, :], func=AF.Square, accum_out=ssv[:n, s:s + 1]),
                 reads=[xr_tr[s]], writes=[junk_tr, stt])
        b.op(POOL, lambda: nc.gpsimd.tensor_scalar(out=rsv[:nmax, 0:nsub], in0=ssv[:nmax, 0:nsub], scalar1=1.0 / D, scalar2=EPS,
                                                   op0=ALU.mult, op1=ALU.add), writes=[stt])
        b.op(POOL, lambda: nc.gpsimd.tensor_tensor(out=rsv[:nmax, 0:nsub], in0=rsv[:nmax, 0:nsub], in1=neghalf[:nmax, 0:nsub], op=ALU.pow),
             reads=[mtr], writes=[stt])
        return rsv, stt

    def rmsnorm_to_hT(xr, xr_tr, subt, goff, sel=None):
        rsv, stt = norm_stats(xr, xr_tr, subt, sel)
        for s in (sel if sel is not None else range(len(subt))):
            c0, n = subt[s]
            xi = cnt["xn"] % 2
            cnt["xn"] += 1
            b.op(ACT, lambda: nc.scalar.activation(out=xn[xi][:n, :], in_=xr[:n, s, :], func=AF.Copy, scale=rsv[:n, s:s + 1]),
                 reads=[xr_tr[s], stt], writes=[xn_tr[xi]])
            pb, pbt = next_psB()

            def tp():
                ins = None
                for k in range(8):
                    ins = nc.tensor.transpose(pb[:, k * 128:k * 128 + n], xn[xi][:n, k * 128:(k + 1) * 128], ident[:n, :n])
                return ins
            b.op(PE, tp, reads=[xn_tr[xi], mtr], writes=[pbt])
            pv = pb[:, :].rearrange("p (k t) -> p k t", t=128)[:, :, 0:n]
            gb = cvec[:, goff:goff + 8].unsqueeze(2).to_broadcast([128, 8, n])
            b.op(DVE, lambda: nc.vector.tensor_tensor(out=hT[:, :, c0:c0 + n], in0=pv, in1=gb, op=ALU.mult),
                 reads=[pbt, const_tr], writes=[hT_tr[s]])

    def fm_proj(wb, wt, ch4, ncols, rhs_of_k, nk, rd, koff=0):
        ps, pt = next_psF()
        mm(ps[:, 0:ncols], [(wb[:, koff + k, ch4 * 128:(ch4 + 1) * 128], rhs_of_k(k)) for k in range(nk)],
           reads=[wt] + rd, ptr=pt)
        return ps, pt

    def fm_block_split(wb, wt):
        banks = [next_psF() for _ in range(4)]
        for (lo, hi, trs) in [(0, 384, hT_tr[0:3]), (384, 512, hT_tr[3:4])]:
            for ch4 in range(4):
                ps, pt = banks[ch4]

                def fn():
                    ins = None
                    for k in range(8):
                        ins = nc.tensor.matmul(ps[:, lo:hi], lhsT=wb[:, k, ch4 * 128:(ch4 + 1) * 128], rhs=hT[:, k, lo:hi],
                                               start=(lo == 0 and k == 0), stop=(hi == 512 and k == 7), skip_group_check=True)
                    return ins
                b.op(PE, fn, reads=[wt] + trs, writes=[pt])
        return banks

    def tok_proj(wb, wt, lhs_of_k, nk, n, rd, ps=None, pt=None, start=True, stop=True):
        if ps is None:
            ps, pt = next_psF()
        mm(ps[:n, :], [(lhs_of_k(k), wb[:, k, :]) for k in range(nk)], reads=[wt] + rd, ptr=pt, start=start, stop=stop)
        return ps, pt

    def load_x(src_rows, xr, xr_tr, subt):
        for s, (c0, n) in enumerate(subt):
            b.dma(SP, xr[:n, s, :], src_rows[c0:c0 + n, :], ld_slot(), writes=[xr_tr[s]])

    SUB4 = [(i * 128, 128) for i in range(4)]
    tiles = [("pre", p) for p in range(n_pre)] + [("main", t) for t in range(n_main)] + ([("sample", 0)] if sample else [])
    tstate = {"loaded": 0}

    def ensure_loaded(i):
        while tstate["loaded"] <= i and tstate["loaded"] < len(tiles):
            j = tstate["loaded"]
            kind, idx = tiles[j]
            if kind == "pre":
                src, st = xp[idx * TM:(idx + 1) * TM, :], SUB4
            elif kind == "main":
                src, st = xm[idx * TM:(idx + 1) * TM, :], SUB4
            else:
                src, st = xs, [(0, 64)]
            load_x(src, xres[j % NXB], xres_tr[j % NXB], st)
            tstate["loaded"] += 1
        return xres[i % NXB], xres_tr[i % NXB]

    def hgrn_gates(T, L, last, ar, hTr, tmp, tmp_tr, part=0):
        fb, eb, kt, kpb = ar["fb"], ar["eb"], ar["kt"], ar["kp"]
        nch = T // L
        if part in (0, 1):
            wb, wt = w_get("hf")
        for h in (range(4) if part in (0, 1) else []):
            ps, pt = fm_proj(wb, wt, h, T, lambda k: hT[:, k, 0:T], 8, hTr)
            b.op(ACT, lambda: nc.scalar.activation(out=fb.t[:, h, :], in_=ps[:, 0:T], func=AF.Sigmoid),
                 reads=[pt], writes=[fb.tr[h]])
            b.op(ACT, lambda: nc.scalar.activation(out=fb.t[:, h, :], in_=fb.t[:, h, :], func=AF.Identity,
                                                   scale=omlb[:, h:h + 1], bias=lbv[:, h:h + 1]),
                 reads=[const_tr, lb_tr], writes=[fb.tr[h]])
        for h in (range(4) if part in (0, 1) else []):
            b.op(ACT, lambda: nc.scalar.activation(out=tmp[h][:, 0:T], in_=fb.t[:, h, :], func=AF.Ln), reads=[fb.tr[h]], writes=[tmp_tr[h]])
        for h in (range(4) if part in (0, 1) else []):
            def scans():
                ins = None
                for c in range(nch):
                    ins = nc.vector.tensor_tensor_scan(out=eb.t[:, h, c * L:(c + 1) * L], data0=tmp[h][:, c * L:(c + 1) * L],
                                                       data1=zeros[:, 0:L], initial=0.0, op0=ALU.add, op1=ALU.add)
                return ins
            b.op(DVE, scans, reads=[tmp_tr[h], mtr], writes=[eb.tr[h]])
        if part == 1:
            return
        for h in range(4):
            b.op(ACT, lambda: nc.scalar.activation(out=tmp[h][:, 0:T], in_=eb.t[:, h, :], func=AF.Exp, scale=-1.0), reads=[eb.tr[h]], writes=[tmp_tr[h]])
            b.op(ACT, lambda: nc.scalar.activation(out=eb.t[:, h, :], in_=eb.t[:, h, :], func=AF.Exp), writes=[eb.tr[h]])
        for h in range(4):
            b.op(POOL, lambda: nc.gpsimd.tensor_scalar(out=fb.t[:, h, :], in0=fb.t[:, h, :], scalar1=-1.0, scalar2=1.0,
                                                       op0=ALU.mult, op1=ALU.add), writes=[fb.tr[h]])
            b.op(POOL, lambda: nc.gpsimd.tensor_tensor(out=kt.t[:, h, :], in0=fb.t[:, h, :], in1=tmp[h][:, 0:T], op=ALU.mult),
                 reads=[fb.tr[h], tmp_tr[h]], writes=[kt.tr[h]])
            ktv = kt.t[:, h, :].rearrange("p (c l) -> p c l", l=L)
            kpv = kpb.t[:, h, :].rearrange("p (c l) -> p c l", l=L)
            elb = eb.t[:, h, :].rearrange("p (c l) -> p c l", l=L)[:, :, last:last + 1].to_broadcast([128, nch, L])
            b.op(POOL, lambda: nc.gpsimd.tensor_tensor(out=kpv, in0=ktv, in1=elb, op=ALU.mult),
                 reads=[kt.tr[h], eb.tr[h]], writes=[kpb.tr[h]])

    def v_proj(subt, ar, hTr):
        wb, wt = w_get("hi")
        v = ar["v"]
        for s, (c0, n) in enumerate(subt):
            ps, pt = tok_proj(wb, wt, lambda k: hT[:, k, c0:c0 + n], 8, n, hTr)
            b.op(ACT, lambda: nc.scalar.copy(out=v.t[:n, s, :], in_=ps[:n, :]), reads=[pt], writes=[v.tr[s]])

    def kp_transpose(subt, ar):
        kpb, kptok = ar["kp"], ar["kptok"]
        for s, (c0, n) in enumerate(subt):
            pb, pbt = next_psB()

            def tp():
                ins = None
                for h in range(4):
                    ins = nc.tensor.transpose(pb[:n, h * 128:(h + 1) * 128], kpb.t[:, h, c0:c0 + n], ident[:, :])
                return ins
            b.op(PE, tp, reads=kpb.tr + [mtr], writes=[pbt])
            b.op(ACT, lambda: nc.scalar.copy(out=kptok.t[:n, s, :], in_=pb[:n, 0:512]), reads=[pbt], writes=[kptok.tr[s]])

    def state_update(s, h, c, L, last, c0, ar, S_ap, S_t, e_ap):
        kptok, v, eb = ar["kptok"], ar["v"], ar["eb"]
        pP, pPt = next_psF()
        r0 = c * L
        mm(pP[:, 0:128], [(kptok.t[r0:r0 + L, s, h * 128:(h + 1) * 128], v.t[r0:r0 + L, s, h * 128:(h + 1) * 128])],
           reads=[kptok.tr[s], v.tr[s]], ptr=pPt)
        b.op(DVE, lambda: nc.vector.scalar_tensor_tensor(out=S_ap, in0=S_ap, scalar=e_ap, in1=pP[:, 0:128],
                                                         op0=ALU.mult, op1=ALU.add),
             reads=[pPt, eb.tr[h]], writes=[S_t])

    def mem_kv():
        subt = [(0, 128), (128, 128)]
        xr, xrt = xres[0], xres_tr[0]
        load_x(mem, xr, xrt, subt)
        rmsnorm_to_hT(xr, xrt, subt, G_MEM)
        hTr = hT_tr[0:2]
        for c in range(2):
            wb, wt = w_get("xk%d" % c)
            for ch4 in range(4):
                ps, pt = fm_proj(wb, wt, ch4, 256, lambda k: hT[:, k, 0:256], 8, hTr)
                b.op(ACT, lambda: nc.scalar.copy(out=mkT[:, c * 4 + ch4, :], in_=ps[:, 0:256]), reads=[pt], writes=[mkT_tr])
            for s, (c0, n) in enumerate(subt):
                ps, pt = tok_proj(wb, wt, lambda k: hT[:, k, c0:c0 + n], 8, n, hTr)
                yi = cnt["yo"] % 2
                cnt["yo"] += 1
                b.op(ACT, lambda: nc.scalar.copy(out=yout[yi][:, 0:512], in_=ps[:, :]), reads=[pt], writes=[yout_tr[yi]])
                b.dma(POOL, mk_o[c0:c0 + n, c * 512:(c + 1) * 512], yout[yi][:, 0:512], st_slot(), reads=[yout_tr[yi]])
        for c in range(2):
            wb, wt = w_get("xv%d" % c)
            for s, (c0, n) in enumerate(subt):
                ps, pt = tok_proj(wb, wt, lambda k: hT[:, k, c0:c0 + n], 8, n, hTr)
                yi = cnt["yo"] % 2
                cnt["yo"] += 1
                b.op(ACT, lambda: nc.scalar.copy(out=yout[yi][:, 0:512], in_=ps[:, :]), reads=[pt], writes=[yout_tr[yi]])
                b.op(DVE, lambda: nc.vector.tensor_copy(out=mv[:, s, c * 512:(c + 1) * 512], in_=ps[:, :]), reads=[pt], writes=[mv_tr])
                b.dma(POOL, mv_o[c0:c0 + n, c * 512:(c + 1) * 512], yout[yi][:, 0:512], st_slot(), reads=[yout_tr[yi]])

    pre_sets = []

    def pre_alloc(ph):
        A = lambda n: b.newtr(n, arena=True)
        for i in range(2):
            ar = {}
            for nme, dt, W in [("fb", F32, TM), ("cf", F32, TM + 1), ("kp", BF16, TM)]:
                f = FM.__new__(FM)
                f.t = ph.enter_context(nc.sbuf_tensor(b.nm(nme), [128, 4, W], dt))
                f.tr = [A(nme) for _ in range(4)]
                ar[nme] = f
            for nme in ["v", "kptok"]:
                f = FM.__new__(FM)
                f.t = ph.enter_context(nc.sbuf_tensor(b.nm(nme), [128, 4, 512], BF16))
                f.tr = [A(nme) for _ in range(4)]
                ar[nme] = f
            pre_sets.append(ar)
        ccs = ph.enter_context(nc.sbuf_tensor(b.nm("ccs"), [128, 4, TM], F32))
        pre_sets.append((ccs, [A("ccs") for _ in range(4)]))

    def pre_tile(p, lastp):
        T = TM
        subt = [(i * 128, 128) for i in range(4)]
        xr, xrt = ensure_loaded(p)
        rmsnorm_to_hT(xr, xrt, subt, G_MIX)
        ensure_loaded(p + 1)
        ar = pre_sets[p % 2]
        fb, cf, kpb, kptok, v = ar["fb"], ar["cf"], ar["kp"], ar["kptok"], ar["v"]
        wb, wt = w_get("hf")
        for h in range(4):
            ps, pt = fm_proj(wb, wt, h, T, lambda k: hT[:, k, 0:T], 8, hT_tr)
            b.op(ACT, lambda: nc.scalar.activation(out=fb.t[:, h, :], in_=ps[:, 0:T], func=AF.Sigmoid), reads=[pt], writes=[fb.tr[h]])
            b.op(DVE, lambda: nc.vector.tensor_scalar(out=fb.t[:, h, :], in0=fb.t[:, h, :], scalar1=omlb[:, h:h + 1],
                                                      scalar2=lbv[:, h:h + 1], op0=ALU.mult, op1=ALU.add),
                 reads=[const_tr, lb_tr], writes=[fb.tr[h]])
            b.op(POOL, lambda: nc.gpsimd.memset(cf.t[:, h, T:T + 1], 1.0), writes=[cf.tr[h]])
            b.op(DVE, lambda: nc.vector.tensor_tensor_scan(out=cf.t[:, h, T - 1::-1], data0=fb.t[:, h, T - 1::-1], data1=ccs_zero[:, 0:T],
                                                           initial=1.0, op0=ALU.mult, op1=ALU.add),
                 reads=[fb.tr[h], mtr], writes=[cf.tr[h]])
            b.op(POOL, lambda: nc.gpsimd.tensor_scalar(out=fb.t[:, h, :], in0=fb.t[:, h, :], scalar1=-1.0, scalar2=1.0,
                                                       op0=ALU.mult, op1=ALU.add), writes=[fb.tr[h]])
            b.op(POOL, lambda: nc.gpsimd.tensor_tensor(out=kpb.t[:, h, :], in0=fb.t[:, h, :], in1=cf.t[:, h, 1:T + 1], op=ALU.mult),
                 reads=[fb.tr[h], cf.tr[h]], writes=[kpb.tr[h]])
        v_proj(subt, ar, hT_tr)
        kp_transpose(subt, ar)
        pP, pPt = next_psF()

        def pm():
            ins = None
            for h in range(4):
                for s in range(4):
                    ins = nc.tensor.matmul(pP[:, h * 128:(h + 1) * 128], lhsT=kptok.t[:, s, h * 128:(h + 1) * 128],
                                           rhs=v.t[:, s, h * 128:(h + 1) * 128], start=(h == 0 and s == 0), stop=(s == 3), skip_group_check=True)
            return ins
        b.op(PE, pm, reads=kptok.tr + v.tr, writes=[pPt])
        for h in range(4):
            b.op(DVE, lambda: nc.vector.scalar_tensor_tensor(out=Sst[:, h, :], in0=Sst[:, h, :], scalar=cf.t[:, h, 0:1], in1=pP[:, h * 128:(h + 1) * 128],
                                                             op0=ALU.mult, op1=ALU.add),
                 reads=[pPt, cf.tr[h]], writes=[S_tr[h]])
        if lastp:
            ccs, ccs_tr = pre_sets[2]
            wb, wt = w_get("cc")
            for ch in range(4):
                ps, pt = fm_proj(wb, wt, ch, T, lambda k: hT[:, k, 0:T], 8, hT_tr)
                b.op(ACT, lambda: nc.scalar.copy(out=ccs[:, ch, :], in_=ps[:, 0:T]), reads=[pt], writes=[ccs_tr[ch]])
            wb, wt = w_get("cx")
            for ch in range(4):
                ps, pt = fm_proj(wb, wt, ch, T, lambda k: hT[:, k, 0:T], 8, hT_tr)
                b.op(DVE, lambda: nc.vector.tensor_tensor(out=uprev[:, ch, :], in0=ps[:, T - 2:T], in1=ccs[:, ch, T - 2:T], op=ALU.mult),
                     reads=[pt, ccs_tr[ch]], writes=[uprev_tr])

    def full_tile(kind, t, prenormed=False):
        is_s = kind == "sample"
        if is_s:
            T, L, last = 64, 32, 15
            subt = [(0, 64)]
            segs = [(0, 32), (32, 32)]
        else:
            T, L, last = TM, 64, 63
            subt = [(i * 128, 128) for i in range(4)]
            segs = [(0, TM)]
        nsub = len(subt)
        nseg = len(segs)
        ti = n_pre + (n_main if is_s else t)
        xr, xrt = ensure_loaded(ti)
        hTr = hT_tr[0:nsub]
        if not prenormed:
            rmsnorm_to_hT(xr, xrt, subt, G_MIX)
        did_prenorm = False
        A = lambda n: b.newtr(n, arena=True)
        hcol = lambda k: hT[:, k, 0:T]

        with ExitStack() as ph:
            def fm(nme, nch, dt, W=T):
                f = FM.__new__(FM)
                f.t = ph.enter_context(nc.sbuf_tensor(b.nm(nme), [128, nch, W], dt))
                f.tr = [A(nme) for _ in range(nch)]
                return f
            UW = T + 2 * nseg
            ccs = fm("ccs", 4, F32)
            u = fm("u", 4, F32, UW)
            z = fm("z", 4, BF16)
            ar = {"fb": fm("fb", 4, F32), "eb": fm("eb", 4, F32),
                  "kt": fm("kt", 4, BF16), "kp": fm("kp", 4, BF16)}
            qt = fm("qt", 4, BF16)
            sgt = fm("sgt", 4, BF16)
            oN = fm("oN", 4, BF16)
            sga = fm("sga", 8, BF16)
            sgb = fm("sgb", 8, BF16)
            mrgb = fm("mrgb", 8, BF16)
            for nme in ["v", "kptok"]:
                f = FM.__new__(FM)
                f.t = ph.enter_context(nc.sbuf_tensor(b.nm(nme), [128, nsub, 512], BF16))
                f.tr = [A(nme) for _ in range(nsub)]
                ar[nme] = f
            tmpA = [ph.enter_context(nc.sbuf_tensor(b.nm("tmpA"), [128, 512], F32)) for _ in range(4)]
            tmpA_tr = [A("tmpA") for _ in range(4)]
            tmpB = [ph.enter_context(nc.sbuf_tensor(b.nm("tmpB"), [128, 512], F32)) for _ in range(2)]
            tmpB_tr = [A("tmpB"), A("tmpB")]
            scb = [ph.enter_context(nc.sbuf_tensor(b.nm("scb"), [128, 512], BF16)) for _ in range(2)]
            scb_tr = [A("scb"), A("scb")]
            sqb = ph.enter_context(nc.sbuf_tensor(b.nm("sqb"), [128, 512], BF16))
            sqb_tr = A("sqb")
            tcnt = {"a": 0, "b": 0, "sc": 0, "cv": 0}
            ctmp, ctmp_tr = tmpB, tmpB_tr
            if is_s:
                Ss = ph.enter_context(nc.sbuf_tensor(b.nm("Ss"), [128, 2, 4, 128], F32))
                Ssb = ph.enter_context(nc.sbuf_tensor(b.nm("Ssb"), [128, 2, 4, 128], BF16))
                Ss_tr = [[A("Ss") for _ in range(4)] for _ in range(2)]
                Ssb_tr = [[A("Ssb") for _ in range(4)] for _ in range(2)]
                for j in range(2):
                    b.dma(SP, Ss[:, j, :, :], shg[j].rearrange("h k v -> k h v"), ld_slot(), writes=Ss_tr[j])
                    b.op(ACT, lambda: nc.scalar.copy(out=Ssb[:, j, :, :], in_=Ss[:, j, :, :]), reads=Ss_tr[j], writes=Ssb_tr[j])

            hgrn_gates(T, L, last, ar, hTr, tmpA, tmpA_tr, part=1)
            wb, wt = w_get("cc")
            for ch in range(4):
                ps, pt = fm_proj(wb, wt, ch, T, hcol, 8, hTr)
                b.op(ACT, lambda: nc.scalar.copy(out=ccs.t[:, ch, :], in_=ps[:, 0:T]), reads=[pt], writes=[ccs.tr[ch]])
            hgrn_gates(T, L, last, ar, hTr, tmpA, tmpA_tr, part=2)
            wb, wt = w_get("cx")
            if is_s:
                for g in range(2):
                    o = g * (32 + 2)
                    b.dma(SP, u.t[:, :, o:o + 2], sconv[:, :, g, :], ld_slot(), writes=u.tr)
            else:
                b.op(POOL, lambda: nc.gpsimd.tensor_copy(out=u.t[:, :, 0:2], in_=uprev[:, :, :]), reads=[uprev_tr], writes=u.tr)
            for ch in range(4):
                ps, pt = fm_proj(wb, wt, ch, T, hcol, 8, hTr)
                for g, (g0, gl) in enumerate(segs):
                    o = g0 + 2 * g + 2
                    b.op(DVE, lambda: nc.vector.tensor_tensor(out=u.t[:, ch, o:o + gl], in0=ps[:, g0:g0 + gl], in1=ccs.t[:, ch, g0:g0 + gl], op=ALU.mult),
                         reads=[pt, ccs.tr[ch]], writes=[u.tr[ch]])
            if is_s:
                for g in range(2):
                    o = g * 34 + 16
                    b.dma(POOL, convs_o[:, :, g, :], u.t[:, :, o:o + 2], st_slot(), reads=u.tr)
            else:
                b.op(POOL, lambda: nc.gpsimd.tensor_copy(out=uprev[:, :, :], in_=u.t[:, :, T:T + 2]), reads=u.tr, writes=[uprev_tr])
            for ch in range(4):
                for g, (g0, gl) in enumerate(segs):
                    o = g0 + 2 * g
                    yv = ccs.t[:, ch, g0:g0 + gl]
                    cw = lambda j: cvec[:, 40 + ch * 3 + j:40 + ch * 3 + j + 1]
                    b.op(DVE, lambda: nc.vector.tensor_scalar(out=yv, in0=u.t[:, ch, o:o + gl], scalar1=cw(0), scalar2=None, op0=ALU.mult),
                         reads=[u.tr[ch], const_tr], writes=[ccs.tr[ch]])
                    b.op(DVE, lambda: nc.vector.scalar_tensor_tensor(out=yv, in0=u.t[:, ch, o + 1:o + 1 + gl], scalar=cw(1), in1=yv, op0=ALU.mult, op1=ALU.add),
                         reads=[u.tr[ch]], writes=[ccs.tr[ch]])
                    b.op(DVE, lambda: nc.vector.scalar_tensor_tensor(out=yv, in0=u.t[:, ch, o + 2:o + 2 + gl], scalar=cw(2), in1=yv, op0=ALU.mult, op1=ALU.add),
                         reads=[u.tr[ch]], writes=[ccs.tr[ch]])
            wb, wt = w_get("cb")
            ensure_loaded(ti + 1)
            for ch in range(4):
                ps, pt = fm_proj(wb, wt, ch, T, hcol, 8, hTr)
                b.op(DVE, lambda: nc.vector.tensor_tensor(out=z.t[:, ch, :], in0=ps[:, 0:T], in1=ccs.t[:, ch, :], op=ALU.mult),
                     reads=[pt, ccs.tr[ch]], writes=[z.tr[ch]])

            eb = ar["eb"]
            wb, wt = w_get("hq")
            for h in range(4):
                ps, pt = fm_proj(wb, wt, h, T, hcol, 8, hTr)
                ai = tcnt["a"] % 4
                tcnt["a"] += 1
                b.op(ACT, lambda: nc.scalar.activation(out=tmpA[ai][:, 0:T], in_=ps[:, 0:T], func=AF.Silu), reads=[pt], writes=[tmpA_tr[ai]])
                b.op(POOL, lambda: nc.gpsimd.tensor_tensor(out=qt.t[:, h, :], in0=tmpA[ai][:, 0:T], in1=eb.t[:, h, :], op=ALU.mult),
                     reads=[tmpA_tr[ai], eb.tr[h]], writes=[qt.tr[h]])
            v_proj(subt, ar, hTr)
            kp_transpose(subt, ar)

            kt, v, kptok = ar["kt"], ar["v"], ar["kptok"]
            mask = mask32 if is_s else mask64
            gate_units = [("hg", sgt, h, h, AF.Silu) for h in range(4)]
            gate_units += [(nme, dst, ch4, int(nme[2]) * 4 + ch4, AF.Sigmoid)
                           for nme, dst in [("ga0", sga), ("ga1", sga), ("gb0", sgb), ("gb1", sgb)] for ch4 in range(4)]
            gstate = {"i": 0, "wb": None}

            def emit_gates(kmax):
                for _ in range(kmax):
                    if gstate["i"] >= len(gate_units):
                        return
                    nme, dst, ch4, dch, gfunc = gate_units[gstate["i"]]
                    gstate["i"] += 1
                    if ch4 == 0:
                        gstate["wb"] = w_get(nme)
                    wb, wt = gstate["wb"]
                    ps, pt = fm_proj(wb, wt, ch4, T, hcol, 8, hTr)
                    b.op(ACT, lambda: nc.scalar.activation(out=dst.t[:, dch, :], in_=ps[:, 0:T], func=gfunc), reads=[pt], writes=[dst.tr[dch]])

            def emit_scores(s):
                c0, n = subt[s]
                W4 = 4 * n
                pS, pSt = next_psF()

                def sc():
                    ins = None
                    for h in range(4):
                        ins = nc.tensor.matmul(pS[:n, h * n:(h + 1) * n], lhsT=kt.t[:, h, c0:c0 + n], rhs=qt.t[:, h, c0:c0 + n], start=(h == 0), stop=True, skip_group_check=True)
                    return ins
                b.op(PE, sc, reads=kt.tr + qt.tr, writes=[pSt])
                si = tcnt["sc"] % 2
                tcnt["sc"] += 1
                mb = mask[:n, :n].unsqueeze(1).to_broadcast([n, 4, n])
                b.op(DVE, lambda: nc.vector.tensor_tensor(out=scb[si][:n, 0:W4].rearrange("p (h t) -> p h t", h=4),
                                                          in0=pS[:n, 0:W4].rearrange("p (h t) -> p h t", h=4), in1=mb, op=ALU.mult),
                     reads=[pSt, mtr], writes=[scb_tr[si]])
                return si

            cnt["in_rec"] = 1
            if is_s:
                emit_gates(4)
            pend_si = emit_scores(0)
            for s, (c0, n) in enumerate(subt):
                nchk = n // L
                W4 = 4 * n
                pO, pOt = next_psO()
                si = pend_si

                def intra():
                    ins = None
                    for h in range(4):
                        ins = nc.tensor.matmul(pO[:, h * n:(h + 1) * n], lhsT=v.t[:n, s, h * 128:(h + 1) * 128], rhs=scb[si][:n, h * n:(h + 1) * n],
                                               start=(h == 0), stop=False, skip_group_check=True)
                    return ins
                b.op(PE, intra, reads=[v.tr[s], scb_tr[si]], writes=[pOt])
                for c in range(nchk):
                    r0 = c * L

                    def sterm():
                        ins = None
                        for h in range(4):
                            Sb_ap = Ssb[:, c, h, :] if is_s else Sb[:, h, :]
                            ins = nc.tensor.matmul(pO[:, h * n + r0:h * n + r0 + L], lhsT=Sb_ap, rhs=qt.t[:, h, c0 + r0:c0 + r0 + L],
                                                   start=False, stop=(c == nchk - 1), skip_group_check=True)
                        return ins
                    b.op(PE, sterm, reads=(Ssb_tr[c] if is_s else Sb_tr) + qt.tr, writes=[pOt])
                    pP, pPt = next_psF()

                    def pm():
                        ins = None
                        for h in range(4):
                            ins = nc.tensor.matmul(pP[:, h * 128:(h + 1) * 128], lhsT=kptok.t[r0:r0 + L, s, h * 128:(h + 1) * 128],
                                                   rhs=v.t[r0:r0 + L, s, h * 128:(h + 1) * 128], start=(h == 0), stop=True, skip_group_check=True)
                        return ins
                    b.op(PE, pm, reads=[kptok.tr[s], v.tr[s]], writes=[pPt])
                    for h in range(4):
                        if is_s:
                            S_ap, S_t = Ss[:, c, h, :], Ss_tr[c][h]
                        else:
                            S_ap, S_t = Sst[:, h, :], S_tr[h]
                        e_ap = eb.t[:, h, c0 + r0 + last:c0 + r0 + last + 1]
                        b.op(DVE, lambda: nc.vector.scalar_tensor_tensor(out=S_ap, in0=S_ap, scalar=e_ap, in1=pP[:, h * 128:(h + 1) * 128],
                                                                         op0=ALU.mult, op1=ALU.add),
                             reads=[pPt, eb.tr[h]], writes=[S_t])
                    if not is_s:
                        b.op(ACT, lambda: nc.scalar.copy(out=Sb[:, :, :], in_=Sst[:, :, :]), reads=S_tr, writes=Sb_tr)
                    emit_gates(2)
                if s + 1 < nsub:
                    pend_si = emit_scores(s + 1)
                b.op(ACT, lambda: nc.scalar.activation(out=sqb[:, 0:W4], in_=pO[:, 0:W4], func=AF.Square), reads=[pOt], writes=[sqb_tr])
                pN, pNt = next_psF()
                mm(pN[:, 0:W4], [(ones[:, :], sqb[:, 0:W4])], reads=[sqb_tr, mtr], ptr=pNt)
                ai = tcnt["a"] % 4
                tcnt["a"] += 1
                ta = tmpA[ai]
                b.op(ACT, lambda: nc.scalar.activation(out=ta[:, 0:W4], in_=pN[:, 0:W4], func=AF.Ln, scale=1.0 / 128, bias=EPS),
                     reads=[pNt], writes=[tmpA_tr[ai]])
                b.op(ACT, lambda: nc.scalar.activation(out=ta[:, 0:W4], in_=ta[:, 0:W4], func=AF.Exp, scale=-0.5), writes=[tmpA_tr[ai]])
                b.op(DVE, lambda: nc.vector.tensor_tensor(out=ta[:, 0:W4], in0=pO[:, 0:W4], in1=ta[:, 0:W4], op=ALU.mult),
                     reads=[pOt], writes=[tmpA_tr[ai]])
                tav = ta[:, 0:W4].rearrange("p (h t) -> p h t", h=4)
                b.op(DVE, lambda: nc.vector.scalar_tensor_tensor(out=oN.t[:, :, c0:c0 + n], in0=tav, scalar=cvec[:, 52:53],
                                                                 in1=sgt.t[:, :, c0:c0 + n], op0=ALU.mult, op1=ALU.mult),
                     reads=[tmpA_tr[ai], const_tr] + sgt.tr, writes=oN.tr)
            if is_s:
                for j in range(2):
                    b.dma(POOL, hgs_o[j].rearrange("h k v -> k h v"), Ss[:, j, :, :], st_slot(), reads=Ss_tr[j])
            cnt["in_rec"] = 0
            emit_gates(100)
            for c in range(2):
                wb, wt = w_get("cvhg%d" % c)
                for ch4 in range(4):
                    dch = c * 4 + ch4
                    pa, pat = fm_proj(wb, wt, ch4, T, lambda k: z.t[:, k, :], 4, z.tr, koff=0)
                    pbb, pbt = fm_proj(wb, wt, ch4, T, lambda k: oN.t[:, k, :], 4, oN.tr, koff=4)
                    ai = tcnt["a"] % 4
                    tcnt["a"] += 1
                    bi = tcnt["b"] % 2
                    tcnt["b"] += 1
                    b.op(DVE, lambda: nc.vector.tensor_tensor(out=tmpA[ai][:, 0:T], in0=pa[:, 0:T], in1=sga.t[:, dch, :], op=ALU.mult),
                         reads=[pat, sga.tr[dch]], writes=[tmpA_tr[ai]])
                    b.op(DVE, lambda: nc.vector.tensor_tensor(out=tmpB[bi][:, 0:T], in0=pbb[:, 0:T], in1=sgb.t[:, dch, :], op=ALU.mult),
                         reads=[pbt, sgb.tr[dch]], writes=[tmpB_tr[bi]])
                    b.op(POOL, lambda: nc.gpsimd.tensor_tensor(out=mrgb.t[:, dch, :], in0=tmpA[ai][:, 0:T], in1=tmpB[bi][:, 0:T], op=ALU.add),
                         reads=[tmpA_tr[ai], tmpB_tr[bi]], writes=[mrgb.tr[dch]])
            wpair = [w_get("wo0"), w_get("wo1", ahead=False)]
            for s, (c0, n) in enumerate(subt):
                for c in range(2):
                    wb, wt = wpair[c]
                    ps, pt = tok_proj(wb, wt, lambda k: mrgb.t[:, k, c0:c0 + n], 8, n, mrgb.tr)
                    xsl = xr[:n, s, c * 512:(c + 1) * 512]
                    b.op(DVE, lambda: nc.vector.tensor_tensor(out=xsl, in0=ps[:n, :], in1=xsl, op=ALU.add), reads=[pt], writes=[xrt[s]])
            b.end_phase()

        if nsub == 4:
            rmsnorm_to_hT(xr, xrt, subt, G_X, sel=[0, 1, 2])
            rmsnorm_to_hT(xr, xrt, subt, G_X, sel=[3])
        else:
            rmsnorm_to_hT(xr, xrt, subt, G_X)
        with ExitStack() as ph:
            def fm(nme, nch, dt, W=T):
                f = FM.__new__(FM)
                f.t = ph.enter_context(nc.sbuf_tensor(b.nm(nme), [128, nch, W], dt))
                f.tr = [A(nme) for _ in range(nch)]
                return f
            qT = fm("qT", 8, BF16)
            eT = fm("eT", 8, BF16)
            oxT = fm("oxT", 8, BF16)
            rden = fm("rden", 4, F32)
            if is_s:
                kst = ph.enter_context(nc.sbuf_tensor(b.nm("kst"), [128, 2, D], BF16))
                kst_tr = A("kst")
                mkTs = [fm("mkTs", 8, BF16, 256) for _ in range(2)]
                mvs = [fm("mvs", 2, BF16, D) for _ in range(2)]
                for j in range(2):
                    b.dma(POOL, kst[:, :, :], ck[j].rearrange("(c p) d -> p c d", p=128), pl_slot(), writes=[kst_tr])
                    for mc in range(2):
                        pb, pbt = next_psB()

                        def tp():
                            ins = None
                            for dc in range(8):
                                ins = nc.tensor.transpose(pb[:, dc * 128:(dc + 1) * 128], kst[:, mc, dc * 128:(dc + 1) * 128], ident[:, :])
                            return ins
                        b.op(PE, tp, reads=[kst_tr, mtr], writes=[pbt])
                        b.op(ACT, lambda: nc.scalar.copy(out=mkTs[j].t[:, :, mc * 128:(mc + 1) * 128],
                                                         in_=pb[:, :].rearrange("p (k t) -> p k t", t=128)), reads=[pbt], writes=mkTs[j].tr)
                    b.dma(POOL, mvs[j].t[:, :, :], cv[j].rearrange("(c p) d -> p c d", p=128), pl_slot(), writes=mvs[j].tr)
                agroups = [((0, 32), mkTs[0].t, mkTs[0].tr, mvs[0].t, mvs[0].tr), ((32, 32), mkTs[1].t, mkTs[1].tr, mvs[1].t, mvs[1].tr)]
            else:
                agroups = [((0, T), mkT, [mkT_tr], mv, [mv_tr])]
            for c in range(2):
                wb, wt = w_get("xq%d" % c)
                banks = fm_block_split(wb, wt) if (c == 0 and nsub == 4) else None
                for ch4 in range(4):
                    ps, pt = banks[ch4] if banks else fm_proj(wb, wt, ch4, T, hcol, 8, hTr)
                    dch = c * 4 + ch4
                    b.op(ACT, lambda: nc.scalar.activation(out=qT.t[:, dch, :], in_=ps[:, 0:T], func=AF.Copy, scale=1.0 / 16.0), reads=[pt], writes=[qT.tr[dch]])
            for ((g0, gl), mk_t, mk_trs, mv_t, mv_trs) in agroups:
                gs = slice(g0, g0 + gl)
                for h in range(4):
                    for mc in range(2):
                        ps, pt = next_psF()
                        mm(ps[:, 0:gl], [(mk_t[:, 2 * h + dd, mc * 128:(mc + 1) * 128], qT.t[:, 2 * h + dd, gs]) for dd in range(2)],
                           reads=mk_trs + [qT.tr[2 * h], qT.tr[2 * h + 1]], ptr=pt)
                        b.op(ACT, lambda: nc.scalar.activation(out=eT.t[:, h * 2 + mc, gs], in_=ps[:, 0:gl], func=AF.Exp), reads=[pt], writes=[eT.tr[h * 2 + mc]])
                    ps, pt = next_psF()
                    mm(ps[:, 0:gl], [(ones[:, :], eT.t[:, h * 2 + mc, gs]) for mc in range(2)], reads=[mtr, eT.tr[h * 2], eT.tr[h * 2 + 1]], ptr=pt)
                    b.op(ACT, lambda: nc.scalar.activation(out=rden.t[:, h, gs], in_=ps[:, 0:gl], func=AF.Ln), reads=[pt], writes=[rden.tr[h]])
                    b.op(ACT, lambda: nc.scalar.activation(out=rden.t[:, h, gs], in_=rden.t[:, h, gs], func=AF.Exp, scale=-1.0), writes=[rden.tr[h]])
                for dc in range(8):
                    h = dc // 2
                    ps, pt = next_psF()
                    mm(ps[:, 0:gl], [(mv_t[:, mc, dc * 128:(dc + 1) * 128], eT.t[:, h * 2 + mc, gs]) for mc in range(2)],
                       reads=mv_trs + [eT.tr[h * 2], eT.tr[h * 2 + 1]], ptr=pt)
                    b.op(DVE, lambda: nc.vector.tensor_tensor(out=oxT.t[:, dc, gs], in0=ps[:, 0:gl], in1=rden.t[:, h, gs], op=ALU.mult),
                         reads=[pt, rden.tr[h]], writes=[oxT.tr[dc]])
            wpair = [w_get("xo0"), w_get("xo1", ahead=False)]
            for s, (c0, n) in enumerate(subt):
                for c in range(2):
                    wb, wt = wpair[c]
                    ps, pt = tok_proj(wb, wt, lambda k: oxT.t[:, k, c0:c0 + n], 8, n, oxT.tr)
                    xsl = xr[:n, s, c * 512:(c + 1) * 512]
                    b.op(DVE, lambda: nc.vector.tensor_tensor(out=xsl, in0=ps[:n, :], in1=xsl, op=ALU.add), reads=[pt], writes=[xrt[s]])
            b.end_phase()

        if nsub == 4:
            rmsnorm_to_hT(xr, xrt, subt, G_FFN, sel=[0, 1, 2])
            rmsnorm_to_hT(xr, xrt, subt, G_FFN, sel=[3])
        else:
            rmsnorm_to_hT(xr, xrt, subt, G_FFN)
        with ExitStack() as ph:
            a2 = FM.__new__(FM)
            a2.t = ph.enter_context(nc.sbuf_tensor(b.nm("a2"), [128, 32, T], BF16))
            a2.tr = [A("a2") for _ in range(32)]
            rt = [ph.enter_context(nc.sbuf_tensor(b.nm("rt"), [128, 512], F32)) for _ in range(2)]
            rt_tr = [A("rt"), A("rt")]
            rc = 0
            for c in range(8):
                wb, wt = w_get("up%d" % c)
                banks = fm_block_split(wb, wt) if (c == 0 and nsub == 4) else None
                for ch4 in range(4):
                    ps, pt = banks[ch4] if banks else fm_proj(wb, wt, ch4, T, hcol, 8, hTr)
                    hch = c * 4 + ch4
                    ri = rc % 2
                    rc += 1
                    b.op(ACT, lambda: nc.scalar.activation(out=rt[ri][:, 0:T], in_=ps[:, 0:T], func=AF.Relu), reads=[pt], writes=[rt_tr[ri]])
                    b.op(POOL, lambda: nc.gpsimd.tensor_tensor(out=a2.t[:, hch, :], in0=rt[ri][:, 0:T], in1=rt[ri][:, 0:T], op=ALU.mult),
                         reads=[rt_tr[ri]], writes=[a2.tr[hch]])
            for c in range(2):
                acc = [next_psF() for _ in range(nsub)]
                for r in range(4):
                    wb, wt = w_get("dn%d_%d" % (r, c))
                    for s, (c0, n) in enumerate(subt):
                        tok_proj(wb, wt, lambda k: a2.t[:, r * 8 + k, c0:c0 + n], 8, n, a2.tr[r * 8:(r + 1) * 8],
                                 ps=acc[s][0], pt=acc[s][1], start=(r == 0), stop=(r == 3))
                for s, (c0, n) in enumerate(subt):
                    ps, pt = acc[s]
                    xsl = xr[:n, s, c * 512:(c + 1) * 512]
                    b.op(DVE, lambda: nc.vector.tensor_tensor(out=xsl, in0=ps[:n, :], in1=xsl, op=ALU.add), reads=[pt], writes=[xrt[s]])
                if c == 0 and ti + 1 < len(tiles) and tiles[ti + 1][0] != "pre":
                    nxr, nxrt = ensure_loaded(ti + 1)
                    nsubt = [(0, 64)] if tiles[ti + 1][0] == "sample" else SUB4
                    rmsnorm_to_hT(nxr, nxrt, nsubt, G_MIX)
                    did_prenorm = True
            b.end_phase()

        rsv, stt = norm_stats(xr, xrt, subt)
        for s, (c0, n) in enumerate(subt):
            yi = cnt["yo"] % 2
            cnt["yo"] += 1
            b.op(DVE, lambda: nc.vector.scalar_tensor_tensor(out=yout[yi][:n, :], in0=xr[:n, s, :], scalar=rsv[:n, s:s + 1], in1=gfin[:n, :],
                                                             op0=ALU.mult, op1=ALU.mult),
                 reads=[xrt[s], stt, gfin_tr], writes=[yout_tr[yi]])
            if is_s:
                b.dma(POOL, ys_o[c0:c0 + n, :], yout[yi][:n, :], st_slot(), reads=[yout_tr[yi]])
            else:
                b.dma(POOL, y_o[t * TM + c0:t * TM + c0 + n, :], yout[yi][:n, :], st_slot(), reads=[yout_tr[yi]])
        return did_prenorm

    if do_mem:
        mem_kv()
    with ExitStack() as pph:
        if n_pre:
            pre_alloc(pph)
        for p in range(n_pre):
            pre_tile(p, p == n_pre - 1)
        b.end_phase()
    b.op(ACT, lambda: nc.scalar.copy(out=Sb[:, :, :], in_=Sst[:, :, :]), reads=S_tr, writes=Sb_tr)
    pn = False
    for t in range(n_main):
        pn = full_tile("main", t, pn)
    b.dma(POOL, hg_o.rearrange("h k v -> k h v"), Sst[:, :, :], st_slot(), reads=S_tr)
    b.dma(POOL, conv_o[:, :, :], uprev[:, :, :], st_slot(), reads=[uprev_tr])
    if sample:
        full_tile("sample", 0, pn)
    fin = []
    for s in b.slots.values():
        if s.count:
            fin.append((s, s.count))
    for e in [PE, ACT, DVE, POOL]:
        if e.count:
            fin.append((e, e.count))
    b._wait(SP, fin)
    es.close()
    return nc


_NC_CACHE = {}


def _prep_inputs(inp):
    f = lambda a: np.ascontiguousarray(np.asarray(a, dtype=np.float32))
    x_prompt = f(inp["x_prompt"])
    x_sample = f(inp["x_sample"])
    mem_prompt = f(inp["mem_prompt"])
    state_conv = f(inp["state_conv"])
    state_hgrn = f(inp["state_hgrn"])
    ckk = f(inp["cache_mem_k"])
    cvv = f(inp["cache_mem_v"])
    p8 = lambda v: np.asarray(v, np.float32).reshape(8, 128).T
    cvec = np.zeros((128, NCV), np.float32)
    cvec[:, 0:8] = p8(inp["norm_mix"][0])
    cvec[:, 8:16] = p8(inp["norm_x"][0])
    cvec[:, 16:24] = p8(inp["norm_ffn"][0])
    cvec[:, 24:32] = p8(inp["norm_mem"][0])
    hl = np.asarray(inp["hg_lb"], np.float32)
    cvec[:, 32:36] = hl[0].reshape(4, 128).T
    cvec[:, 36:40] = hl[1].reshape(4, 128).T
    cw = np.asarray(inp["conv_w"], np.float32)[0]
    cvec[:, 40:52] = cw.reshape(3, 4, 128).transpose(2, 1, 0).reshape(128, 12)
    cvec[:, 52] = np.asarray(inp["hg_norm"], np.float32)[0]
    gfin = np.ascontiguousarray(np.broadcast_to(np.asarray(inp["norm_final"], np.float32)[None, :], (128, D)))
    shared = {
        "cvec": cvec, "gfin": gfin,
        "w_in": f(inp["w_in"][0]), "w_conv_out": f(inp["w_conv_out"][0]), "w_hg_out": f(inp["w_hg_out"][0]),
        "w_o": f(inp["w_o"][0]), "w_xq": f(inp["w_xq"][0]), "w_xk": f(inp["w_xk"][0]), "w_xv": f(inp["w_xv"][0]),
        "w_xo": f(inp["w_xo"][0]), "w_up": f(inp["w_up"][0]), "w_down": f(inp["w_down"][0]),
    }
    maps = []
    for c in range(N_CORES):
        bi, half = c // 2, c % 2
        m = dict(shared)
        m["xm"] = np.ascontiguousarray(x_prompt[bi, half * 4096:(half + 1) * 4096])
        m["xp"] = np.ascontiguousarray(x_prompt[bi, 0:4096]) if half == 1 else np.zeros((4096, D), np.float32)
        xs = np.zeros((64, D), np.float32)
        for j in range(2):
            xs[32 * j:32 * j + 16] = x_sample[2 * c + j]
        m["xs"] = xs
        m["mem"] = np.ascontiguousarray(mem_prompt[bi])
        sc = state_conv[0, 2 * c:2 * c + 2]
        m["sconv"] = np.ascontiguousarray(sc.reshape(2, 2, 4, 128).transpose(3, 2, 0, 1))
        m["shg"] = np.ascontiguousarray(state_hgrn[0, 2 * c:2 * c + 2])
        m["ck"] = np.ascontiguousarray(ckk[0, 2 * c:2 * c + 2].reshape(2, 256, D))
        m["cv"] = np.ascontiguousarray(cvv[0, 2 * c:2 * c + 2].reshape(2, 256, D))
        maps.append(m)
    return maps


def kernel(**inputs):
    if "nc" not in _NC_CACHE:
        _NC_CACHE["nc"] = build()
    nc = _NC_CACHE["nc"]
    maps = _prep_inputs(inputs)
    res = run_bass_kernel_spmd(nc, maps, core_ids=list(range(N_CORES)))
    R = res.results
    y_prompt = np.zeros((4, 8192, D), np.float32)
    y_sample = np.zeros((16, 16, D), np.float32)
    conv_p = np.zeros((1, 4, 2, 512), np.float32)
    hg_p = np.zeros((1, 4, 4, 128, 128), np.float32)
    mk_p = np.zeros((1, 4, 256, 4, 256), np.float32)
    mv_p = np.zeros((1, 4, 256, 4, 256), np.float32)
    conv_s = np.zeros((1, 16, 2, 512), np.float32)
    hg_s = np.zeros((1, 16, 4, 128, 128), np.float32)
    for c in range(N_CORES):
        bi, half = c // 2, c % 2
        r = R[c]
        y_prompt[bi, half * 4096:(half + 1) * 4096] = r["y"]
        for j in range(2):
            y_sample[2 * c + j] = r["ys"][32 * j:32 * j + 16]
            conv_s[0, 2 * c + j] = r["convs_o"][:, :, j, :].transpose(2, 1, 0).reshape(2, 512)
            hg_s[0, 2 * c + j] = r["hgs_o"][j]
        if half == 1:
            conv_p[0, bi] = r["conv_o"].transpose(2, 1, 0).reshape(2, 512)
            hg_p[0, bi] = r["hg_o"]
        else:
            mk_p[0, bi] = r["mk_o"].reshape(256, 4, 256)
            mv_p[0, bi] = r["mv_o"].reshape(256, 4, 256)
    return (y_prompt, y_sample, conv_p, hg_p, mk_p, mv_p, conv_s, hg_s)
```

```python
import numpy as np
from contextlib import ExitStack
import concourse.bass as bass
import concourse.mybir as mybir
from concourse.bass_utils import run_bass_kernel_spmd

F32 = mybir.dt.float32
BF16 = mybir.dt.bfloat16
AF = mybir.ActivationFunctionType
ALU = mybir.AluOpType

D = 1024
TM = 512
EPS = 1e-6
NCV = 56
N_CORES = 8


class Tr:
    __slots__ = ("w", "r", "name")

    def __init__(self, name="", fence=None):
        self.w = None
        self.r = list(fence) if fence else []
        self.name = name


class Src:
    def __init__(self, name, sem, unit):
        self.name, self.sem, self.unit, self.count = name, sem, unit, 0


class Eng(Src):
    def __init__(self, name, h, sem, self_sync):
        super().__init__(name, sem, 1)
        self.h = h
        self.seen = {}
        self.self_sync = self_sync


class FM:
    def __init__(self, b, name, nch, T, dt, es):
        self.t = es.enter_context(b.nc.sbuf_tensor(b.nm(name), [128, nch, T], dt))
        self.tr = [b.newtr(name + str(i)) for i in range(nch)]
        self.nch, self.T = nch, T


class Bld:
    def __init__(self):
        self.nc = bass.Bass("TRN2", target_bir_lowering=False)
        self.es = ExitStack()
        self.uid = 0
        self.fence = {}
        self.phase_trs = []
        nc = self.nc
        mk = lambda n: self.es.enter_context(nc.semaphore(n))
        self.PE = Eng("pe", nc.tensor, mk("s_pe"), False)
        self.ACT = Eng("act", nc.scalar, mk("s_act"), True)
        self.DVE = Eng("dve", nc.vector, mk("s_dve"), True)
        self.POOL = Eng("pool", nc.gpsimd, mk("s_pool"), True)
        self.SP = Eng("sp", nc.sync, mk("s_sp"), False)
        self.slots = {}

    def nm(self, n):
        self.uid += 1
        return "%s_%d" % (n, self.uid)

    def slot(self, name):
        if name not in self.slots:
            self.slots[name] = Src(name, self.es.enter_context(self.nc.semaphore("d_" + name)), 16)
        return self.slots[name]

    def newtr(self, name="", arena=False):
        if arena:
            t = Tr(name, fence=list(self.fence.items()))
            self.phase_trs.append(t)
            return t
        return Tr(name)

    def end_phase(self):
        for t in self.phase_trs:
            acc = list(t.r)
            if t.w:
                acc.append(t.w)
            for (s, i) in acc:
                if self.fence.get(s, 0) < i:
                    self.fence[s] = i
        self.phase_trs = []

    def _wait(self, e, deps):
        need = {}
        for (s, i) in deps:
            if s is e:
                if not e.self_sync:
                    continue
            if need.get(s, 0) < i:
                need[s] = i
        for s, i in need.items():
            if e.seen.get(s, 0) < i:
                e.h.wait_ge(s.sem, i * s.unit)
                e.seen[s] = i

    @staticmethod
    def _deps(reads, writes):
        deps = []
        for t in reads:
            if t.w:
                deps.append(t.w)
        for t in writes:
            if t.w:
                deps.append(t.w)
            deps.extend(t.r)
        return deps

    def op(self, e, fn, reads=(), writes=()):
        self._wait(e, self._deps(reads, writes))
        ins = fn()
        e.count += 1
        ins.then_inc(e.sem, 1)
        me = (e, e.count)
        for t in reads:
            t.r.append(me)
        for t in writes:
            t.w = me
            t.r = []

    def dma(self, e, out, in_, slot, reads=(), writes=(), chain=False):
        deps = self._deps(reads, writes)
        if slot.count > 0 and not chain:
            deps.append((slot, slot.count))
        self._wait(e, deps)
        e.h.dma_start(out=out, in_=in_).then_inc(slot.sem, 16)
        slot.count += 1
        me = (slot, slot.count)
        for t in reads:
            t.r.append(me)
        for t in writes:
            t.w = me
            t.r = []


def build(n_pre=8, n_main=8, sample=True, do_mem=True):
    b = Bld()
    nc = b.nc
    es = b.es
    PE, ACT, DVE, POOL, SP = b.PE, b.ACT, b.DVE, b.POOL, b.SP

    def din(name, shape):
        return nc.dram_tensor(name, list(shape), F32, kind="ExternalInput").ap()

    def dout(name, shape):
        return nc.dram_tensor(name, list(shape), F32, kind="ExternalOutput").ap()

    xm = din("xm", [4096, D])
    xp = din("xp", [4096, D])
    xs = din("xs", [64, D])
    mem = din("mem", [256, D])
    sconv = din("sconv", [128, 4, 2, 2])
    shg = din("shg", [2, 4, 128, 128])
    ck = din("ck", [2, 256, D])
    cv = din("cv", [2, 256, D])
    cvec_d = din("cvec", [128, NCV])
    gfin_d = din("gfin", [128, D])
    w_in = din("w_in", [D, 5632])
    w_conv_out = din("w_conv_out", [512, D])
    w_hg_out = din("w_hg_out", [512, D])
    w_o = din("w_o", [D, D])
    w_xq = din("w_xq", [D, D])
    w_xk = din("w_xk", [D, D])
    w_xv = din("w_xv", [D, D])
    w_xo = din("w_xo", [D, D])
    w_up = din("w_up", [D, 4096])
    w_down = din("w_down", [4096, D])

    y_o = dout("y", [4096, D])
    ys_o = dout("ys", [64, D])
    conv_o = dout("conv_o", [128, 4, 2])
    hg_o = dout("hg_o", [4, 128, 128])
    mk_o = dout("mk_o", [256, D])
    mv_o = dout("mv_o", [256, D])
    convs_o = dout("convs_o", [128, 4, 2, 2])
    hgs_o = dout("hgs_o", [2, 4, 128, 128])

    def kp(ap):
        return ap.rearrange("(k p) c -> p k c", p=128)

    blocks = {}
    for j, nme in enumerate(["cb", "cc", "cx", "hq", "hf", "hi", "hg", "ga0", "ga1", "gb0", "gb1"]):
        blocks[nme] = [(kp(w_in)[:, :, j * 512:(j + 1) * 512], 0, 8)]
    for c in range(2):
        cs = slice(c * 512, (c + 1) * 512)
        blocks["cvhg%d" % c] = [(kp(w_conv_out)[:, :, cs], 0, 4), (kp(w_hg_out)[:, :, cs], 4, 4)]
        blocks["wo%d" % c] = [(kp(w_o)[:, :, cs], 0, 8)]
        blocks["xq%d" % c] = [(kp(w_xq)[:, :, cs], 0, 8)]
        blocks["xk%d" % c] = [(kp(w_xk)[:, :, cs], 0, 8)]
        blocks["xv%d" % c] = [(kp(w_xv)[:, :, cs], 0, 8)]
        blocks["xo%d" % c] = [(kp(w_xo)[:, :, cs], 0, 8)]
        for r in range(4):
            blocks["dn%d_%d" % (r, c)] = [(kp(w_down[r * 1024:(r + 1) * 1024, :])[:, :, cs], 0, 8)]
    for c in range(8):
        blocks["up%d" % c] = [(kp(w_up)[:, :, c * 512:(c + 1) * 512], 0, 8)]
    bnames = list(blocks.keys())
    bidx = {n: i for i, n in enumerate(bnames)}
    wscr = nc.dram_tensor("wscr", [len(bnames), 128, 8, 512], BF16, kind="Internal").ap()
    scr_tr = {n: Tr("scr_" + n) for n in bnames}
    converted = set()

    main_seq = (["hf", "cc", "cx", "cb", "hq", "hi", "hg", "ga0", "ga1", "gb0", "gb1", "cvhg0", "cvhg1",
                 "wo0", "wo1", "xq0", "xq1", "xo0", "xo1"] + ["up%d" % c for c in range(8)]
                + ["dn%d_%d" % (r, c) for c in range(2) for r in range(4)])
    seq = ["xk0", "xk1", "xv0", "xv1"] if do_mem else []
    for p in range(n_pre):
        seq += ["hf", "hi"]
        if p == n_pre - 1:
            seq += ["cc", "cx"]
    for t in range(n_main):
        seq += main_seq
    if sample:
        seq += main_seq

    NWB = 3
    wbuf = [es.enter_context(nc.sbuf_tensor("wbuf%d" % i, [128, 8, 512], BF16)) for i in range(NWB)]
    wtr = [Tr("wbuf%d" % i) for i in range(NWB)]
    wslot = [b.slot("wld%d" % i) for i in range(NWB)]
    wslot_sw = [b.slot("wlds%d" % i) for i in range(NWB)]
    sslot = [b.slot("wst%d" % i) for i in range(4)]
    wstate = {"issued": 0, "pos": 0, "nst": 0}

    def w_issue(j):
        name = seq[j]
        s = j % NWB
        if name not in converted:
            converted.add(name)
            first = True
            for (src, k0, nk) in blocks[name]:
                b.dma(POOL, wbuf[s][:, k0:k0 + nk, :], src, wslot_sw[s], writes=[wtr[s]], chain=not first)
                first = False
            st = sslot[wstate["nst"] % 4]
            wstate["nst"] += 1
            b.dma(POOL, wscr[bidx[name]], wbuf[s][:, :, :], st, reads=[wtr[s]], writes=[scr_tr[name]])
        else:
            b.dma(SP, wbuf[s][:, :, :], wscr[bidx[name]], wslot[s], reads=[scr_tr[name]], writes=[wtr[s]])

    def w_get(expect, ahead=True):
        i = wstate["pos"]
        assert seq[i] == expect, (seq[i], expect, i)
        while wstate["issued"] < (min(i + NWB, len(seq)) if ahead else i + 1):
            w_issue(wstate["issued"])
            wstate["issued"] += 1
        wstate["pos"] += 1
        return wbuf[i % NWB], wtr[i % NWB]

    def sb(name, shape, dt=F32):
        return es.enter_context(nc.sbuf_tensor("sb_" + name, list(shape), dt))

    NXB = 2
    xres = [sb("xres%d" % i, [128, 4, D]) for i in range(NXB)]
    xres_tr = [[Tr("xres") for _ in range(4)] for _ in range(NXB)]
    hT = sb("hT", [128, 8, TM], BF16)
    hT_tr = [Tr("hT%d" % s) for s in range(4)]
    Sst = sb("Sst", [128, 4, 128])
    S_tr = [Tr("S%d" % h) for h in range(4)]
    Sb = sb("Sb", [128, 4, 128], BF16)
    Sb_tr = [Tr("Sb%d" % h) for h in range(4)]
    cvec = sb("cvec", [128, NCV])
    gfin = sb("gfin", [128, D])
    lbv = sb("lbv", [128, 4])
    omlb = sb("omlb", [128, 4])
    ident = sb("ident", [128, 128], BF16)
    ones = sb("ones", [128, 128], BF16)
    mask64 = sb("mask64", [128, 128])
    mask32 = sb("mask32", [64, 64])
    zeros = sb("zeros", [128, 64])
    neghalf = sb("neghalf", [128, 4])
    ccs_zero = sb("zeros512", [128, TM])
    mkT = sb("mkT", [128, 8, 256], BF16)
    mv = sb("mv", [128, 2, D], BF16)
    mkT_tr, mv_tr = Tr("mkT"), Tr("mv")
    uprev = sb("uprev", [128, 4, 2])
    uprev_tr = Tr("uprev")
    junk = sb("junk", [128, D], BF16)
    junk_tr = Tr("junk")
    xn = [sb("xn%d" % i, [128, D], BF16) for i in range(2)]
    xn_tr = [Tr("xn0"), Tr("xn1")]
    yout = [sb("yout%d" % i, [128, D]) for i in range(2)]
    yout_tr = [Tr("yo0"), Tr("yo1")]
    stat = sb("stat", [128, 16])
    stat_tr = [Tr("stat%d" % i) for i in range(4)]
    const_tr = Tr("const")
    cnt = {"xn": 0, "yo": 0, "st": 0, "psF": 0, "psB": 0, "ld": 0, "os": 0}

    NPF = 4
    psF = [es.enter_context(nc.psum_tensor("psF%d" % i, [128, 512], F32)) for i in range(NPF)]
    psF_tr = [Tr("psF%d" % i) for i in range(NPF)]
    psB = [es.enter_context(nc.psum_tensor("psB%d" % i, [128, 1024], BF16)) for i in range(2)]
    psB_tr = [Tr("psB0"), Tr("psB1")]

    psO = [es.enter_context(nc.psum_tensor("psO%d" % i, [128, 512], F32)) for i in range(2)]
    psO_tr = [Tr("psO0"), Tr("psO1")]

    def next_psO():
        i = cnt.setdefault("psO", 0) % 2
        cnt["psO"] += 1
        return psO[i], psO_tr[i]

    def next_psF():
        if cnt.get("in_rec", 0):
            i = cnt["psF"] % NPF
            cnt["psF"] += 1
            return psF[i], psF_tr[i]
        i = cnt.setdefault("ps6", 0) % (NPF + 2)
        cnt["ps6"] += 1
        if i < NPF:
            return psF[i], psF_tr[i]
        return psO[i - NPF], psO_tr[i - NPF]

    def next_psB():
        i = cnt["psB"] % 2
        cnt["psB"] += 1
        return psB[i], psB_tr[i]

    ldslots = [b.slot("ld%d" % i) for i in range(6)]
    stslots = [b.slot("st%d" % i) for i in range(6)]

    def ld_slot():
        cnt["ld"] += 1
        return ldslots[cnt["ld"] % 6]

    plslots = [b.slot("pl%d" % i) for i in range(2)]

    def pl_slot():
        cnt["pl"] = cnt.get("pl", 0) + 1
        return plslots[cnt["pl"] % 2]

    def st_slot():
        cnt["os"] += 1
        return stslots[cnt["os"] % 6]

    def mm(out_ap, pairs, reads, ptr, start=True, stop=True):
        def fn():
            ins = None
            n = len(pairs)
            for i, (l, r) in enumerate(pairs):
                ins = nc.tensor.matmul(out_ap, lhsT=l, rhs=r, start=(start and i == 0), stop=(stop and i == n - 1))
            return ins
        b.op(PE, fn, reads=reads, writes=[ptr])

    b.dma(SP, cvec[:, :], cvec_d[:, :], ld_slot(), writes=[const_tr])
    gfin_tr = Tr("gfin")
    b.dma(SP, gfin[:, :], gfin_d[:, :], ld_slot(), writes=[gfin_tr])
    mtr = Tr("masks")

    P = lambda fn, wr: b.op(POOL, fn, writes=wr)
    id_tr = Tr("ident")
    P(lambda: nc.gpsimd.memset(ident[:, :], 1.0), [id_tr])
    P(lambda: nc.gpsimd.affine_select(out=ident[:, :], in_=ident[:, :], pattern=[[-1, 128]], compare_op=ALU.is_equal,
                                      fill=0.0, base=0, channel_multiplier=1), [id_tr])
    m64_tr, m32_tr = Tr("m64"), Tr("m32")
    P(lambda: nc.gpsimd.memset(mask64[:, :], 1.0), [m64_tr])
    P(lambda: nc.gpsimd.memset(mask32[:, :], 1.0), [m32_tr])
    P(lambda: nc.gpsimd.affine_select(out=mask64[:, :], in_=mask64[:, :], pattern=[[1, 128]], compare_op=ALU.is_ge,
                                      fill=0.0, base=0, channel_multiplier=-1), [m64_tr])
    P(lambda: nc.gpsimd.affine_select(out=mask32[:, :], in_=mask32[:, :], pattern=[[1, 64]], compare_op=ALU.is_ge,
                                      fill=0.0, base=0, channel_multiplier=-1), [m32_tr])
    P(lambda: nc.gpsimd.memset(mask64[0:64, 64:128], 0.0), [m64_tr])
    P(lambda: nc.gpsimd.memset(mask32[0:32, 32:64], 0.0), [m32_tr])
    P(lambda: nc.gpsimd.memset(uprev[:, :, :], 0.0), [uprev_tr])
    P(lambda: nc.gpsimd.memset(stat[:, :], 1.0), stat_tr)
    P(lambda: nc.gpsimd.memset(Sst[:, :, :], 0.0), S_tr)
    P(lambda: nc.gpsimd.memset(zeros[:, :], 0.0), [mtr])
    P(lambda: nc.gpsimd.memset(neghalf[:, :], -0.5), [mtr])
    P(lambda: nc.gpsimd.memset(ccs_zero[:, :], 0.0), [mtr])
    b.op(POOL, lambda: nc.gpsimd.memset(ones[:, :], 1.0), reads=[id_tr, m64_tr, m32_tr], writes=[mtr])
    lb_tr = Tr("lb")
    b.op(DVE, lambda: nc.vector.tensor_tensor(out=lbv[:, :], in0=cvec[:, 32:36], in1=cvec[:, 36:40], op=ALU.subtract),
         reads=[const_tr], writes=[lb_tr])
    b.op(ACT, lambda: nc.scalar.activation(out=lbv[:, :], in_=lbv[:, :], func=AF.Sigmoid), writes=[lb_tr])
    b.op(DVE, lambda: nc.vector.tensor_scalar(out=omlb[:, :], in0=lbv[:, :], scalar1=-1.0, scalar2=1.0,
                                              op0=ALU.mult, op1=ALU.add), reads=[lb_tr], writes=[const_tr])
    G_MIX, G_X, G_FFN, G_MEM = 0, 8, 16, 24

    def norm_stats(xr, xr_tr, subt, sel=None):
        nsub = len(subt)
        nmax = max(n for (_, n) in subt)
        if sel is None:
            sel = list(range(nsub))
        g = cnt["st"] % 2
        cnt["st"] += 1
        ssv = stat[:, g * 8:g * 8 + 4]
        rsv = stat[:, g * 8 + 4:g * 8 + 8]
        stt = stat_tr[g]
        for s in sel:
            c0, n = subt[s]
            b.op(ACT, lambda: nc.scalar.activation(out=junk[:n, :], in_=xr[:n, s, :], func=AF.Square, accum_out=ssv[:n, s:s + 1]),
                 reads=[xr_tr[s]], writes=[junk_tr, stt])
        b.op(ACT, lambda: nc.scalar.activation(out=rsv[:nmax, 0:nsub], in_=ssv[:nmax, 0:nsub], func=AF.Ln, scale=1.0 / D, bias=EPS), writes=[stt])
        b.op(ACT, lambda: nc.scalar.activation(out=rsv[:nmax, 0:nsub], in_=rsv[:nmax, 0:nsub], func=AF.Exp, scale=-0.5), writes=[stt])
        return rsv, stt

    def rmsnorm_to_hT(xr, xr_tr, subt, goff, sel=None):
        rsv, stt = norm_stats(xr, xr_tr, subt, sel)
        for s in (sel if sel is not None else range(len(subt))):
            c0, n = subt[s]
            xi = cnt["xn"] % 2
            cnt["xn"] += 1
            b.op(ACT, lambda: nc.scalar.activation(out=xn[xi][:n, :], in_=xr[:n, s, :], func=AF.Copy, scale=rsv[:n, s:s + 1]),
                 reads=[xr_tr[s], stt], writes=[xn_tr[xi]])
            pb, pbt = next_psB()

            def tp():
                ins = None
                for k in range(8):
                    ins = nc.tensor.transpose(pb[:, k * 128:k * 128 + n], xn[xi][:n, k * 128:(k + 1) * 128], ident[:n, :n])
                return ins
            b.op(PE, tp, reads=[xn_tr[xi], mtr], writes=[pbt])
            pv = pb[:, :].rearrange("p (k t) -> p k t", t=128)[:, :, 0:n]
            gb = cvec[:, goff:goff + 8].unsqueeze(2).to_broadcast([128, 8, n])
            b.op(DVE, lambda: nc.vector.tensor_tensor(out=hT[:, :, c0:c0 + n], in0=pv, in1=gb, op=ALU.mult),
                 reads=[pbt, const_tr], writes=[hT_tr[s]])

    def fm_proj(wb, wt, ch4, ncols, rhs_of_k, nk, rd, koff=0):
        ps, pt = next_psF()
        mm(ps[:, 0:ncols], [(wb[:, koff + k, ch4 * 128:(ch4 + 1) * 128], rhs_of_k(k)) for k in range(nk)],
           reads=[wt] + rd, ptr=pt)
        return ps, pt

    def fm_block_split(wb, wt):
        banks = [next_psF() for _ in range(4)]
        for (lo, hi, trs) in [(0, 384, hT_tr[0:3]), (384, 512, hT_tr[3:4])]:
            for ch4 in range(4):
                ps, pt = banks[ch4]

                def fn():
                    ins = None
                    for k in range(8):
                        ins = nc.tensor.matmul(ps[:, lo:hi], lhsT=wb[:, k, ch4 * 128:(ch4 + 1) * 128], rhs=hT[:, k, lo:hi],
                                               start=(lo == 0 and k == 0), stop=(hi == 512 and k == 7), skip_group_check=True)
                    return ins
                b.op(PE, fn, reads=[wt] + trs, writes=[pt])
        return banks

    def tok_proj(wb, wt, lhs_of_k, nk, n, rd, ps=None, pt=None, start=True, stop=True):
        if ps is None:
            ps, pt = next_psF()
        mm(ps[:n, :], [(lhs_of_k(k), wb[:, k, :]) for k in range(nk)], reads=[wt] + rd, ptr=pt, start=start, stop=stop)
        return ps, pt

    def load_x(src_rows, xr, xr_tr, subt):
        for s, (c0, n) in enumerate(subt):
            b.dma(SP, xr[:n, s, :], src_rows[c0:c0 + n, :], ld_slot(), writes=[xr_tr[s]])

    SUB4 = [(i * 128, 128) for i in range(4)]
    tiles = [("pre", p) for p in range(n_pre)] + [("main", t) for t in range(n_main)] + ([("sample", 0)] if sample else [])
    tstate = {"loaded": 0}

    def ensure_loaded(i):
        while tstate["loaded"] <= i and tstate["loaded"] < len(tiles):
            j = tstate["loaded"]
            kind, idx = tiles[j]
            if kind == "pre":
                src, st = xp[idx * TM:(idx + 1) * TM, :], SUB4
            elif kind == "main":
                src, st = xm[idx * TM:(idx + 1) * TM, :], SUB4
            else:
                src, st = xs, [(0, 64)]
            load_x(src, xres[j % NXB], xres_tr[j % NXB], st)
            tstate["loaded"] += 1
        return xres[i % NXB], xres_tr[i % NXB]

    def hgrn_gates(T, L, last, ar, hTr, tmp, tmp_tr, part=0):
        fb, eb, kt, kpb = ar["fb"], ar["eb"], ar["kt"], ar["kp"]
        nch = T // L
        if part in (0, 1):
            wb, wt = w_get("hf")
        for h in (range(4) if part in (0, 1) else []):
            ps, pt = fm_proj(wb, wt, h, T, lambda k: hT[:, k, 0:T], 8, hTr)
            b.op(ACT, lambda: nc.scalar.activation(out=fb.t[:, h, :], in_=ps[:, 0:T], func=AF.Sigmoid),
                 reads=[pt], writes=[fb.tr[h]])
            b.op(ACT, lambda: nc.scalar.activation(out=fb.t[:, h, :], in_=fb.t[:, h, :], func=AF.Identity,
                                                   scale=omlb[:, h:h + 1], bias=lbv[:, h:h + 1]),
                 reads=[const_tr, lb_tr], writes=[fb.tr[h]])
        for h in (range(4) if part in (0, 1) else []):
            b.op(ACT, lambda: nc.scalar.activation(out=tmp[h][:, 0:T], in_=fb.t[:, h, :], func=AF.Ln), reads=[fb.tr[h]], writes=[tmp_tr[h]])
        for h in (range(4) if part in (0, 1) else []):
            def scans():
                ins = None
                for c in range(nch):
                    ins = nc.vector.tensor_tensor_scan(out=eb.t[:, h, c * L:(c + 1) * L], data0=tmp[h][:, c * L:(c + 1) * L],
                                                       data1=zeros[:, 0:L], initial=0.0, op0=ALU.add, op1=ALU.add)
                return ins
            b.op(DVE, scans, reads=[tmp_tr[h], mtr], writes=[eb.tr[h]])
        if part == 1:
            return
        for h in range(4):
            b.op(ACT, lambda: nc.scalar.activation(out=tmp[h][:, 0:T], in_=eb.t[:, h, :], func=AF.Exp, scale=-1.0), reads=[eb.tr[h]], writes=[tmp_tr[h]])
            b.op(ACT, lambda: nc.scalar.activation(out=eb.t[:, h, :], in_=eb.t[:, h, :], func=AF.Exp), writes=[eb.tr[h]])
        for h in range(4):
            b.op(POOL, lambda: nc.gpsimd.tensor_scalar(out=fb.t[:, h, :], in0=fb.t[:, h, :], scalar1=-1.0, scalar2=1.0,
                                                       op0=ALU.mult, op1=ALU.add), writes=[fb.tr[h]])
            b.op(POOL, lambda: nc.gpsimd.tensor_tensor(out=kt.t[:, h, :], in0=fb.t[:, h, :], in1=tmp[h][:, 0:T], op=ALU.mult),
                 reads=[fb.tr[h], tmp_tr[h]], writes=[kt.tr[h]])
            ktv = kt.t[:, h, :].rearrange("p (c l) -> p c l", l=L)
            kpv = kpb.t[:, h, :].rearrange("p (c l) -> p c l", l=L)
            elb = eb.t[:, h, :].rearrange("p (c l) -> p c l", l=L)[:, :, last:last + 1].to_broadcast([128, nch, L])
            b.op(POOL, lambda: nc.gpsimd.tensor_tensor(out=kpv, in0=ktv, in1=elb, op=ALU.mult),
                 reads=[kt.tr[h], eb.tr[h]], writes=[kpb.tr[h]])

    def v_proj(subt, ar, hTr):
        wb, wt = w_get("hi")
        v = ar["v"]
        for s, (c0, n) in enumerate(subt):
            ps, pt = tok_proj(wb, wt, lambda k: hT[:, k, c0:c0 + n], 8, n, hTr)
            b.op(ACT, lambda: nc.scalar.copy(out=v.t[:n, s, :], in_=ps[:n, :]), reads=[pt], writes=[v.tr[s]])

    def kp_transpose(subt, ar):
        kpb, kptok = ar["kp"], ar["kptok"]
        for s, (c0, n) in enumerate(subt):
            pb, pbt = next_psB()

            def tp():
                ins = None
                for h in range(4):
                    ins = nc.tensor.transpose(pb[:n, h * 128:(h + 1) * 128], kpb.t[:, h, c0:c0 + n], ident[:, :])
                return ins
            b.op(PE, tp, reads=kpb.tr + [mtr], writes=[pbt])
            b.op(ACT, lambda: nc.scalar.copy(out=kptok.t[:n, s, :], in_=pb[:n, 0:512]), reads=[pbt], writes=[kptok.tr[s]])

    def state_update(s, h, c, L, last, c0, ar, S_ap, S_t, e_ap):
        kptok, v, eb = ar["kptok"], ar["v"], ar["eb"]
        pP, pPt = next_psF()
        r0 = c * L
        mm(pP[:, 0:128], [(kptok.t[r0:r0 + L, s, h * 128:(h + 1) * 128], v.t[r0:r0 + L, s, h * 128:(h + 1) * 128])],
           reads=[kptok.tr[s], v.tr[s]], ptr=pPt)
        b.op(DVE, lambda: nc.vector.scalar_tensor_tensor(out=S_ap, in0=S_ap, scalar=e_ap, in1=pP[:, 0:128],
                                                         op0=ALU.mult, op1=ALU.add),
             reads=[pPt, eb.tr[h]], writes=[S_t])

    def mem_kv():
        subt = [(0, 128), (128, 128)]
        xr, xrt = xres[0], xres_tr[0]
        load_x(mem, xr, xrt, subt)
        rmsnorm_to_hT(xr, xrt, subt, G_MEM)
        hTr = hT_tr[0:2]
        for c in range(2):
            wb, wt = w_get("xk%d" % c)
            for ch4 in range(4):
                ps, pt = fm_proj(wb, wt, ch4, 256, lambda k: hT[:, k, 0:256], 8, hTr)
                b.op(ACT, lambda: nc.scalar.copy(out=mkT[:, c * 4 + ch4, :], in_=ps[:, 0:256]), reads=[pt], writes=[mkT_tr])
            for s, (c0, n) in enumerate(subt):
                ps, pt = tok_proj(wb, wt, lambda k: hT[:, k, c0:c0 + n], 8, n, hTr)
                yi = cnt["yo"] % 2
                cnt["yo"] += 1
                b.op(ACT, lambda: nc.scalar.copy(out=yout[yi][:, 0:512], in_=ps[:, :]), reads=[pt], writes=[yout_tr[yi]])
                b.dma(POOL, mk_o[c0:c0 + n, c * 512:(c + 1) * 512], yout[yi][:, 0:512], st_slot(), reads=[yout_tr[yi]])
        for c in range(2):
            wb, wt = w_get("xv%d" % c)
            for s, (c0, n) in enumerate(subt):
                ps, pt = tok_proj(wb, wt, lambda k: hT[:, k, c0:c0 + n], 8, n, hTr)
                yi = cnt["yo"] % 2
                cnt["yo"] += 1
                b.op(ACT, lambda: nc.scalar.copy(out=yout[yi][:, 0:512], in_=ps[:, :]), reads=[pt], writes=[yout_tr[yi]])
                b.op(DVE, lambda: nc.vector.tensor_copy(out=mv[:, s, c * 512:(c + 1) * 512], in_=ps[:, :]), reads=[pt], writes=[mv_tr])
                b.dma(POOL, mv_o[c0:c0 + n, c * 512:(c + 1) * 512], yout[yi][:, 0:512], st_slot(), reads=[yout_tr[yi]])

    pre_sets = []

    def pre_alloc(ph):
        A = lambda n: b.newtr(n, arena=True)
        for i in range(2):
            ar = {}
            for nme, dt, W in [("fb", F32, TM), ("cf", F32, TM + 1), ("kp", BF16, TM)]:
                f = FM.__new__(FM)
                f.t = ph.enter_context(nc.sbuf_tensor(b.nm(nme), [128, 4, W], dt))
                f.tr = [A(nme) for _ in range(4)]
                ar[nme] = f
            for nme in ["v", "kptok"]:
                f = FM.__new__(FM)
                f.t = ph.enter_context(nc.sbuf_tensor(b.nm(nme), [128, 4, 512], BF16))
                f.tr = [A(nme) for _ in range(4)]
                ar[nme] = f
            pre_sets.append(ar)
        ccs = ph.enter_context(nc.sbuf_tensor(b.nm("ccs"), [128, 4, TM], F32))
        pre_sets.append((ccs, [A("ccs") for _ in range(4)]))

    def pre_tile(p, lastp):
        T = TM
        subt = [(i * 128, 128) for i in range(4)]
        xr, xrt = ensure_loaded(p)
        rmsnorm_to_hT(xr, xrt, subt, G_MIX)
        ensure_loaded(p + 1)
        ar = pre_sets[p % 2]
        fb, cf, kpb, kptok, v = ar["fb"], ar["cf"], ar["kp"], ar["kptok"], ar["v"]
        wb, wt = w_get("hf")
        for h in range(4):
            ps, pt = fm_proj(wb, wt, h, T, lambda k: hT[:, k, 0:T], 8, hT_tr)
            b.op(ACT, lambda: nc.scalar.activation(out=fb.t[:, h, :], in_=ps[:, 0:T], func=AF.Sigmoid), reads=[pt], writes=[fb.tr[h]])
            b.op(DVE, lambda: nc.vector.tensor_scalar(out=fb.t[:, h, :], in0=fb.t[:, h, :], scalar1=omlb[:, h:h + 1],
                                                      scalar2=lbv[:, h:h + 1], op0=ALU.mult, op1=ALU.add),
                 reads=[const_tr, lb_tr], writes=[fb.tr[h]])
            b.op(POOL, lambda: nc.gpsimd.memset(cf.t[:, h, T:T + 1], 1.0), writes=[cf.tr[h]])
            b.op(DVE, lambda: nc.vector.tensor_tensor_scan(out=cf.t[:, h, T - 1::-1], data0=fb.t[:, h, T - 1::-1], data1=ccs_zero[:, 0:T],
                                                           initial=1.0, op0=ALU.mult, op1=ALU.add),
                 reads=[fb.tr[h], mtr], writes=[cf.tr[h]])
            b.op(POOL, lambda: nc.gpsimd.tensor_scalar(out=fb.t[:, h, :], in0=fb.t[:, h, :], scalar1=-1.0, scalar2=1.0,
                                                       op0=ALU.mult, op1=ALU.add), writes=[fb.tr[h]])
            b.op(POOL, lambda: nc.gpsimd.tensor_tensor(out=kpb.t[:, h, :], in0=fb.t[:, h, :], in1=cf.t[:, h, 1:T + 1], op=ALU.mult),
                 reads=[fb.tr[h], cf.tr[h]], writes=[kpb.tr[h]])
        v_proj(subt, ar, hT_tr)
        kp_transpose(subt, ar)
        pP, pPt = next_psF()

        def pm():
            ins = None
            for h in range(4):
                for s in range(4):
                    ins = nc.tensor.matmul(pP[:, h * 128:(h + 1) * 128], lhsT=kptok.t[:, s, h * 128:(h + 1) * 128],
                                           rhs=v.t[:, s, h * 128:(h + 1) * 128], start=(h == 0 and s == 0), stop=(s == 3), skip_group_check=True)
            return ins
        b.op(PE, pm, reads=kptok.tr + v.tr, writes=[pPt])
        for h in range(4):
            b.op(DVE, lambda: nc.vector.scalar_tensor_tensor(out=Sst[:, h, :], in0=Sst[:, h, :], scalar=cf.t[:, h, 0:1], in1=pP[:, h * 128:(h + 1) * 128],
                                                             op0=ALU.mult, op1=ALU.add),
                 reads=[pPt, cf.tr[h]], writes=[S_tr[h]])
        if lastp:
            ccs, ccs_tr = pre_sets[2]
            wb, wt = w_get("cc")
            for ch in range(4):
                ps, pt = fm_proj(wb, wt, ch, T, lambda k: hT[:, k, 0:T], 8, hT_tr)
                b.op(ACT, lambda: nc.scalar.copy(out=ccs[:, ch, :], in_=ps[:, 0:T]), reads=[pt], writes=[ccs_tr[ch]])
            wb, wt = w_get("cx")
            for ch in range(4):
                ps, pt = fm_proj(wb, wt, ch, T, lambda k: hT[:, k, 0:T], 8, hT_tr)
                b.op(DVE, lambda: nc.vector.tensor_tensor(out=uprev[:, ch, :], in0=ps[:, T - 2:T], in1=ccs[:, ch, T - 2:T], op=ALU.mult),
                     reads=[pt, ccs_tr[ch]], writes=[uprev_tr])

    def full_tile(kind, t, prenormed=False):
        is_s = kind == "sample"
        if is_s:
            T, L, last = 64, 32, 15
            subt = [(0, 64)]
            segs = [(0, 32), (32, 32)]
        else:
            T, L, last = TM, 64, 63
            subt = [(i * 128, 128) for i in range(4)]
            segs = [(0, TM)]
        nsub = len(subt)
        nseg = len(segs)
        ti = n_pre + (n_main if is_s else t)
        xr, xrt = ensure_loaded(ti)
        hTr = hT_tr[0:nsub]
        if not prenormed:
            rmsnorm_to_hT(xr, xrt, subt, G_MIX)
        ensure_loaded(ti + 1)
        did_prenorm = False
        A = lambda n: b.newtr(n, arena=True)
        hcol = lambda k: hT[:, k, 0:T]

        with ExitStack() as ph:
            def fm(nme, nch, dt, W=T):
                f = FM.__new__(FM)
                f.t = ph.enter_context(nc.sbuf_tensor(b.nm(nme), [128, nch, W], dt))
                f.tr = [A(nme) for _ in range(nch)]
                return f
            UW = T + 2 * nseg
            ccs = fm("ccs", 4, F32)
            u = fm("u", 4, F32, UW)
            z = fm("z", 4, BF16)
            ar = {"fb": fm("fb", 4, F32), "eb": fm("eb", 4, F32),
                  "kt": fm("kt", 4, BF16), "kp": fm("kp", 4, BF16)}
            qt = fm("qt", 4, BF16)
            sgt = fm("sgt", 4, BF16)
            oN = fm("oN", 4, BF16)
            sga = fm("sga", 8, BF16)
            sgb = fm("sgb", 8, BF16)
            mrgb = fm("mrgb", 8, BF16)
            for nme in ["v", "kptok"]:
                f = FM.__new__(FM)
                f.t = ph.enter_context(nc.sbuf_tensor(b.nm(nme), [128, nsub, 512], BF16))
                f.tr = [A(nme) for _ in range(nsub)]
                ar[nme] = f
            tmpA = [ph.enter_context(nc.sbuf_tensor(b.nm("tmpA"), [128, 512], F32)) for _ in range(4)]
            tmpA_tr = [A("tmpA") for _ in range(4)]
            tmpB = [ph.enter_context(nc.sbuf_tensor(b.nm("tmpB"), [128, 512], F32)) for _ in range(2)]
            tmpB_tr = [A("tmpB"), A("tmpB")]
            scb = [ph.enter_context(nc.sbuf_tensor(b.nm("scb"), [128, 512], BF16)) for _ in range(2)]
            scb_tr = [A("scb"), A("scb")]
            sqb = ph.enter_context(nc.sbuf_tensor(b.nm("sqb"), [128, 512], BF16))
            sqb_tr = A("sqb")
            tcnt = {"a": 0, "b": 0, "sc": 0, "cv": 0}
            ctmp, ctmp_tr = tmpB, tmpB_tr
            if is_s:
                Ss = ph.enter_context(nc.sbuf_tensor(b.nm("Ss"), [128, 2, 4, 128], F32))
                Ssb = ph.enter_context(nc.sbuf_tensor(b.nm("Ssb"), [128, 2, 4, 128], BF16))
                Ss_tr = [[A("Ss") for _ in range(4)] for _ in range(2)]
                Ssb_tr = [[A("Ssb") for _ in range(4)] for _ in range(2)]
                for j in range(2):
                    b.dma(SP, Ss[:, j, :, :], shg[j].rearrange("h k v -> k h v"), ld_slot(), writes=Ss_tr[j])
                    b.op(ACT, lambda: nc.scalar.copy(out=Ssb[:, j, :, :], in_=Ss[:, j, :, :]), reads=Ss_tr[j], writes=Ssb_tr[j])

            hgrn_gates(T, L, last, ar, hTr, tmpA, tmpA_tr, part=1)
            wb, wt = w_get("cc")
            for ch in range(4):
                ps, pt = fm_proj(wb, wt, ch, T, hcol, 8, hTr)
                b.op(ACT, lambda: nc.scalar.copy(out=ccs.t[:, ch, :], in_=ps[:, 0:T]), reads=[pt], writes=[ccs.tr[ch]])
            hgrn_gates(T, L, last, ar, hTr, tmpA, tmpA_tr, part=2)
            wb, wt = w_get("cx")
            if is_s:
                for g in range(2):
                    o = g * (32 + 2)
                    b.dma(SP, u.t[:, :, o:o + 2], sconv[:, :, g, :], ld_slot(), writes=u.tr)
            else:
                b.op(POOL, lambda: nc.gpsimd.tensor_copy(out=u.t[:, :, 0:2], in_=uprev[:, :, :]), reads=[uprev_tr], writes=u.tr)
            for ch in range(4):
                ps, pt = fm_proj(wb, wt, ch, T, hcol, 8, hTr)
                for g, (g0, gl) in enumerate(segs):
                    o = g0 + 2 * g + 2
                    b.op(DVE, lambda: nc.vector.tensor_tensor(out=u.t[:, ch, o:o + gl], in0=ps[:, g0:g0 + gl], in1=ccs.t[:, ch, g0:g0 + gl], op=ALU.mult),
                         reads=[pt, ccs.tr[ch]], writes=[u.tr[ch]])
            if is_s:
                for g in range(2):
                    o = g * 34 + 16
                    b.dma(POOL, convs_o[:, :, g, :], u.t[:, :, o:o + 2], st_slot(), reads=u.tr)
            else:
                b.op(POOL, lambda: nc.gpsimd.tensor_copy(out=uprev[:, :, :], in_=u.t[:, :, T:T + 2]), reads=u.tr, writes=[uprev_tr])
            for ch in range(4):
                for g, (g0, gl) in enumerate(segs):
                    o = g0 + 2 * g
                    yv = ccs.t[:, ch, g0:g0 + gl]
                    cw = lambda j: cvec[:, 40 + ch * 3 + j:40 + ch * 3 + j + 1]
                    b.op(DVE, lambda: nc.vector.tensor_scalar(out=yv, in0=u.t[:, ch, o:o + gl], scalar1=cw(0), scalar2=None, op0=ALU.mult),
                         reads=[u.tr[ch], const_tr], writes=[ccs.tr[ch]])
                    b.op(DVE, lambda: nc.vector.scalar_tensor_tensor(out=yv, in0=u.t[:, ch, o + 1:o + 1 + gl], scalar=cw(1), in1=yv, op0=ALU.mult, op1=ALU.add),
                         reads=[u.tr[ch]], writes=[ccs.tr[ch]])
                    b.op(DVE, lambda: nc.vector.scalar_tensor_tensor(out=yv, in0=u.t[:, ch, o + 2:o + 2 + gl], scalar=cw(2), in1=yv, op0=ALU.mult, op1=ALU.add),
                         reads=[u.tr[ch]], writes=[ccs.tr[ch]])
            wb, wt = w_get("cb")
            for ch in range(4):
                ps, pt = fm_proj(wb, wt, ch, T, hcol, 8, hTr)
                b.op(DVE, lambda: nc.vector.tensor_tensor(out=z.t[:, ch, :], in0=ps[:, 0:T], in1=ccs.t[:, ch, :], op=ALU.mult),
                     reads=[pt, ccs.tr[ch]], writes=[z.tr[ch]])

            eb = ar["eb"]
            wb, wt = w_get("hq")
            for h in range(4):
                ps, pt = fm_proj(wb, wt, h, T, hcol, 8, hTr)
                ai = tcnt["a"] % 4
                tcnt["a"] += 1
                b.op(ACT, lambda: nc.scalar.activation(out=tmpA[ai][:, 0:T], in_=ps[:, 0:T], func=AF.Silu), reads=[pt], writes=[tmpA_tr[ai]])
                b.op(POOL, lambda: nc.gpsimd.tensor_tensor(out=qt.t[:, h, :], in0=tmpA[ai][:, 0:T], in1=eb.t[:, h, :], op=ALU.mult),
                     reads=[tmpA_tr[ai], eb.tr[h]], writes=[qt.tr[h]])
            v_proj(subt, ar, hTr)
            kp_transpose(subt, ar)

            kt, v, kptok = ar["kt"], ar["v"], ar["kptok"]
            mask = mask32 if is_s else mask64
            gate_units = [("hg", sgt, h, h, AF.Silu) for h in range(4)]
            gate_units += [(nme, dst, ch4, int(nme[2]) * 4 + ch4, AF.Sigmoid)
                           for nme, dst in [("ga0", sga), ("ga1", sga), ("gb0", sgb), ("gb1", sgb)] for ch4 in range(4)]
            gstate = {"i": 0, "wb": None}

            def emit_gates(kmax):
                for _ in range(kmax):
                    if gstate["i"] >= len(gate_units):
                        return
                    nme, dst, ch4, dch, gfunc = gate_units[gstate["i"]]
                    gstate["i"] += 1
                    if ch4 == 0:
                        gstate["wb"] = w_get(nme)
                    wb, wt = gstate["wb"]
                    ps, pt = fm_proj(wb, wt, ch4, T, hcol, 8, hTr)
                    b.op(ACT, lambda: nc.scalar.activation(out=dst.t[:, dch, :], in_=ps[:, 0:T], func=gfunc), reads=[pt], writes=[dst.tr[dch]])

            def emit_scores(s):
                c0, n = subt[s]
                W4 = 4 * n
                pS, pSt = next_psF()

                def sc():
                    ins = None
                    for h in range(4):
                        ins = nc.tensor.matmul(pS[:n, h * n:(h + 1) * n], lhsT=kt.t[:, h, c0:c0 + n], rhs=qt.t[:, h, c0:c0 + n], start=(h == 0), stop=True, skip_group_check=True)
                    return ins
                b.op(PE, sc, reads=kt.tr + qt.tr, writes=[pSt])
                si = tcnt["sc"] % 2
                tcnt["sc"] += 1
                mb = mask[:n, :n].unsqueeze(1).to_broadcast([n, 4, n])
                b.op(DVE, lambda: nc.vector.tensor_tensor(out=scb[si][:n, 0:W4].rearrange("p (h t) -> p h t", h=4),
                                                          in0=pS[:n, 0:W4].rearrange("p (h t) -> p h t", h=4), in1=mb, op=ALU.mult),
                     reads=[pSt, mtr], writes=[scb_tr[si]])
                return si

            cnt["in_rec"] = 1
            if is_s:
                emit_gates(4)
            pend_si = emit_scores(0)
            for s, (c0, n) in enumerate(subt):
                nchk = n // L
                W4 = 4 * n
                pO, pOt = next_psO()
                si = pend_si

                def intra():
                    ins = None
                    for h in range(4):
                        ins = nc.tensor.matmul(pO[:, h * n:(h + 1) * n], lhsT=v.t[:n, s, h * 128:(h + 1) * 128], rhs=scb[si][:n, h * n:(h + 1) * n],
                                               start=(h == 0), stop=False, skip_group_check=True)
                    return ins
                b.op(PE, intra, reads=[v.tr[s], scb_tr[si]], writes=[pOt])
                for c in range(nchk):
                    r0 = c * L

                    def sterm():
                        ins = None
                        for h in range(4):
                            Sb_ap = Ssb[:, c, h, :] if is_s else Sb[:, h, :]
                            ins = nc.tensor.matmul(pO[:, h * n + r0:h * n + r0 + L], lhsT=Sb_ap, rhs=qt.t[:, h, c0 + r0:c0 + r0 + L],
                                                   start=False, stop=(c == nchk - 1), skip_group_check=True)
                        return ins
                    b.op(PE, sterm, reads=(Ssb_tr[c] if is_s else Sb_tr) + qt.tr, writes=[pOt])
                    pP, pPt = next_psF()

                    def pm():
                        ins = None
                        for h in range(4):
                            ins = nc.tensor.matmul(pP[:, h * 128:(h + 1) * 128], lhsT=kptok.t[r0:r0 + L, s, h * 128:(h + 1) * 128],
                                                   rhs=v.t[r0:r0 + L, s, h * 128:(h + 1) * 128], start=(h == 0), stop=True, skip_group_check=True)
                        return ins
                    b.op(PE, pm, reads=[kptok.tr[s], v.tr[s]], writes=[pPt])
                    for h in range(4):
                        if is_s:
                            S_ap, S_t = Ss[:, c, h, :], Ss_tr[c][h]
                        else:
                            S_ap, S_t = Sst[:, h, :], S_tr[h]
                        e_ap = eb.t[:, h, c0 + r0 + last:c0 + r0 + last + 1]
                        b.op(DVE, lambda: nc.vector.scalar_tensor_tensor(out=S_ap, in0=S_ap, scalar=e_ap, in1=pP[:, h * 128:(h + 1) * 128],
                                                                         op0=ALU.mult, op1=ALU.add),
                             reads=[pPt, eb.tr[h]], writes=[S_t])
                    if not is_s:
                        b.op(ACT, lambda: nc.scalar.copy(out=Sb[:, :, :], in_=Sst[:, :, :]), reads=S_tr, writes=Sb_tr)
                    emit_gates(2)
                if s + 1 < nsub:
                    pend_si = emit_scores(s + 1)
                b.op(ACT, lambda: nc.scalar.activation(out=sqb[:, 0:W4], in_=pO[:, 0:W4], func=AF.Square), reads=[pOt], writes=[sqb_tr])
                pN, pNt = next_psF()
                mm(pN[:, 0:W4], [(ones[:, :], sqb[:, 0:W4])], reads=[sqb_tr, mtr], ptr=pNt)
                ai = tcnt["a"] % 4
                tcnt["a"] += 1
                ta = tmpA[ai]
                b.op(ACT, lambda: nc.scalar.activation(out=ta[:, 0:W4], in_=pN[:, 0:W4], func=AF.Ln, scale=1.0 / 128, bias=EPS),
                     reads=[pNt], writes=[tmpA_tr[ai]])
                b.op(ACT, lambda: nc.scalar.activation(out=ta[:, 0:W4], in_=ta[:, 0:W4], func=AF.Exp, scale=-0.5), writes=[tmpA_tr[ai]])
                b.op(DVE, lambda: nc.vector.tensor_tensor(out=ta[:, 0:W4], in0=pO[:, 0:W4], in1=ta[:, 0:W4], op=ALU.mult),
                     reads=[pOt], writes=[tmpA_tr[ai]])
                tav = ta[:, 0:W4].rearrange("p (h t) -> p h t", h=4)
                b.op(DVE, lambda: nc.vector.scalar_tensor_tensor(out=oN.t[:, :, c0:c0 + n], in0=tav, scalar=cvec[:, 52:53],
                                                                 in1=sgt.t[:, :, c0:c0 + n], op0=ALU.mult, op1=ALU.mult),
                     reads=[tmpA_tr[ai], const_tr] + sgt.tr, writes=oN.tr)
            if is_s:
                for j in range(2):
                    b.dma(POOL, hgs_o[j].rearrange("h k v -> k h v"), Ss[:, j, :, :], st_slot(), reads=Ss_tr[j])
            cnt["in_rec"] = 0
            emit_gates(100)
            for c in range(2):
                wb, wt = w_get("cvhg%d" % c)
                for ch4 in range(4):
                    dch = c * 4 + ch4
                    pa, pat = fm_proj(wb, wt, ch4, T, lambda k: z.t[:, k, :], 4, z.tr, koff=0)
                    pbb, pbt = fm_proj(wb, wt, ch4, T, lambda k: oN.t[:, k, :], 4, oN.tr, koff=4)
                    ai = tcnt["a"] % 4
                    tcnt["a"] += 1
                    bi = tcnt["b"] % 2
                    tcnt["b"] += 1
                    b.op(DVE, lambda: nc.vector.tensor_tensor(out=tmpA[ai][:, 0:T], in0=pa[:, 0:T], in1=sga.t[:, dch, :], op=ALU.mult),
                         reads=[pat, sga.tr[dch]], writes=[tmpA_tr[ai]])
                    b.op(DVE, lambda: nc.vector.tensor_tensor(out=tmpB[bi][:, 0:T], in0=pbb[:, 0:T], in1=sgb.t[:, dch, :], op=ALU.mult),
                         reads=[pbt, sgb.tr[dch]], writes=[tmpB_tr[bi]])
                    b.op(POOL, lambda: nc.gpsimd.tensor_tensor(out=mrgb.t[:, dch, :], in0=tmpA[ai][:, 0:T], in1=tmpB[bi][:, 0:T], op=ALU.add),
                         reads=[tmpA_tr[ai], tmpB_tr[bi]], writes=[mrgb.tr[dch]])
            wpair = [w_get("wo0"), w_get("wo1", ahead=False)]
            for s, (c0, n) in enumerate(subt):
                for c in range(2):
                    wb, wt = wpair[c]
                    ps, pt = tok_proj(wb, wt, lambda k: mrgb.t[:, k, c0:c0 + n], 8, n, mrgb.tr)
                    xsl = xr[:n, s, c * 512:(c + 1) * 512]
                    b.op(DVE, lambda: nc.vector.tensor_tensor(out=xsl, in0=ps[:n, :], in1=xsl, op=ALU.add), reads=[pt], writes=[xrt[s]])
            b.end_phase()

        if nsub == 4:
            rmsnorm_to_hT(xr, xrt, subt, G_X, sel=[0, 1, 2])
            rmsnorm_to_hT(xr, xrt, subt, G_X, sel=[3])
        else:
            rmsnorm_to_hT(xr, xrt, subt, G_X)
        with ExitStack() as ph:
            def fm(nme, nch, dt, W=T):
                f = FM.__new__(FM)
                f.t = ph.enter_context(nc.sbuf_tensor(b.nm(nme), [128, nch, W], dt))
                f.tr = [A(nme) for _ in range(nch)]
                return f
            qT = fm("qT", 8, BF16)
            eT = fm("eT", 8, BF16)
            oxT = fm("oxT", 8, BF16)
            rden = fm("rden", 4, F32)
            if is_s:
                kst = ph.enter_context(nc.sbuf_tensor(b.nm("kst"), [128, 2, D], BF16))
                kst_tr = A("kst")
                mkTs = [fm("mkTs", 8, BF16, 256) for _ in range(2)]
                mvs = [fm("mvs", 2, BF16, D) for _ in range(2)]
                for j in range(2):
                    b.dma(POOL, kst[:, :, :], ck[j].rearrange("(c p) d -> p c d", p=128), pl_slot(), writes=[kst_tr])
                    for mc in range(2):
                        pb, pbt = next_psB()

                        def tp():
                            ins = None
                            for dc in range(8):
                                ins = nc.tensor.transpose(pb[:, dc * 128:(dc + 1) * 128], kst[:, mc, dc * 128:(dc + 1) * 128], ident[:, :])
                            return ins
                        b.op(PE, tp, reads=[kst_tr, mtr], writes=[pbt])
                        b.op(ACT, lambda: nc.scalar.copy(out=mkTs[j].t[:, :, mc * 128:(mc + 1) * 128],
                                                         in_=pb[:, :].rearrange("p (k t) -> p k t", t=128)), reads=[pbt], writes=mkTs[j].tr)
                    b.dma(POOL, mvs[j].t[:, :, :], cv[j].rearrange("(c p) d -> p c d", p=128), pl_slot(), writes=mvs[j].tr)
                agroups = [((0, 32), mkTs[0].t, mkTs[0].tr, mvs[0].t, mvs[0].tr), ((32, 32), mkTs[1].t, mkTs[1].tr, mvs[1].t, mvs[1].tr)]
            else:
                agroups = [((0, T), mkT, [mkT_tr], mv, [mv_tr])]
            for c in range(2):
                wb, wt = w_get("xq%d" % c)
                banks = fm_block_split(wb, wt) if (c == 0 and nsub == 4) else None
                for ch4 in range(4):
                    ps, pt = banks[ch4] if banks else fm_proj(wb, wt, ch4, T, hcol, 8, hTr)
                    dch = c * 4 + ch4
                    b.op(ACT, lambda: nc.scalar.activation(out=qT.t[:, dch, :], in_=ps[:, 0:T], func=AF.Copy, scale=1.0 / 16.0), reads=[pt], writes=[qT.tr[dch]])
            for ((g0, gl), mk_t, mk_trs, mv_t, mv_trs) in agroups:
                gs = slice(g0, g0 + gl)
                for h in range(4):
                    for mc in range(2):
                        ps, pt = next_psF()
                        mm(ps[:, 0:gl], [(mk_t[:, 2 * h + dd, mc * 128:(mc + 1) * 128], qT.t[:, 2 * h + dd, gs]) for dd in range(2)],
                           reads=mk_trs + [qT.tr[2 * h], qT.tr[2 * h + 1]], ptr=pt)
                        b.op(ACT, lambda: nc.scalar.activation(out=eT.t[:, h * 2 + mc, gs], in_=ps[:, 0:gl], func=AF.Exp), reads=[pt], writes=[eT.tr[h * 2 + mc]])
                    ps, pt = next_psF()
                    mm(ps[:, 0:gl], [(ones[:, :], eT.t[:, h * 2 + mc, gs]) for mc in range(2)], reads=[mtr, eT.tr[h * 2], eT.tr[h * 2 + 1]], ptr=pt)
                    b.op(ACT, lambda: nc.scalar.activation(out=rden.t[:, h, gs], in_=ps[:, 0:gl], func=AF.Ln), reads=[pt], writes=[rden.tr[h]])
                    b.op(ACT, lambda: nc.scalar.activation(out=rden.t[:, h, gs], in_=rden.t[:, h, gs], func=AF.Exp, scale=-1.0), writes=[rden.tr[h]])
                for dc in range(8):
                    h = dc // 2
                    ps, pt = next_psF()
                    mm(ps[:, 0:gl], [(mv_t[:, mc, dc * 128:(dc + 1) * 128], eT.t[:, h * 2 + mc, gs]) for mc in range(2)],
                       reads=mv_trs + [eT.tr[h * 2], eT.tr[h * 2 + 1]], ptr=pt)
                    b.op(DVE, lambda: nc.vector.tensor_tensor(out=oxT.t[:, dc, gs], in0=ps[:, 0:gl], in1=rden.t[:, h, gs], op=ALU.mult),
                         reads=[pt, rden.tr[h]], writes=[oxT.tr[dc]])
            wpair = [w_get("xo0"), w_get("xo1", ahead=False)]
            for s, (c0, n) in enumerate(subt):
                for c in range(2):
                    wb, wt = wpair[c]
                    ps, pt = tok_proj(wb, wt, lambda k: oxT.t[:, k, c0:c0 + n], 8, n, oxT.tr)
                    xsl = xr[:n, s, c * 512:(c + 1) * 512]
                    b.op(DVE, lambda: nc.vector.tensor_tensor(out=xsl, in0=ps[:n, :], in1=xsl, op=ALU.add), reads=[pt], writes=[xrt[s]])
            b.end_phase()

        if nsub == 4:
            rmsnorm_to_hT(xr, xrt, subt, G_FFN, sel=[0, 1, 2])
            rmsnorm_to_hT(xr, xrt, subt, G_FFN, sel=[3])
        else:
            rmsnorm_to_hT(xr, xrt, subt, G_FFN)
        with ExitStack() as ph:
            a2 = FM.__new__(FM)
            a2.t = ph.enter_context(nc.sbuf_tensor(b.nm("a2"), [128, 32, T], BF16))
            a2.tr = [A("a2") for _ in range(32)]
            rt = [ph.enter_context(nc.sbuf_tensor(b.nm("rt"), [128, 512], F32)) for _ in range(2)]
            rt_tr = [A("rt"), A("rt")]
            rc = 0
            for c in range(8):
                wb, wt = w_get("up%d" % c)
                banks = fm_block_split(wb, wt) if (c == 0 and nsub == 4) else None
                for ch4 in range(4):
                    ps, pt = banks[ch4] if banks else fm_proj(wb, wt, ch4, T, hcol, 8, hTr)
                    hch = c * 4 + ch4
                    ri = rc % 2
                    rc += 1
                    b.op(ACT, lambda: nc.scalar.activation(out=rt[ri][:, 0:T], in_=ps[:, 0:T], func=AF.Relu), reads=[pt], writes=[rt_tr[ri]])
                    b.op(POOL, lambda: nc.gpsimd.tensor_tensor(out=a2.t[:, hch, :], in0=rt[ri][:, 0:T], in1=rt[ri][:, 0:T], op=ALU.mult),
                         reads=[rt_tr[ri]], writes=[a2.tr[hch]])
            for c in range(2):
                acc = [next_psF() for _ in range(nsub)]
                for r in range(4):
                    wb, wt = w_get("dn%d_%d" % (r, c))
                    for s, (c0, n) in enumerate(subt):
                        tok_proj(wb, wt, lambda k: a2.t[:, r * 8 + k, c0:c0 + n], 8, n, a2.tr[r * 8:(r + 1) * 8],
                                 ps=acc[s][0], pt=acc[s][1], start=(r == 0), stop=(r == 3))
                for s, (c0, n) in enumerate(subt):
                    ps, pt = acc[s]
                    xsl = xr[:n, s, c * 512:(c + 1) * 512]
                    b.op(DVE, lambda: nc.vector.tensor_tensor(out=xsl, in0=ps[:n, :], in1=xsl, op=ALU.add), reads=[pt], writes=[xrt[s]])
                if c == 0 and ti + 1 < len(tiles) and tiles[ti + 1][0] != "pre":
                    nxr, nxrt = ensure_loaded(ti + 1)
                    nsubt = [(0, 64)] if tiles[ti + 1][0] == "sample" else SUB4
                    rmsnorm_to_hT(nxr, nxrt, nsubt, G_MIX)
                    did_prenorm = True
            b.end_phase()

        rsv, stt = norm_stats(xr, xrt, subt)
        for s, (c0, n) in enumerate(subt):
            yi = cnt["yo"] % 2
            cnt["yo"] += 1
            b.op(DVE, lambda: nc.vector.scalar_tensor_tensor(out=yout[yi][:n, :], in0=xr[:n, s, :], scalar=rsv[:n, s:s + 1], in1=gfin[:n, :],
                                                             op0=ALU.mult, op1=ALU.mult),
                 reads=[xrt[s], stt, gfin_tr], writes=[yout_tr[yi]])
            if is_s:
                b.dma(POOL, ys_o[c0:c0 + n, :], yout[yi][:n, :], st_slot(), reads=[yout_tr[yi]])
            else:
                b.dma(POOL, y_o[t * TM + c0:t * TM + c0 + n, :], yout[yi][:n, :], st_slot(), reads=[yout_tr[yi]])
        return did_prenorm

    if do_mem:
        mem_kv()
    with ExitStack() as pph:
        if n_pre:
            pre_alloc(pph)
        for p in range(n_pre):
            pre_tile(p, p == n_pre - 1)
        b.end_phase()
    b.op(ACT, lambda: nc.scalar.copy(out=Sb[:, :, :], in_=Sst[:, :, :]), reads=S_tr, writes=Sb_tr)
    pn = False
    for t in range(n_main):
        pn = full_tile("main", t, pn)
    b.dma(POOL, hg_o.rearrange("h k v -> k h v"), Sst[:, :, :], st_slot(), reads=S_tr)
    b.dma(POOL, conv_o[:, :, :], uprev[:, :, :], st_slot(), reads=[uprev_tr])
    if sample:
        full_tile("sample", 0, pn)
    fin = []
    for s in b.slots.values():
        if s.count:
            fin.append((s, s.count))
    for e in [PE, ACT, DVE, POOL]:
        if e.count:
            fin.append((e, e.count))
    b._wait(SP, fin)
    es.close()
    return nc


_NC_CACHE = {}


def _prep_inputs(inp):
    f = lambda a: np.ascontiguousarray(np.asarray(a, dtype=np.float32))
    x_prompt = f(inp["x_prompt"])
    x_sample = f(inp["x_sample"])
    mem_prompt = f(inp["mem_prompt"])
    state_conv = f(inp["state_conv"])
    state_hgrn = f(inp["state_hgrn"])
    ckk = f(inp["cache_mem_k"])
    cvv = f(inp["cache_mem_v"])
    p8 = lambda v: np.asarray(v, np.float32).reshape(8, 128).T
    cvec = np.zeros((128, NCV), np.float32)
    cvec[:, 0:8] = p8(inp["norm_mix"][0])
    cvec[:, 8:16] = p8(inp["norm_x"][0])
    cvec[:, 16:24] = p8(inp["norm_ffn"][0])
    cvec[:, 24:32] = p8(inp["norm_mem"][0])
    hl = np.asarray(inp["hg_lb"], np.float32)
    cvec[:, 32:36] = hl[0].reshape(4, 128).T
    cvec[:, 36:40] = hl[1].reshape(4, 128).T
    cw = np.asarray(inp["conv_w"], np.float32)[0]
    cvec[:, 40:52] = cw.reshape(3, 4, 128).transpose(2, 1, 0).reshape(128, 12)
    cvec[:, 52] = np.asarray(inp["hg_norm"], np.float32)[0]
    gfin = np.ascontiguousarray(np.broadcast_to(np.asarray(inp["norm_final"], np.float32)[None, :], (128, D)))
    shared = {
        "cvec": cvec, "gfin": gfin,
        "w_in": f(inp["w_in"][0]), "w_conv_out": f(inp["w_conv_out"][0]), "w_hg_out": f(inp["w_hg_out"][0]),
        "w_o": f(inp["w_o"][0]), "w_xq": f(inp["w_xq"][0]), "w_xk": f(inp["w_xk"][0]), "w_xv": f(inp["w_xv"][0]),
        "w_xo": f(inp["w_xo"][0]), "w_up": f(inp["w_up"][0]), "w_down": f(inp["w_down"][0]),
    }
    maps = []
    for c in range(N_CORES):
        bi, half = c // 2, c % 2
        m = dict(shared)
        m["xm"] = np.ascontiguousarray(x_prompt[bi, half * 4096:(half + 1) * 4096])
        m["xp"] = np.ascontiguousarray(x_prompt[bi, 0:4096]) if half == 1 else np.zeros((4096, D), np.float32)
        xs = np.zeros((64, D), np.float32)
        for j in range(2):
            xs[32 * j:32 * j + 16] = x_sample[2 * c + j]
        m["xs"] = xs
        m["mem"] = np.ascontiguousarray(mem_prompt[bi])
        sc = state_conv[0, 2 * c:2 * c + 2]
        m["sconv"] = np.ascontiguousarray(sc.reshape(2, 2, 4, 128).transpose(3, 2, 0, 1))
        m["shg"] = np.ascontiguousarray(state_hgrn[0, 2 * c:2 * c + 2])
        m["ck"] = np.ascontiguousarray(ckk[0, 2 * c:2 * c + 2].reshape(2, 256, D))
        m["cv"] = np.ascontiguousarray(cvv[0, 2 * c:2 * c + 2].reshape(2, 256, D))
        maps.append(m)
    return maps


def kernel(**inputs):
    if "nc" not in _NC_CACHE:
        _NC_CACHE["nc"] = build()
    nc = _NC_CACHE["nc"]
    maps = _prep_inputs(inputs)
    res = run_bass_kernel_spmd(nc, maps, core_ids=list(range(N_CORES)))
    R = res.results
    y_prompt = np.zeros((4, 8192, D), np.float32)
    y_sample = np.zeros((16, 16, D), np.float32)
    conv_p = np.zeros((1, 4, 2, 512), np.float32)
    hg_p = np.zeros((1, 4, 4, 128, 128), np.float32)
    mk_p = np.zeros((1, 4, 256, 4, 256), np.float32)
    mv_p = np.zeros((1, 4, 256, 4, 256), np.float32)
    conv_s = np.zeros((1, 16, 2, 512), np.float32)
    hg_s = np.zeros((1, 16, 4, 128, 128), np.float32)
    for c in range(N_CORES):
        bi, half = c // 2, c % 2
        r = R[c]
        y_prompt[bi, half * 4096:(half + 1) * 4096] = r["y"]
        for j in range(2):
            y_sample[2 * c + j] = r["ys"][32 * j:32 * j + 16]
            conv_s[0, 2 * c + j] = r["convs_o"][:, :, j, :].transpose(2, 1, 0).reshape(2, 512)
            hg_s[0, 2 * c + j] = r["hgs_o"][j]
        if half == 1:
            conv_p[0, bi] = r["conv_o"].transpose(2, 1, 0).reshape(2, 512)
            hg_p[0, bi] = r["hg_o"]
        else:
            mk_p[0, bi] = r["mk_o"].reshape(256, 4, 256)
            mv_p[0, bi] = r["mv_o"].reshape(256, 4, 256)
    return (y_prompt, y_sample, conv_p, hg_p, mk_p, mv_p, conv_s, hg_s)
```

```python
import numpy as np
from contextlib import ExitStack
import concourse.bass as bass
import concourse.mybir as mybir
from concourse.bass_utils import run_bass_kernel_spmd

F32 = mybir.dt.float32
BF16 = mybir.dt.bfloat16
AF = mybir.ActivationFunctionType
ALU = mybir.AluOpType

D = 1024
TM = 512
EPS = 1e-6
NCV = 56
N_CORES = 8


class Tr:
    __slots__ = ("w", "r", "name")

    def __init__(self, name="", fence=None):
        self.w = None
        self.r = list(fence) if fence else []
        self.name = name


class Src:
    def __init__(self, name, sem, unit):
        self.name, self.sem, self.unit, self.count = name, sem, unit, 0


class Eng(Src):
    def __init__(self, name, h, sem, self_sync):
        super().__init__(name, sem, 1)
        self.h = h
        self.seen = {}
        self.self_sync = self_sync


class FM:
    def __init__(self, b, name, nch, T, dt, es):
        self.t = es.enter_context(b.nc.sbuf_tensor(b.nm(name), [128, nch, T], dt))
        self.tr = [b.newtr(name + str(i)) for i in range(nch)]
        self.nch, self.T = nch, T


class Bld:
    def __init__(self):
        self.nc = bass.Bass("TRN2", target_bir_lowering=False)
        self.es = ExitStack()
        self.uid = 0
        self.fence = {}
        self.phase_trs = []
        nc = self.nc
        mk = lambda n: self.es.enter_context(nc.semaphore(n))
        self.PE = Eng("pe", nc.tensor, mk("s_pe"), False)
        self.ACT = Eng("act", nc.scalar, mk("s_act"), True)
        self.DVE = Eng("dve", nc.vector, mk("s_dve"), True)
        self.POOL = Eng("pool", nc.gpsimd, mk("s_pool"), True)
        self.SP = Eng("sp", nc.sync, mk("s_sp"), False)
        self.slots = {}

    def nm(self, n):
        self.uid += 1
        return "%s_%d" % (n, self.uid)

    def slot(self, name):
        if name not in self.slots:
            self.slots[name] = Src(name, self.es.enter_context(self.nc.semaphore("d_" + name)), 16)
        return self.slots[name]

    def newtr(self, name="", arena=False):
        if arena:
            t = Tr(name, fence=list(self.fence.items()))
            self.phase_trs.append(t)
            return t
        return Tr(name)

    def end_phase(self):
        for t in self.phase_trs:
            acc = list(t.r)
            if t.w:
                acc.append(t.w)
            for (s, i) in acc:
                if self.fence.get(s, 0) < i:
                    self.fence[s] = i
        self.phase_trs = []

    def _wait(self, e, deps):
        need = {}
        for (s, i) in deps:
            if s is e:
                if not e.self_sync:
                    continue
            if need.get(s, 0) < i:
                need[s] = i
        for s, i in need.items():
            if e.seen.get(s, 0) < i:
                e.h.wait_ge(s.sem, i * s.unit)
                e.seen[s] = i

    @staticmethod
    def _deps(reads, writes):
        deps = []
        for t in reads:
            if t.w:
                deps.append(t.w)
        for t in writes:
            if t.w:
                deps.append(t.w)
            deps.extend(t.r)
        return deps

    def op(self, e, fn, reads=(), writes=()):
        self._wait(e, self._deps(reads, writes))
        ins = fn()
        e.count += 1
        ins.then_inc(e.sem, 1)
        me = (e, e.count)
        for t in reads:
            t.r.append(me)
        for t in writes:
            t.w = me
            t.r = []

    def dma(self, e, out, in_, slot, reads=(), writes=(), chain=False):
        deps = self._deps(reads, writes)
        if slot.count > 0 and not chain:
            deps.append((slot, slot.count))
        self._wait(e, deps)
        e.h.dma_start(out=out, in_=in_).then_inc(slot.sem, 16)
        slot.count += 1
        me = (slot, slot.count)
        for t in reads:
            t.r.append(me)
        for t in writes:
            t.w = me
            t.r = []


def build(n_pre=8, n_main=8, sample=True, do_mem=True):
    b = Bld()
    nc = b.nc
    es = b.es
    PE, ACT, DVE, POOL, SP = b.PE, b.ACT, b.DVE, b.POOL, b.SP

    def din(name, shape):
        return nc.dram_tensor(name, list(shape), F32, kind="ExternalInput").ap()

    def dout(name, shape):
        return nc.dram_tensor(name, list(shape), F32, kind="ExternalOutput").ap()

    xm = din("xm", [4096, D])
    xp = din("xp", [4096, D])
    xs = din("xs", [64, D])
    mem = din("mem", [256, D])
    sconv = din("sconv", [128, 4, 2, 2])
    shg = din("shg", [2, 4, 128, 128])
    ck = din("ck", [2, 256, D])
    cv = din("cv", [2, 256, D])
    cvec_d = din("cvec", [128, NCV])
    gfin_d = din("gfin", [128, D])
    w_in = din("w_in", [D, 5632])
    w_conv_out = din("w_conv_out", [512, D])
    w_hg_out = din("w_hg_out", [512, D])
    w_o = din("w_o", [D, D])
    w_xq = din("w_xq", [D, D])
    w_xk = din("w_xk", [D, D])
    w_xv = din("w_xv", [D, D])
    w_xo = din("w_xo", [D, D])
    w_up = din("w_up", [D, 4096])
    w_down = din("w_down", [4096, D])

    y_o = dout("y", [4096, D])
    ys_o = dout("ys", [64, D])
    conv_o = dout("conv_o", [128, 4, 2])
    hg_o = dout("hg_o", [4, 128, 128])
    mk_o = dout("mk_o", [256, D])
    mv_o = dout("mv_o", [256, D])
    convs_o = dout("convs_o", [128, 4, 2, 2])
    hgs_o = dout("hgs_o", [2, 4, 128, 128])

    def kp(ap):
        return ap.rearrange("(k p) c -> p k c", p=128)

    blocks = {}
    for j, nme in enumerate(["cb", "cc", "cx", "hq", "hf", "hi", "hg", "ga0", "ga1", "gb0", "gb1"]):
        blocks[nme] = [(kp(w_in)[:, :, j * 512:(j + 1) * 512], 0, 8)]
    for c in range(2):
        cs = slice(c * 512, (c + 1) * 512)
        blocks["cvhg%d" % c] = [(kp(w_conv_out)[:, :, cs], 0, 4), (kp(w_hg_out)[:, :, cs], 4, 4)]
        blocks["wo%d" % c] = [(kp(w_o)[:, :, cs], 0, 8)]
        blocks["xq%d" % c] = [(kp(w_xq)[:, :, cs], 0, 8)]
        blocks["xk%d" % c] = [(kp(w_xk)[:, :, cs], 0, 8)]
        blocks["xv%d" % c] = [(kp(w_xv)[:, :, cs], 0, 8)]
        blocks["xo%d" % c] = [(kp(w_xo)[:, :, cs], 0, 8)]
        for r in range(4):
            blocks["dn%d_%d" % (r, c)] = [(kp(w_down[r * 1024:(r + 1) * 1024, :])[:, :, cs], 0, 8)]
    for c in range(8):
        blocks["up%d" % c] = [(kp(w_up)[:, :, c * 512:(c + 1) * 512], 0, 8)]
    bnames = list(blocks.keys())
    bidx = {n: i for i, n in enumerate(bnames)}
    wscr = nc.dram_tensor("wscr", [len(bnames), 128, 8, 512], BF16, kind="Internal").ap()
    scr_tr = {n: Tr("scr_" + n) for n in bnames}
    converted = set()

    main_seq = (["hf", "cc", "cx", "cb", "hq", "hi", "hg", "ga0", "ga1", "gb0", "gb1", "cvhg0", "cvhg1",
                 "wo0", "wo1", "xq0", "xq1", "xo0", "xo1"] + ["up%d" % c for c in range(8)]
                + ["dn%d_%d" % (r, c) for c in range(2) for r in range(4)])
    seq = ["xk0", "xk1", "xv0", "xv1"] if do_mem else []
    for p in range(n_pre):
        seq += ["hf", "hi"]
        if p == n_pre - 1:
            seq += ["cc", "cx"]
    for t in range(n_main):
        seq += main_seq
    if sample:
        seq += main_seq

    NWB = 3
    wbuf = [es.enter_context(nc.sbuf_tensor("wbuf%d" % i, [128, 8, 512], BF16)) for i in range(NWB)]
    wtr = [Tr("wbuf%d" % i) for i in range(NWB)]
    wslot = [b.slot("wld%d" % i) for i in range(NWB)]
    wslot_sw = [b.slot("wlds%d" % i) for i in range(NWB)]
    sslot = [b.slot("wst%d" % i) for i in range(4)]
    wstate = {"issued": 0, "pos": 0, "nst": 0}

    def w_issue(j):
        name = seq[j]
        s = j % NWB
        if name not in converted:
            converted.add(name)
            first = True
            for (src, k0, nk) in blocks[name]:
                b.dma(POOL, wbuf[s][:, k0:k0 + nk, :], src, wslot_sw[s], writes=[wtr[s]], chain=not first)
                first = False
            st = sslot[wstate["nst"] % 4]
            wstate["nst"] += 1
            b.dma(POOL, wscr[bidx[name]], wbuf[s][:, :, :], st, reads=[wtr[s]], writes=[scr_tr[name]])
        else:
            b.dma(SP, wbuf[s][:, :, :], wscr[bidx[name]], wslot[s], reads=[scr_tr[name]], writes=[wtr[s]])

    def w_get(expect, ahead=True):
        i = wstate["pos"]
        assert seq[i] == expect, (seq[i], expect, i)
        while wstate["issued"] < (min(i + NWB, len(seq)) if ahead else i + 1):
            w_issue(wstate["issued"])
            wstate["issued"] += 1
        wstate["pos"] += 1
        return wbuf[i % NWB], wtr[i % NWB]

    def sb(name, shape, dt=F32):
        return es.enter_context(nc.sbuf_tensor("sb_" + name, list(shape), dt))

    NXB = 2
    xres = [sb("xres%d" % i, [128, 4, D]) for i in range(NXB)]
    xres_tr = [[Tr("xres") for _ in range(4)] for _ in range(NXB)]
    hT = sb("hT", [128, 8, TM], BF16)
    hT_tr = [Tr("hT%d" % s) for s in range(4)]
    Sst = sb("Sst", [128, 4, 128])
    S_tr = [Tr("S%d" % h) for h in range(4)]
    Sb = sb("Sb", [128, 4, 128], BF16)
    Sb_tr = [Tr("Sb%d" % h) for h in range(4)]
    cvec = sb("cvec", [128, NCV])
    gfin = sb("gfin", [128, D])
    lbv = sb("lbv", [128, 4])
    omlb = sb("omlb", [128, 4])
    ident = sb("ident", [128, 128], BF16)
    ones = sb("ones", [128, 128], BF16)
    mask64 = sb("mask64", [128, 128])
    mask32 = sb("mask32", [64, 64])
    zeros = sb("zeros", [128, 64])
    neghalf = sb("neghalf", [128, 4])
    ccs_zero = sb("zeros512", [128, TM])
    mkT = sb("mkT", [128, 8, 256], BF16)
    mv = sb("mv", [128, 2, D], BF16)
    mkT_tr, mv_tr = Tr("mkT"), Tr("mv")
    uprev = sb("uprev", [128, 4, 2])
    uprev_tr = Tr("uprev")
    junk = sb("junk", [128, D], BF16)
    junk_tr = Tr("junk")
    xn = [sb("xn%d" % i, [128, D], BF16) for i in range(2)]
    xn_tr = [Tr("xn0"), Tr("xn1")]
    yout = [sb("yout%d" % i, [128, D]) for i in range(2)]
    yout_tr = [Tr("yo0"), Tr("yo1")]
    stat = sb("stat", [128, 16])
    stat_tr = [Tr("stat%d" % i) for i in range(4)]
    const_tr = Tr("const")
    cnt = {"xn": 0, "yo": 0, "st": 0, "psF": 0, "psB": 0, "ld": 0, "os": 0}

    NPF = 4
    psF = [es.enter_context(nc.psum_tensor("psF%d" % i, [128, 512], F32)) for i in range(NPF)]
    psF_tr = [Tr("psF%d" % i) for i in range(NPF)]
    psB = [es.enter_context(nc.psum_tensor("psB%d" % i, [128, 1024], BF16)) for i in range(2)]
    psB_tr = [Tr("psB0"), Tr("psB1")]

    psO = [es.enter_context(nc.psum_tensor("psO%d" % i, [128, 512], F32)) for i in range(2)]
    psO_tr = [Tr("psO0"), Tr("psO1")]

    def next_psO():
        i = cnt.setdefault("psO", 0) % 2
        cnt["psO"] += 1
        return psO[i], psO_tr[i]

    def next_psF():
        if cnt.get("in_rec", 0):
            i = cnt["psF"] % NPF
            cnt["psF"] += 1
            return psF[i], psF_tr[i]
        i = cnt.setdefault("ps6", 0) % (NPF + 2)
        cnt["ps6"] += 1
        if i < NPF:
            return psF[i], psF_tr[i]
        return psO[i - NPF], psO_tr[i - NPF]

    def next_psB():
        i = cnt["psB"] % 2
        cnt["psB"] += 1
        return psB[i], psB_tr[i]

    ldslots = [b.slot("ld%d" % i) for i in range(6)]
    stslots = [b.slot("st%d" % i) for i in range(6)]

    def ld_slot():
        cnt["ld"] += 1
        return ldslots[cnt["ld"] % 6]

    plslots = [b.slot("pl%d" % i) for i in range(2)]

    def pl_slot():
        cnt["pl"] = cnt.get("pl", 0) + 1
        return plslots[cnt["pl"] % 2]

    def st_slot():
        cnt["os"] += 1
        return stslots[cnt["os"] % 6]

    def mm(out_ap, pairs, reads, ptr, start=True, stop=True):
        def fn():
            ins = None
            n = len(pairs)
            for i, (l, r) in enumerate(pairs):
                ins = nc.tensor.matmul(out_ap, lhsT=l, rhs=r, start=(start and i == 0), stop=(stop and i == n - 1))
            return ins
        b.op(PE, fn, reads=reads, writes=[ptr])

    b.dma(SP, cvec[:, :], cvec_d[:, :], ld_slot(), writes=[const_tr])
    gfin_tr = Tr("gfin")
    b.dma(SP, gfin[:, :], gfin_d[:, :], ld_slot(), writes=[gfin_tr])
    mtr = Tr("masks")

    P = lambda fn, wr: b.op(POOL, fn, writes=wr)
    id_tr = Tr("ident")
    P(lambda: nc.gpsimd.memset(ident[:, :], 1.0), [id_tr])
    P(lambda: nc.gpsimd.affine_select(out=ident[:, :], in_=ident[:, :], pattern=[[-1, 128]], compare_op=ALU.is_equal,
                                      fill=0.0, base=0, channel_multiplier=1), [id_tr])
    m64_tr, m32_tr = Tr("m64"), Tr("m32")
    P(lambda: nc.gpsimd.memset(mask64[:, :], 1.0), [m64_tr])
    P(lambda: nc.gpsimd.memset(mask32[:, :], 1.0), [m32_tr])
    P(lambda: nc.gpsimd.affine_select(out=mask64[:, :], in_=mask64[:, :], pattern=[[1, 128]], compare_op=ALU.is_ge,
                                      fill=0.0, base=0, channel_multiplier=-1), [m64_tr])
    P(lambda: nc.gpsimd.affine_select(out=mask32[:, :], in_=mask32[:, :], pattern=[[1, 64]], compare_op=ALU.is_ge,
                                      fill=0.0, base=0, channel_multiplier=-1), [m32_tr])
    P(lambda: nc.gpsimd.memset(mask64[0:64, 64:128], 0.0), [m64_tr])
    P(lambda: nc.gpsimd.memset(mask32[0:32, 32:64], 0.0), [m32_tr])
    P(lambda: nc.gpsimd.memset(uprev[:, :, :], 0.0), [uprev_tr])
    P(lambda: nc.gpsimd.memset(stat[:, :], 1.0), stat_tr)
    P(lambda: nc.gpsimd.memset(Sst[:, :, :], 0.0), S_tr)
    P(lambda: nc.gpsimd.memset(zeros[:, :], 0.0), [mtr])
    P(lambda: nc.gpsimd.memset(neghalf[:, :], -0.5), [mtr])
    P(lambda: nc.gpsimd.memset(ccs_zero[:, :], 0.0), [mtr])
    b.op(POOL, lambda: nc.gpsimd.memset(ones[:, :], 1.0), reads=[id_tr, m64_tr, m32_tr], writes=[mtr])
    lb_tr = Tr("lb")
    b.op(DVE, lambda: nc.vector.tensor_tensor(out=lbv[:, :], in0=cvec[:, 32:36], in1=cvec[:, 36:40], op=ALU.subtract),
         reads=[const_tr], writes=[lb_tr])
    b.op(ACT, lambda: nc.scalar.activation(out=lbv[:, :], in_=lbv[:, :], func=AF.Sigmoid), writes=[lb_tr])
    b.op(DVE, lambda: nc.vector.tensor_scalar(out=omlb[:, :], in0=lbv[:, :], scalar1=-1.0, scalar2=1.0,
                                              op0=ALU.mult, op1=ALU.add), reads=[lb_tr], writes=[const_tr])
    G_MIX, G_X, G_FFN, G_MEM = 0, 8, 16, 24

    def norm_stats(xr, xr_tr, subt, sel=None):
        nsub = len(subt)
        nmax = max(n for (_, n) in subt)
        if sel is None:
            sel = list(range(nsub))
        g = cnt["st"] % 2
        cnt["st"] += 1
        ssv = stat[:, g * 8:g * 8 + 4]
        rsv = stat[:, g * 8 + 4:g * 8 + 8]
        stt = stat_tr[g]
        for s in sel:
            c0, n = subt[s]
            b.op(ACT, lambda: nc.scalar.activation(out=junk[:n, :], in_=xr[:n, s, :], func=AF.Square, accum_out=ssv[:n, s:s + 1]),
                 reads=[xr_tr[s]], writes=[junk_tr, stt])
        b.op(ACT, lambda: nc.scalar.activation(out=rsv[:nmax, 0:nsub], in_=ssv[:nmax, 0:nsub], func=AF.Ln, scale=1.0 / D, bias=EPS), writes=[stt])
        b.op(ACT, lambda: nc.scalar.activation(out=rsv[:nmax, 0:nsub], in_=rsv[:nmax, 0:nsub], func=AF.Exp, scale=-0.5), writes=[stt])
        return rsv, stt

    def rmsnorm_to_hT(xr, xr_tr, subt, goff, sel=None):
        rsv, stt = norm_stats(xr, xr_tr, subt, sel)
        for s in (sel if sel is not None else range(len(subt))):
            c0, n = subt[s]
            xi = cnt["xn"] % 2
            cnt["xn"] += 1
            b.op(ACT, lambda: nc.scalar.activation(out=xn[xi][:n, :], in_=xr[:n, s, :], func=AF.Copy, scale=rsv[:n, s:s + 1]),
                 reads=[xr_tr[s], stt], writes=[xn_tr[xi]])
            pb, pbt = next_psB()

            def tp():
                ins = None
                for k in range(8):
                    ins = nc.tensor.transpose(pb[:, k * 128:k * 128 + n], xn[xi][:n, k * 128:(k + 1) * 128], ident[:n, :n])
                return ins
            b.op(PE, tp, reads=[xn_tr[xi], mtr], writes=[pbt])
            pv = pb[:, :].rearrange("p (k t) -> p k t", t=128)[:, :, 0:n]
            gb = cvec[:, goff:goff + 8].unsqueeze(2).to_broadcast([128, 8, n])
            b.op(DVE, lambda: nc.vector.tensor_tensor(out=hT[:, :, c0:c0 + n], in0=pv, in1=gb, op=ALU.mult),
                 reads=[pbt, const_tr], writes=[hT_tr[s]])

    def fm_proj(wb, wt, ch4, ncols, rhs_of_k, nk, rd, koff=0):
        ps, pt = next_psF()
        mm(ps[:, 0:ncols], [(wb[:, koff + k, ch4 * 128:(ch4 + 1) * 128], rhs_of_k(k)) for k in range(nk)],
           reads=[wt] + rd, ptr=pt)
        return ps, pt

    def fm_block_split(wb, wt):
        banks = [next_psF() for _ in range(4)]
        for (lo, hi, trs) in [(0, 384, hT_tr[0:3]), (384, 512, hT_tr[3:4])]:
            for ch4 in range(4):
                ps, pt = banks[ch4]

                def fn():
                    ins = None
                    for k in range(8):
                        ins = nc.tensor.matmul(ps[:, lo:hi], lhsT=wb[:, k, ch4 * 128:(ch4 + 1) * 128], rhs=hT[:, k, lo:hi],
                                               start=(lo == 0 and k == 0), stop=(hi == 512 and k == 7), skip_group_check=True)
                    return ins
                b.op(PE, fn, reads=[wt] + trs, writes=[pt])
        return banks

    def tok_proj(wb, wt, lhs_of_k, nk, n, rd, ps=None, pt=None, start=True, stop=True):
        if ps is None:
            ps, pt = next_psF()
        mm(ps[:n, :], [(lhs_of_k(k), wb[:, k, :]) for k in range(nk)], reads=[wt] + rd, ptr=pt, start=start, stop=stop)
        return ps, pt

    def load_x(src_rows, xr, xr_tr, subt):
        for s, (c0, n) in enumerate(subt):
            b.dma(SP, xr[:n, s, :], src_rows[c0:c0 + n, :], ld_slot(), writes=[xr_tr[s]])

    SUB4 = [(i * 128, 128) for i in range(4)]
    tiles = [("pre", p) for p in range(n_pre)] + [("main", t) for t in range(n_main)] + ([("sample", 0)] if sample else [])
    tstate = {"loaded": 0}

    def ensure_loaded(i):
        while tstate["loaded"] <= i and tstate["loaded"] < len(tiles):
            j = tstate["loaded"]
            kind, idx = tiles[j]
            if kind == "pre":
                src, st = xp[idx * TM:(idx + 1) * TM, :], SUB4
            elif kind == "main":
                src, st = xm[idx * TM:(idx + 1) * TM, :], SUB4
            else:
                src, st = xs, [(0, 64)]
            load_x(src, xres[j % NXB], xres_tr[j % NXB], st)
            tstate["loaded"] += 1
        return xres[i % NXB], xres_tr[i % NXB]

    def hgrn_gates(T, L, last, ar, hTr, tmp, tmp_tr, part=0):
        fb, eb, kt, kpb = ar["fb"], ar["eb"], ar["kt"], ar["kp"]
        nch = T // L
        if part in (0, 1):
            wb, wt = w_get("hf")
        for h in (range(4) if part in (0, 1) else []):
            ps, pt = fm_proj(wb, wt, h, T, lambda k: hT[:, k, 0:T], 8, hTr)
            b.op(ACT, lambda: nc.scalar.activation(out=fb.t[:, h, :], in_=ps[:, 0:T], func=AF.Sigmoid),
                 reads=[pt], writes=[fb.tr[h]])
            b.op(ACT, lambda: nc.scalar.activation(out=fb.t[:, h, :], in_=fb.t[:, h, :], func=AF.Identity,
                                                   scale=omlb[:, h:h + 1], bias=lbv[:, h:h + 1]),
                 reads=[const_tr, lb_tr], writes=[fb.tr[h]])
        for h in (range(4) if part in (0, 1) else []):
            b.op(ACT, lambda: nc.scalar.activation(out=tmp[h][:, 0:T], in_=fb.t[:, h, :], func=AF.Ln), reads=[fb.tr[h]], writes=[tmp_tr[h]])
        for h in (range(4) if part in (0, 1) else []):
            def scans():
                ins = None
                for c in range(nch):
                    ins = nc.vector.tensor_tensor_scan(out=eb.t[:, h, c * L:(c + 1) * L], data0=tmp[h][:, c * L:(c + 1) * L],
                                                       data1=zeros[:, 0:L], initial=0.0, op0=ALU.add, op1=ALU.add)
                return ins
            b.op(DVE, scans, reads=[tmp_tr[h], mtr], writes=[eb.tr[h]])
        if part == 1:
            return
        for h in range(4):
            b.op(ACT, lambda: nc.scalar.activation(out=tmp[h][:, 0:T], in_=eb.t[:, h, :], func=AF.Exp, scale=-1.0), reads=[eb.tr[h]], writes=[tmp_tr[h]])
            b.op(ACT, lambda: nc.scalar.activation(out=eb.t[:, h, :], in_=eb.t[:, h, :], func=AF.Exp), writes=[eb.tr[h]])
        for h in range(4):
            b.op(POOL, lambda: nc.gpsimd.tensor_scalar(out=fb.t[:, h, :], in0=fb.t[:, h, :], scalar1=-1.0, scalar2=1.0,
                                                       op0=ALU.mult, op1=ALU.add), writes=[fb.tr[h]])
            b.op(POOL, lambda: nc.gpsimd.tensor_tensor(out=kt.t[:, h, :], in0=fb.t[:, h, :], in1=tmp[h][:, 0:T], op=ALU.mult),
                 reads=[fb.tr[h], tmp_tr[h]], writes=[kt.tr[h]])
            ktv = kt.t[:, h, :].rearrange("p (c l) -> p c l", l=L)
            kpv = kpb.t[:, h, :].rearrange("p (c l) -> p c l", l=L)
            elb = eb.t[:, h, :].rearrange("p (c l) -> p c l", l=L)[:, :, last:last + 1].to_broadcast([128, nch, L])
            b.op(POOL, lambda: nc.gpsimd.tensor_tensor(out=kpv, in0=ktv, in1=elb, op=ALU.mult),
                 reads=[kt.tr[h], eb.tr[h]], writes=[kpb.tr[h]])

    def v_proj(subt, ar, hTr):
        wb, wt = w_get("hi")
        v = ar["v"]
        for s, (c0, n) in enumerate(subt):
            ps, pt = tok_proj(wb, wt, lambda k: hT[:, k, c0:c0 + n], 8, n, hTr)
            b.op(ACT, lambda: nc.scalar.copy(out=v.t[:n, s, :], in_=ps[:n, :]), reads=[pt], writes=[v.tr[s]])

    def kp_transpose(subt, ar):
        kpb, kptok = ar["kp"], ar["kptok"]
        for s, (c0, n) in enumerate(subt):
            pb, pbt = next_psB()

            def tp():
                ins = None
                for h in range(4):
                    ins = nc.tensor.transpose(pb[:n, h * 128:(h + 1) * 128], kpb.t[:, h, c0:c0 + n], ident[:, :])
                return ins
            b.op(PE, tp, reads=kpb.tr + [mtr], writes=[pbt])
            b.op(ACT, lambda: nc.scalar.copy(out=kptok.t[:n, s, :], in_=pb[:n, 0:512]), reads=[pbt], writes=[kptok.tr[s]])

    def state_update(s, h, c, L, last, c0, ar, S_ap, S_t, e_ap):
        kptok, v, eb = ar["kptok"], ar["v"], ar["eb"]
        pP, pPt = next_psF()
        r0 = c * L
        mm(pP[:, 0:128], [(kptok.t[r0:r0 + L, s, h * 128:(h + 1) * 128], v.t[r0:r0 + L, s, h * 128:(h + 1) * 128])],
           reads=[kptok.tr[s], v.tr[s]], ptr=pPt)
        b.op(DVE, lambda: nc.vector.scalar_tensor_tensor(out=S_ap, in0=S_ap, scalar=e_ap, in1=pP[:, 0:128],
                                                         op0=ALU.mult, op1=ALU.add),
             reads=[pPt, eb.tr[h]], writes=[S_t])

    def mem_kv():
        subt = [(0, 128), (128, 128)]
        xr, xrt = xres[0], xres_tr[0]
        load_x(mem, xr, xrt, subt)
        rmsnorm_to_hT(xr, xrt, subt, G_MEM)
        hTr = hT_tr[0:2]
        for c in range(2):
            wb, wt = w_get("xk%d" % c)
            for ch4 in range(4):
                ps, pt = fm_proj(wb, wt, ch4, 256, lambda k: hT[:, k, 0:256], 8, hTr)
                b.op(ACT, lambda: nc.scalar.copy(out=mkT[:, c * 4 + ch4, :], in_=ps[:, 0:256]), reads=[pt], writes=[mkT_tr])
            for s, (c0, n) in enumerate(subt):
                ps, pt = tok_proj(wb, wt, lambda k: hT[:, k, c0:c0 + n], 8, n, hTr)
                yi = cnt["yo"] % 2
                cnt["yo"] += 1
                b.op(ACT, lambda: nc.scalar.copy(out=yout[yi][:, 0:512], in_=ps[:, :]), reads=[pt], writes=[yout_tr[yi]])
                b.dma(POOL, mk_o[c0:c0 + n, c * 512:(c + 1) * 512], yout[yi][:, 0:512], st_slot(), reads=[yout_tr[yi]])
        for c in range(2):
            wb, wt = w_get("xv%d" % c)
            for s, (c0, n) in enumerate(subt):
                ps, pt = tok_proj(wb, wt, lambda k: hT[:, k, c0:c0 + n], 8, n, hTr)
                yi = cnt["yo"] % 2
                cnt["yo"] += 1
                b.op(ACT, lambda: nc.scalar.copy(out=yout[yi][:, 0:512], in_=ps[:, :]), reads=[pt], writes=[yout_tr[yi]])
                b.op(DVE, lambda: nc.vector.tensor_copy(out=mv[:, s, c * 512:(c + 1) * 512], in_=ps[:, :]), reads=[pt], writes=[mv_tr])
                b.dma(POOL, mv_o[c0:c0 + n, c * 512:(c + 1) * 512], yout[yi][:, 0:512], st_slot(), reads=[yout_tr[yi]])

    pre_sets = []

    def pre_alloc(ph):
        A = lambda n: b.newtr(n, arena=True)
        for i in range(2):
            ar = {}
            for nme, dt, W in [("fb", F32, TM), ("cf", F32, TM + 1), ("kp", BF16, TM)]:
                f = FM.__new__(FM)
                f.t = ph.enter_context(nc.sbuf_tensor(b.nm(nme), [128, 4, W], dt))
                f.tr = [A(nme) for _ in range(4)]
                ar[nme] = f
            for nme in ["v", "kptok"]:
                f = FM.__new__(FM)
                f.t = ph.enter_context(nc.sbuf_tensor(b.nm(nme), [128, 4, 512], BF16))
                f.tr = [A(nme) for _ in range(4)]
                ar[nme] = f
            pre_sets.append(ar)
        ccs = ph.enter_context(nc.sbuf_tensor(b.nm("ccs"), [128, 4, TM], F32))
        pre_sets.append((ccs, [A("ccs") for _ in range(4)]))

    def pre_tile(p, lastp):
        T = TM
        subt = [(i * 128, 128) for i in range(4)]
        xr, xrt = ensure_loaded(p)
        rmsnorm_to_hT(xr, xrt, subt, G_MIX)
        ensure_loaded(p + 1)
        ar = pre_sets[p % 2]
        fb, cf, kpb, kptok, v = ar["fb"], ar["cf"], ar["kp"], ar["kptok"], ar["v"]
        wb, wt = w_get("hf")
        for h in range(4):
            ps, pt = fm_proj(wb, wt, h, T, lambda k: hT[:, k, 0:T], 8, hT_tr)
            b.op(ACT, lambda: nc.scalar.activation(out=fb.t[:, h, :], in_=ps[:, 0:T], func=AF.Sigmoid), reads=[pt], writes=[fb.tr[h]])
            b.op(DVE, lambda: nc.vector.tensor_scalar(out=fb.t[:, h, :], in0=fb.t[:, h, :], scalar1=omlb[:, h:h + 1],
                                                      scalar2=lbv[:, h:h + 1], op0=ALU.mult, op1=ALU.add),
                 reads=[const_tr, lb_tr], writes=[fb.tr[h]])
            b.op(POOL, lambda: nc.gpsimd.memset(cf.t[:, h, T:T + 1], 1.0), writes=[cf.tr[h]])
            b.op(DVE, lambda: nc.vector.tensor_tensor_scan(out=cf.t[:, h, T - 1::-1], data0=fb.t[:, h, T - 1::-1], data1=ccs_zero[:, 0:T],
                                                           initial=1.0, op0=ALU.mult, op1=ALU.add),
                 reads=[fb.tr[h], mtr], writes=[cf.tr[h]])
            b.op(POOL, lambda: nc.gpsimd.tensor_scalar(out=fb.t[:, h, :], in0=fb.t[:, h, :], scalar1=-1.0, scalar2=1.0,
                                                       op0=ALU.mult, op1=ALU.add), writes=[fb.tr[h]])
            b.op(POOL, lambda: nc.gpsimd.tensor_tensor(out=kpb.t[:, h, :], in0=fb.t[:, h, :], in1=cf.t[:, h, 1:T + 1], op=ALU.mult),
                 reads=[fb.tr[h], cf.tr[h]], writes=[kpb.tr[h]])
        v_proj(subt, ar, hT_tr)
        kp_transpose(subt, ar)
        pP, pPt = next_psF()

        def pm():
            ins = None
            for h in range(4):
                for s in range(4):
                    ins = nc.tensor.matmul(pP[:, h * 128:(h + 1) * 128], lhsT=kptok.t[:, s, h * 128:(h + 1) * 128],
                                           rhs=v.t[:, s, h * 128:(h + 1) * 128], start=(h == 0 and s == 0), stop=(s == 3), skip_group_check=True)
            return ins
        b.op(PE, pm, reads=kptok.tr + v.tr, writes=[pPt])
        for h in range(4):
            b.op(DVE, lambda: nc.vector.scalar_tensor_tensor(out=Sst[:, h, :], in0=Sst[:, h, :], scalar=cf.t[:, h, 0:1], in1=pP[:, h * 128:(h + 1) * 128],
                                                             op0=ALU.mult, op1=ALU.add),
                 reads=[pPt, cf.tr[h]], writes=[S_tr[h]])
        if lastp:
            ccs, ccs_tr = pre_sets[2]
            wb, wt = w_get("cc")
            for ch in range(4):
                ps, pt = fm_proj(wb, wt, ch, T, lambda k: hT[:, k, 0:T], 8, hT_tr)
                b.op(ACT, lambda: nc.scalar.copy(out=ccs[:, ch, :], in_=ps[:, 0:T]), reads=[pt], writes=[ccs_tr[ch]])
            wb, wt = w_get("cx")
            for ch in range(4):
                ps, pt = fm_proj(wb, wt, ch, T, lambda k: hT[:, k, 0:T], 8, hT_tr)
                b.op(DVE, lambda: nc.vector.tensor_tensor(out=uprev[:, ch, :], in0=ps[:, T - 2:T], in1=ccs[:, ch, T - 2:T], op=ALU.mult),
                     reads=[pt, ccs_tr[ch]], writes=[uprev_tr])

    def full_tile(kind, t, prenormed=False):
        is_s = kind == "sample"
        if is_s:
            T, L, last = 64, 32, 15
            subt = [(0, 64)]
            segs = [(0, 32), (32, 32)]
        else:
            T, L, last = TM, 64, 63
            subt = [(i * 128, 128) for i in range(4)]
            segs = [(0, TM)]
        nsub = len(subt)
        nseg = len(segs)
        ti = n_pre + (n_main if is_s else t)
        xr, xrt = ensure_loaded(ti)
        hTr = hT_tr[0:nsub]
        if not prenormed:
            rmsnorm_to_hT(xr, xrt, subt, G_MIX)
        ensure_loaded(ti + 1)
        did_prenorm = False
        A = lambda n: b.newtr(n, arena=True)
        hcol = lambda k: hT[:, k, 0:T]

        with ExitStack() as ph:
            def fm(nme, nch, dt, W=T):
                f = FM.__new__(FM)
                f.t = ph.enter_context(nc.sbuf_tensor(b.nm(nme), [128, nch, W], dt))
                f.tr = [A(nme) for _ in range(nch)]
                return f
            UW = T + 2 * nseg
            ccs = fm("ccs", 4, F32)
            u = fm("u", 4, F32, UW)
            z = fm("z", 4, BF16)
            ar = {"fb": fm("fb", 4, F32), "eb": fm("eb", 4, F32),
                  "kt": fm("kt", 4, BF16), "kp": fm("kp", 4, BF16)}
            qt = fm("qt", 4, BF16)
            sgt = fm("sgt", 4, BF16)
            oN = fm("oN", 4, BF16)
            sga = fm("sga", 8, BF16)
            sgb = fm("sgb", 8, BF16)
            mrgb = fm("mrgb", 8, BF16)
            for nme in ["v", "kptok"]:
                f = FM.__new__(FM)
                f.t = ph.enter_context(nc.sbuf_tensor(b.nm(nme), [128, nsub, 512], BF16))
                f.tr = [A(nme) for _ in range(nsub)]
                ar[nme] = f
            tmpA = [ph.enter_context(nc.sbuf_tensor(b.nm("tmpA"), [128, 512], F32)) for _ in range(4)]
            tmpA_tr = [A("tmpA") for _ in range(4)]
            tmpB = [ph.enter_context(nc.sbuf_tensor(b.nm("tmpB"), [128, 512], F32)) for _ in range(2)]
            tmpB_tr = [A("tmpB"), A("tmpB")]
            scb = [ph.enter_context(nc.sbuf_tensor(b.nm("scb"), [128, 512], BF16)) for _ in range(2)]
            scb_tr = [A("scb"), A("scb")]
            sqb = ph.enter_context(nc.sbuf_tensor(b.nm("sqb"), [128, 512], BF16))
            sqb_tr = A("sqb")
            tcnt = {"a": 0, "b": 0, "sc": 0, "cv": 0}
            ctmp, ctmp_tr = tmpB, tmpB_tr
            if is_s:
                Ss = ph.enter_context(nc.sbuf_tensor(b.nm("Ss"), [128, 2, 4, 128], F32))
                Ssb = ph.enter_context(nc.sbuf_tensor(b.nm("Ssb"), [128, 2, 4, 128], BF16))
                Ss_tr = [[A("Ss") for _ in range(4)] for _ in range(2)]
                Ssb_tr = [[A("Ssb") for _ in range(4)] for _ in range(2)]
                for j in range(2):
                    b.dma(SP, Ss[:, j, :, :], shg[j].rearrange("h k v -> k h v"), ld_slot(), writes=Ss_tr[j])
                    b.op(ACT, lambda: nc.scalar.copy(out=Ssb[:, j, :, :], in_=Ss[:, j, :, :]), reads=Ss_tr[j], writes=Ssb_tr[j])

            hgrn_gates(T, L, last, ar, hTr, tmpA, tmpA_tr, part=1)
            wb, wt = w_get("cc")
            for ch in range(4):
                ps, pt = fm_proj(wb, wt, ch, T, hcol, 8, hTr)
                b.op(ACT, lambda: nc.scalar.copy(out=ccs.t[:, ch, :], in_=ps[:, 0:T]), reads=[pt], writes=[ccs.tr[ch]])
            hgrn_gates(T, L, last, ar, hTr, tmpA, tmpA_tr, part=2)
            wb, wt = w_get("cx")
            if is_s:
                for g in range(2):
                    o = g * (32 + 2)
                    b.dma(SP, u.t[:, :, o:o + 2], sconv[:, :, g, :], ld_slot(), writes=u.tr)
            else:
                b.op(POOL, lambda: nc.gpsimd.tensor_copy(out=u.t[:, :, 0:2], in_=uprev[:, :, :]), reads=[uprev_tr], writes=u.tr)
            for ch in range(4):
                ps, pt = fm_proj(wb, wt, ch, T, hcol, 8, hTr)
                for g, (g0, gl) in enumerate(segs):
                    o = g0 + 2 * g + 2
                    b.op(DVE, lambda: nc.vector.tensor_tensor(out=u.t[:, ch, o:o + gl], in0=ps[:, g0:g0 + gl], in1=ccs.t[:, ch, g0:g0 + gl], op=ALU.mult),
                         reads=[pt, ccs.tr[ch]], writes=[u.tr[ch]])
            if is_s:
                for g in range(2):
                    o = g * 34 + 16
                    b.dma(POOL, convs_o[:, :, g, :], u.t[:, :, o:o + 2], st_slot(), reads=u.tr)
            else:
                b.op(POOL, lambda: nc.gpsimd.tensor_copy(out=uprev[:, :, :], in_=u.t[:, :, T:T + 2]), reads=u.tr, writes=[uprev_tr])
            for ch in range(4):
                for g, (g0, gl) in enumerate(segs):
                    o = g0 + 2 * g
                    yv = ccs.t[:, ch, g0:g0 + gl]
                    cw = lambda j: cvec[:, 40 + ch * 3 + j:40 + ch * 3 + j + 1]
                    b.op(DVE, lambda: nc.vector.tensor_scalar(out=yv, in0=u.t[:, ch, o:o + gl], scalar1=cw(0), scalar2=None, op0=ALU.mult),
                         reads=[u.tr[ch], const_tr], writes=[ccs.tr[ch]])
                    b.op(DVE, lambda: nc.vector.scalar_tensor_tensor(out=yv, in0=u.t[:, ch, o + 1:o + 1 + gl], scalar=cw(1), in1=yv, op0=ALU.mult, op1=ALU.add),
                         reads=[u.tr[ch]], writes=[ccs.tr[ch]])
                    b.op(DVE, lambda: nc.vector.scalar_tensor_tensor(out=yv, in0=u.t[:, ch, o + 2:o + 2 + gl], scalar=cw(2), in1=yv, op0=ALU.mult, op1=ALU.add),
                         reads=[u.tr[ch]], writes=[ccs.tr[ch]])
            wb, wt = w_get("cb")
            for ch in range(4):
                ps, pt = fm_proj(wb, wt, ch, T, hcol, 8, hTr)
                b.op(DVE, lambda: nc.vector.tensor_tensor(out=z.t[:, ch, :], in0=ps[:, 0:T], in1=ccs.t[:, ch, :], op=ALU.mult),
                     reads=[pt, ccs.tr[ch]], writes=[z.tr[ch]])

            eb = ar["eb"]
            wb, wt = w_get("hq")
            for h in range(4):
                ps, pt = fm_proj(wb, wt, h, T, hcol, 8, hTr)
                ai = tcnt["a"] % 4
                tcnt["a"] += 1
                b.op(ACT, lambda: nc.scalar.activation(out=tmpA[ai][:, 0:T], in_=ps[:, 0:T], func=AF.Silu), reads=[pt], writes=[tmpA_tr[ai]])
                b.op(DVE, lambda: nc.vector.tensor_tensor(out=qt.t[:, h, :], in0=tmpA[ai][:, 0:T], in1=eb.t[:, h, :], op=ALU.mult),
                     reads=[tmpA_tr[ai], eb.tr[h]], writes=[qt.tr[h]])
            v_proj(subt, ar, hTr)
            kp_transpose(subt, ar)

            kt, v, kptok = ar["kt"], ar["v"], ar["kptok"]
            mask = mask32 if is_s else mask64
            gate_units = [("hg", sgt, h, h, AF.Silu) for h in range(4)]
            gate_units += [(nme, dst, ch4, int(nme[2]) * 4 + ch4, AF.Sigmoid)
                           for nme, dst in [("ga0", sga), ("ga1", sga), ("gb0", sgb), ("gb1", sgb)] for ch4 in range(4)]
            gstate = {"i": 0, "wb": None}

            def emit_gates(kmax):
                for _ in range(kmax):
                    if gstate["i"] >= len(gate_units):
                        return
                    nme, dst, ch4, dch, gfunc = gate_units[gstate["i"]]
                    gstate["i"] += 1
                    if ch4 == 0:
                        gstate["wb"] = w_get(nme)
                    wb, wt = gstate["wb"]
                    ps, pt = fm_proj(wb, wt, ch4, T, hcol, 8, hTr)
                    b.op(ACT, lambda: nc.scalar.activation(out=dst.t[:, dch, :], in_=ps[:, 0:T], func=gfunc), reads=[pt], writes=[dst.tr[dch]])

            def emit_scores(s):
                c0, n = subt[s]
                W4 = 4 * n
                pS, pSt = next_psF()

                def sc():
                    ins = None
                    for h in range(4):
                        ins = nc.tensor.matmul(pS[:n, h * n:(h + 1) * n], lhsT=kt.t[:, h, c0:c0 + n], rhs=qt.t[:, h, c0:c0 + n], start=(h == 0), stop=True, skip_group_check=True)
                    return ins
                b.op(PE, sc, reads=kt.tr + qt.tr, writes=[pSt])
                si = tcnt["sc"] % 2
                tcnt["sc"] += 1
                mb = mask[:n, :n].unsqueeze(1).to_broadcast([n, 4, n])
                b.op(DVE, lambda: nc.vector.tensor_tensor(out=scb[si][:n, 0:W4].rearrange("p (h t) -> p h t", h=4),
                                                          in0=pS[:n, 0:W4].rearrange("p (h t) -> p h t", h=4), in1=mb, op=ALU.mult),
                     reads=[pSt, mtr], writes=[scb_tr[si]])
                return si

            cnt["in_rec"] = 1
            if is_s:
                emit_gates(4)
            pend_si = emit_scores(0)
            for s, (c0, n) in enumerate(subt):
                nchk = n // L
                W4 = 4 * n
                pO, pOt = next_psO()
                si = pend_si

                def intra():
                    ins = None
                    for h in range(4):
                        ins = nc.tensor.matmul(pO[:, h * n:(h + 1) * n], lhsT=v.t[:n, s, h * 128:(h + 1) * 128], rhs=scb[si][:n, h * n:(h + 1) * n],
                                               start=(h == 0), stop=False, skip_group_check=True)
                    return ins
                b.op(PE, intra, reads=[v.tr[s], scb_tr[si]], writes=[pOt])
                for c in range(nchk):
                    r0 = c * L

                    def sterm():
                        ins = None
                        for h in range(4):
                            Sb_ap = Ssb[:, c, h, :] if is_s else Sb[:, h, :]
                            ins = nc.tensor.matmul(pO[:, h * n + r0:h * n + r0 + L], lhsT=Sb_ap, rhs=qt.t[:, h, c0 + r0:c0 + r0 + L],
                                                   start=False, stop=(c == nchk - 1), skip_group_check=True)
                        return ins
                    b.op(PE, sterm, reads=(Ssb_tr[c] if is_s else Sb_tr) + qt.tr, writes=[pOt])
                    pP, pPt = next_psF()

                    def pm():
                        ins = None
                        for h in range(4):
                            ins = nc.tensor.matmul(pP[:, h * 128:(h + 1) * 128], lhsT=kptok.t[r0:r0 + L, s, h * 128:(h + 1) * 128],
                                                   rhs=v.t[r0:r0 + L, s, h * 128:(h + 1) * 128], start=(h == 0), stop=True, skip_group_check=True)
                        return ins
                    b.op(PE, pm, reads=[kptok.tr[s], v.tr[s]], writes=[pPt])
                    for h in range(4):
                        if is_s:
                            S_ap, S_t = Ss[:, c, h, :], Ss_tr[c][h]
                        else:
                            S_ap, S_t = Sst[:, h, :], S_tr[h]
                        e_ap = eb.t[:, h, c0 + r0 + last:c0 + r0 + last + 1]
                        b.op(DVE, lambda: nc.vector.scalar_tensor_tensor(out=S_ap, in0=S_ap, scalar=e_ap, in1=pP[:, h * 128:(h + 1) * 128],
                                                                         op0=ALU.mult, op1=ALU.add),
                             reads=[pPt, eb.tr[h]], writes=[S_t])
                    if not is_s:
                        b.op(ACT, lambda: nc.scalar.copy(out=Sb[:, :, :], in_=Sst[:, :, :]), reads=S_tr, writes=Sb_tr)
                    emit_gates(2)
                if s + 1 < nsub:
                    pend_si = emit_scores(s + 1)
                b.op(ACT, lambda: nc.scalar.activation(out=sqb[:, 0:W4], in_=pO[:, 0:W4], func=AF.Square), reads=[pOt], writes=[sqb_tr])
                pN, pNt = next_psF()
                mm(pN[:, 0:W4], [(ones[:, :], sqb[:, 0:W4])], reads=[sqb_tr, mtr], ptr=pNt)
                ai = tcnt["a"] % 4
                tcnt["a"] += 1
                ta = tmpA[ai]
                b.op(ACT, lambda: nc.scalar.activation(out=ta[:, 0:W4], in_=pN[:, 0:W4], func=AF.Ln, scale=1.0 / 128, bias=EPS),
                     reads=[pNt], writes=[tmpA_tr[ai]])
                b.op(ACT, lambda: nc.scalar.activation(out=ta[:, 0:W4], in_=ta[:, 0:W4], func=AF.Exp, scale=-0.5), writes=[tmpA_tr[ai]])
                b.op(DVE, lambda: nc.vector.tensor_tensor(out=ta[:, 0:W4], in0=pO[:, 0:W4], in1=ta[:, 0:W4], op=ALU.mult),
                     reads=[pOt], writes=[tmpA_tr[ai]])
                tav = ta[:, 0:W4].rearrange("p (h t) -> p h t", h=4)
                b.op(DVE, lambda: nc.vector.scalar_tensor_tensor(out=oN.t[:, :, c0:c0 + n], in0=tav, scalar=cvec[:, 52:53],
                                                                 in1=sgt.t[:, :, c0:c0 + n], op0=ALU.mult, op1=ALU.mult),
                     reads=[tmpA_tr[ai], const_tr] + sgt.tr, writes=oN.tr)
            if is_s:
                for j in range(2):
                    b.dma(POOL, hgs_o[j].rearrange("h k v -> k h v"), Ss[:, j, :, :], st_slot(), reads=Ss_tr[j])
            cnt["in_rec"] = 0
            emit_gates(100)
            for c in range(2):
                wb, wt = w_get("cvhg%d" % c)
                for ch4 in range(4):
                    dch = c * 4 + ch4
                    pa, pat = fm_proj(wb, wt, ch4, T, lambda k: z.t[:, k, :], 4, z.tr, koff=0)
                    pbb, pbt = fm_proj(wb, wt, ch4, T, lambda k: oN.t[:, k, :], 4, oN.tr, koff=4)
                    ai = tcnt["a"] % 4
                    tcnt["a"] += 1
                    bi = tcnt["b"] % 2
                    tcnt["b"] += 1
                    b.op(DVE, lambda: nc.vector.tensor_tensor(out=tmpA[ai][:, 0:T], in0=pa[:, 0:T], in1=sga.t[:, dch, :], op=ALU.mult),
                         reads=[pat, sga.tr[dch]], writes=[tmpA_tr[ai]])
                    b.op(DVE, lambda: nc.vector.tensor_tensor(out=tmpB[bi][:, 0:T], in0=pbb[:, 0:T], in1=sgb.t[:, dch, :], op=ALU.mult),
                         reads=[pbt, sgb.tr[dch]], writes=[tmpB_tr[bi]])
                    b.op(POOL, lambda: nc.gpsimd.tensor_tensor(out=mrgb.t[:, dch, :], in0=tmpA[ai][:, 0:T], in1=tmpB[bi][:, 0:T], op=ALU.add),
                         reads=[tmpA_tr[ai], tmpB_tr[bi]], writes=[mrgb.tr[dch]])
            wpair = [w_get("wo0"), w_get("wo1", ahead=False)]
            for s, (c0, n) in enumerate(subt):
                for c in range(2):
                    wb, wt = wpair[c]
                    ps, pt = tok_proj(wb, wt, lambda k: mrgb.t[:, k, c0:c0 + n], 8, n, mrgb.tr)
                    xsl = xr[:n, s, c * 512:(c + 1) * 512]
                    b.op(DVE, lambda: nc.vector.tensor_tensor(out=xsl, in0=ps[:n, :], in1=xsl, op=ALU.add), reads=[pt], writes=[xrt[s]])
            b.end_phase()

        if nsub == 4:
            rmsnorm_to_hT(xr, xrt, subt, G_X, sel=[0, 1, 2])
            rmsnorm_to_hT(xr, xrt, subt, G_X, sel=[3])
        else:
            rmsnorm_to_hT(xr, xrt, subt, G_X)
        with ExitStack() as ph:
            def fm(nme, nch, dt, W=T):
                f = FM.__new__(FM)
                f.t = ph.enter_context(nc.sbuf_tensor(b.nm(nme), [128, nch, W], dt))
                f.tr = [A(nme) for _ in range(nch)]
                return f
            qT = fm("qT", 8, BF16)
            eT = fm("eT", 8, BF16)
            oxT = fm("oxT", 8, BF16)
            rden = fm("rden", 4, F32)
            if is_s:
                kst = ph.enter_context(nc.sbuf_tensor(b.nm("kst"), [128, 2, D], BF16))
                kst_tr = A("kst")
                mkTs = [fm("mkTs", 8, BF16, 256) for _ in range(2)]
                mvs = [fm("mvs", 2, BF16, D) for _ in range(2)]
                for j in range(2):
                    b.dma(POOL, kst[:, :, :], ck[j].rearrange("(c p) d -> p c d", p=128), pl_slot(), writes=[kst_tr])
                    for mc in range(2):
                        pb, pbt = next_psB()

                        def tp():
                            ins = None
                            for dc in range(8):
                                ins = nc.tensor.transpose(pb[:, dc * 128:(dc + 1) * 128], kst[:, mc, dc * 128:(dc + 1) * 128], ident[:, :])
                            return ins
                        b.op(PE, tp, reads=[kst_tr, mtr], writes=[pbt])
                        b.op(ACT, lambda: nc.scalar.copy(out=mkTs[j].t[:, :, mc * 128:(mc + 1) * 128],
                                                         in_=pb[:, :].rearrange("p (k t) -> p k t", t=128)), reads=[pbt], writes=mkTs[j].tr)
                    b.dma(POOL, mvs[j].t[:, :, :], cv[j].rearrange("(c p) d -> p c d", p=128), pl_slot(), writes=mvs[j].tr)
                agroups = [((0, 32), mkTs[0].t, mkTs[0].tr, mvs[0].t, mvs[0].tr), ((32, 32), mkTs[1].t, mkTs[1].tr, mvs[1].t, mvs[1].tr)]
            else:
                agroups = [((0, T), mkT, [mkT_tr], mv, [mv_tr])]
            for c in range(2):
                wb, wt = w_get("xq%d" % c)
                banks = fm_block_split(wb, wt) if (c == 0 and nsub == 4) else None
                for ch4 in range(4):
                    ps, pt = banks[ch4] if banks else fm_proj(wb, wt, ch4, T, hcol, 8, hTr)
                    dch = c * 4 + ch4
                    b.op(ACT, lambda: nc.scalar.activation(out=qT.t[:, dch, :], in_=ps[:, 0:T], func=AF.Copy, scale=1.0 / 16.0), reads=[pt], writes=[qT.tr[dch]])
            for ((g0, gl), mk_t, mk_trs, mv_t, mv_trs) in agroups:
                gs = slice(g0, g0 + gl)
                for h in range(4):
                    for mc in range(2):
                        ps, pt = next_psF()
                        mm(ps[:, 0:gl], [(mk_t[:, 2 * h + dd, mc * 128:(mc + 1) * 128], qT.t[:, 2 * h + dd, gs]) for dd in range(2)],
                           reads=mk_trs + [qT.tr[2 * h], qT.tr[2 * h + 1]], ptr=pt)
                        b.op(ACT, lambda: nc.scalar.activation(out=eT.t[:, h * 2 + mc, gs], in_=ps[:, 0:gl], func=AF.Exp), reads=[pt], writes=[eT.tr[h * 2 + mc]])
                    ps, pt = next_psF()
                    mm(ps[:, 0:gl], [(ones[:, :], eT.t[:, h * 2 + mc, gs]) for mc in range(2)], reads=[mtr, eT.tr[h * 2], eT.tr[h * 2 + 1]], ptr=pt)
                    b.op(ACT, lambda: nc.scalar.activation(out=rden.t[:, h, gs], in_=ps[:, 0:gl], func=AF.Ln), reads=[pt], writes=[rden.tr[h]])
                    b.op(ACT, lambda: nc.scalar.activation(out=rden.t[:, h, gs], in_=rden.t[:, h, gs], func=AF.Exp, scale=-1.0), writes=[rden.tr[h]])
                for dc in range(8):
                    h = dc // 2
                    ps, pt = next_psF()
                    mm(ps[:, 0:gl], [(mv_t[:, mc, dc * 128:(dc + 1) * 128], eT.t[:, h * 2 + mc, gs]) for mc in range(2)],
                       reads=mv_trs + [eT.tr[h * 2], eT.tr[h * 2 + 1]], ptr=pt)
                    b.op(DVE, lambda: nc.vector.tensor_tensor(out=oxT.t[:, dc, gs], in0=ps[:, 0:gl], in1=rden.t[:, h, gs], op=ALU.mult),
                         reads=[pt, rden.tr[h]], writes=[oxT.tr[dc]])
            wpair = [w_get("xo0"), w_get("xo1", ahead=False)]
            for s, (c0, n) in enumerate(subt):
                for c in range(2):
                    wb, wt = wpair[c]
                    ps, pt = tok_proj(wb, wt, lambda k: oxT.t[:, k, c0:c0 + n], 8, n, oxT.tr)
                    xsl = xr[:n, s, c * 512:(c + 1) * 512]
                    b.op(DVE, lambda: nc.vector.tensor_tensor(out=xsl, in0=ps[:n, :], in1=xsl, op=ALU.add), reads=[pt], writes=[xrt[s]])
            b.end_phase()

        if nsub == 4:
            rmsnorm_to_hT(xr, xrt, subt, G_FFN, sel=[0, 1, 2])
            rmsnorm_to_hT(xr, xrt, subt, G_FFN, sel=[3])
        else:
            rmsnorm_to_hT(xr, xrt, subt, G_FFN)
        with ExitStack() as ph:
            a2 = FM.__new__(FM)
            a2.t = ph.enter_context(nc.sbuf_tensor(b.nm("a2"), [128, 32, T], BF16))
            a2.tr = [A("a2") for _ in range(32)]
            rt = [ph.enter_context(nc.sbuf_tensor(b.nm("rt"), [128, 512], F32)) for _ in range(2)]
            rt_tr = [A("rt"), A("rt")]
            rc = 0
            for c in range(8):
                wb, wt = w_get("up%d" % c)
                banks = fm_block_split(wb, wt) if (c == 0 and nsub == 4) else None
                for ch4 in range(4):
                    ps, pt = banks[ch4] if banks else fm_proj(wb, wt, ch4, T, hcol, 8, hTr)
                    hch = c * 4 + ch4
                    ri = rc % 2
                    rc += 1
                    b.op(ACT, lambda: nc.scalar.activation(out=rt[ri][:, 0:T], in_=ps[:, 0:T], func=AF.Relu), reads=[pt], writes=[rt_tr[ri]])
                    b.op(POOL, lambda: nc.gpsimd.tensor_tensor(out=a2.t[:, hch, :], in0=rt[ri][:, 0:T], in1=rt[ri][:, 0:T], op=ALU.mult),
                         reads=[rt_tr[ri]], writes=[a2.tr[hch]])
            for c in range(2):
                acc = [next_psF() for _ in range(nsub)]
                for r in range(4):
                    wb, wt = w_get("dn%d_%d" % (r, c))
                    for s, (c0, n) in enumerate(subt):
                        tok_proj(wb, wt, lambda k: a2.t[:, r * 8 + k, c0:c0 + n], 8, n, a2.tr[r * 8:(r + 1) * 8],
                                 ps=acc[s][0], pt=acc[s][1], start=(r == 0), stop=(r == 3))
                for s, (c0, n) in enumerate(subt):
                    ps, pt = acc[s]
                    xsl = xr[:n, s, c * 512:(c + 1) * 512]
                    b.op(DVE, lambda: nc.vector.tensor_tensor(out=xsl, in0=ps[:n, :], in1=xsl, op=ALU.add), reads=[pt], writes=[xrt[s]])
                if c == 0 and ti + 1 < len(tiles) and tiles[ti + 1][0] != "pre":
                    nxr, nxrt = ensure_loaded(ti + 1)
                    nsubt = [(0, 64)] if tiles[ti + 1][0] == "sample" else SUB4
                    rmsnorm_to_hT(nxr, nxrt, nsubt, G_MIX)
                    did_prenorm = True
            b.end_phase()

        rsv, stt = norm_stats(xr, xrt, subt)
        for s, (c0, n) in enumerate(subt):
            yi = cnt["yo"] % 2
            cnt["yo"] += 1
            b.op(DVE, lambda: nc.vector.scalar_tensor_tensor(out=yout[yi][:n, :], in0=xr[:n, s, :], scalar=rsv[:n, s:s + 1], in1=gfin[:n, :],
                                                             op0=ALU.mult, op1=ALU.mult),
                 reads=[xrt[s], stt, gfin_tr], writes=[yout_tr[yi]])
            if is_s:
                b.dma(POOL, ys_o[c0:c0 + n, :], yout[yi][:n, :], st_slot(), reads=[yout_tr[yi]])
            else:
                b.dma(POOL, y_o[t * TM + c0:t * TM + c0 + n, :], yout[yi][:n, :], st_slot(), reads=[yout_tr[yi]])
        return did_prenorm

    if do_mem:
        mem_kv()
    with ExitStack() as pph:
        if n_pre:
            pre_alloc(pph)
        for p in range(n_pre):
            pre_tile(p, p == n_pre - 1)
        b.end_phase()
    b.op(ACT, lambda: nc.scalar.copy(out=Sb[:, :, :], in_=Sst[:, :, :]), reads=S_tr, writes=Sb_tr)
    pn = False
    for t in range(n_main):
        pn = full_tile("main", t, pn)
    b.dma(POOL, hg_o.rearrange("h k v -> k h v"), Sst[:, :, :], st_slot(), reads=S_tr)
    b.dma(POOL, conv_o[:, :, :], uprev[:, :, :], st_slot(), reads=[uprev_tr])
    if sample:
        full_tile("sample", 0, pn)
    fin = []
    for s in b.slots.values():
        if s.count:
            fin.append((s, s.count))
    for e in [PE, ACT, DVE, POOL]:
        if e.count:
            fin.append((e, e.count))
    b._wait(SP, fin)
    es.close()
    return nc


_NC_CACHE = {}


def _prep_inputs(inp):
    f = lambda a: np.ascontiguousarray(np.asarray(a, dtype=np.float32))
    x_prompt = f(inp["x_prompt"])
    x_sample = f(inp["x_sample"])
    mem_prompt = f(inp["mem_prompt"])
    state_conv = f(inp["state_conv"])
    state_hgrn = f(inp["state_hgrn"])
    ckk = f(inp["cache_mem_k"])
    cvv = f(inp["cache_mem_v"])
    p8 = lambda v: np.asarray(v, np.float32).reshape(8, 128).T
    cvec = np.zeros((128, NCV), np.float32)
    cvec[:, 0:8] = p8(inp["norm_mix"][0])
    cvec[:, 8:16] = p8(inp["norm_x"][0])
    cvec[:, 16:24] = p8(inp["norm_ffn"][0])
    cvec[:, 24:32] = p8(inp["norm_mem"][0])
    hl = np.asarray(inp["hg_lb"], np.float32)
    cvec[:, 32:36] = hl[0].reshape(4, 128).T
    cvec[:, 36:40] = hl[1].reshape(4, 128).T
    cw = np.asarray(inp["conv_w"], np.float32)[0]
    cvec[:, 40:52] = cw.reshape(3, 4, 128).transpose(2, 1, 0).reshape(128, 12)
    cvec[:, 52] = np.asarray(inp["hg_norm"], np.float32)[0]
    gfin = np.ascontiguousarray(np.broadcast_to(np.asarray(inp["norm_final"], np.float32)[None, :], (128, D)))
    shared = {
        "cvec": cvec, "gfin": gfin,
        "w_in": f(inp["w_in"][0]), "w_conv_out": f(inp["w_conv_out"][0]), "w_hg_out": f(inp["w_hg_out"][0]),
        "w_o": f(inp["w_o"][0]), "w_xq": f(inp["w_xq"][0]), "w_xk": f(inp["w_xk"][0]), "w_xv": f(inp["w_xv"][0]),
        "w_xo": f(inp["w_xo"][0]), "w_up": f(inp["w_up"][0]), "w_down": f(inp["w_down"][0]),
    }
    maps = []
    for c in range(N_CORES):
        bi, half = c // 2, c % 2
        m = dict(shared)
        m["xm"] = np.ascontiguousarray(x_prompt[bi, half * 4096:(half + 1) * 4096])
        m["xp"] = np.ascontiguousarray(x_prompt[bi, 0:4096]) if half == 1 else np.zeros((4096, D), np.float32)
        xs = np.zeros((64, D), np.float32)
        for j in range(2):
            xs[32 * j:32 * j + 16] = x_sample[2 * c + j]
        m["xs"] = xs
        m["mem"] = np.ascontiguousarray(mem_prompt[bi])
        sc = state_conv[0, 2 * c:2 * c + 2]
        m["sconv"] = np.ascontiguousarray(sc.reshape(2, 2, 4, 128).transpose(3, 2, 0, 1))
        m["shg"] = np.ascontiguousarray(state_hgrn[0, 2 * c:2 * c + 2])
        m["ck"] = np.ascontiguousarray(ckk[0, 2 * c:2 * c + 2].reshape(2, 256, D))
        m["cv"] = np.ascontiguousarray(cvv[0, 2 * c:2 * c + 2].reshape(2, 256, D))
        maps.append(m)
    return maps


def kernel(**inputs):
    if "nc" not in _NC_CACHE:
        _NC_CACHE["nc"] = build()
    nc = _NC_CACHE["nc"]
    maps = _prep_inputs(inputs)
    res = run_bass_kernel_spmd(nc, maps, core_ids=list(range(N_CORES)))
    R = res.results
    y_prompt = np.zeros((4, 8192, D), np.float32)
    y_sample = np.zeros((16, 16, D), np.float32)
    conv_p = np.zeros((1, 4, 2, 512), np.float32)
    hg_p = np.zeros((1, 4, 4, 128, 128), np.float32)
    mk_p = np.zeros((1, 4, 256, 4, 256), np.float32)
    mv_p = np.zeros((1, 4, 256, 4, 256), np.float32)
    conv_s = np.zeros((1, 16, 2, 512), np.float32)
    hg_s = np.zeros((1, 16, 4, 128, 128), np.float32)
    for c in range(N_CORES):
        bi, half = c // 2, c % 2
        r = R[c]
        y_prompt[bi, half * 4096:(half + 1) * 4096] = r["y"]
        for j in range(2):
            y_sample[2 * c + j] = r["ys"][32 * j:32 * j + 16]
            conv_s[0, 2 * c + j] = r["convs_o"][:, :, j, :].transpose(2, 1, 0).reshape(2, 512)
            hg_s[0, 2 * c + j] = r["hgs_o"][j]
        if half == 1:
            conv_p[0, bi] = r["conv_o"].transpose(2, 1, 0).reshape(2, 512)
            hg_p[0, bi] = r["hg_o"]
        else:
            mk_p[0, bi] = r["mk_o"].reshape(256, 4, 256)
            mv_p[0, bi] = r["mv_o"].reshape(256, 4, 256)
    return (y_prompt, y_sample, conv_p, hg_p, mk_p, mv_p, conv_s, hg_s)
```
